# Optimizing a Trainium2 kernel written in Bass

```python
import math
import jax
import jax.numpy as jnp
from jax import lax
import numpy as np

D_MODEL = 1024
BATCH = 4
SEQ = 8192
DEPTH = 1

CHUNK = 64
Q_BLOCK = 128
PLE_DIM = 256
EPS = 1e-6
NEG_INF = -1e30

MLA_HEADS = 8
MLA_Q_RANK = 256
MLA_KV_RANK = 128
MLA_NOPE = 64
MLA_ROPE = 32
MLA_QK = MLA_NOPE + MLA_ROPE
MLA_V = 64
ROPE_THETA = 10000.0

DIFF_HEADS = 4
DIFF_QK = 64
DIFF_V = 2 * DIFF_QK

MIX_WIDTH = MLA_HEADS * MLA_V + DIFF_HEADS * DIFF_V

NUM_BUCKETS = 32
MAX_DISTANCE = 1024

D_FF = 2816
CONV_WIDTH = 3

OFF_Q_LAT = 0
OFF_KV_LAT = OFF_Q_LAT + MLA_Q_RANK
OFF_K_ROPE = OFF_KV_LAT + MLA_KV_RANK
OFF_DIFF_Q = OFF_K_ROPE + MLA_ROPE
OFF_DIFF_K = OFF_DIFF_Q + DIFF_HEADS * 2 * DIFF_QK
OFF_DIFF_V = OFF_DIFF_K + DIFF_HEADS * 2 * DIFF_QK
IN_COLS = OFF_DIFF_V + DIFF_HEADS * DIFF_V

kernel_name = 'hybrid_mla_diffattn_convffn_ple'


def lambda_init(layer):
    return 0.8 - 0.6 * math.exp(-0.3 * layer)


def rms_norm(x, g):
    xf = x.astype(jnp.float32)
    y = xf * lax.rsqrt(jnp.mean(xf * xf, axis=-1, keepdims=True) + EPS)
    return (y * g.astype(jnp.float32)).astype(x.dtype)


def apply_rope(t, pos):
    half = t.shape[-1] // 2
    inv_freq = ROPE_THETA ** (-jnp.arange(half, dtype=jnp.float32) / half)
    ang = pos.astype(jnp.float32)[:, None] * inv_freq[None, :]
    cos = jnp.cos(ang)[None, :, None, :].astype(t.dtype)
    sin = jnp.sin(ang)[None, :, None, :].astype(t.dtype)
    t1, t2 = t[..., :half], t[..., half:]
    return jnp.concatenate([t1 * cos - t2 * sin, t2 * cos + t1 * sin], axis=-1)


def t5_bucket(rel):
    nb = NUM_BUCKETS // 2
    max_exact = nb // 2
    sign_off = (rel > 0).astype(jnp.int32) * nb
    n = jnp.abs(rel)
    nf = jnp.maximum(n, 1).astype(jnp.float32)
    large = max_exact + (jnp.log(nf / max_exact) / math.log(MAX_DISTANCE / max_exact)
                         * (nb - max_exact)).astype(jnp.int32)
    large = jnp.minimum(large, nb - 1)
    return sign_off + jnp.where(n < max_exact, n, large)


def block_indices(blk, seq):
    q_idx = blk * Q_BLOCK + jnp.arange(Q_BLOCK, dtype=jnp.int32)
    k_idx = jnp.arange(seq, dtype=jnp.int32)
    return q_idx, k_idx


def chunk_allowed(q_idx, k_idx):
    return (k_idx[None, :] // CHUNK) <= (q_idx[:, None] // CHUNK)


def to_blocks(t):
    b, s, h, d = t.shape
    return t.reshape(b, s // Q_BLOCK, Q_BLOCK, h, d).transpose(1, 0, 3, 2, 4)


def from_blocks(t):
    n, b, h, qb, d = t.shape
    return t.transpose(1, 0, 3, 2, 4).reshape(b, n * qb, h, d)


def mla_group(q_lat, kv_lat, k_rope, pos, q_lat_g, w_uq, kv_lat_g, w_ukv, q_g, k_g):
    b, s, _ = q_lat.shape
    q = (rms_norm(q_lat, q_lat_g) @ w_uq).reshape(b, s, MLA_HEADS, MLA_QK)
    kv = (rms_norm(kv_lat, kv_lat_g) @ w_ukv).reshape(b, s, MLA_HEADS, MLA_NOPE + MLA_V)
    k_nope, v = kv[..., :MLA_NOPE], kv[..., MLA_NOPE:]
    k_r = jnp.broadcast_to(k_rope[:, :, None, :], (b, s, MLA_HEADS, MLA_ROPE))
    k = jnp.concatenate([k_nope, k_r], axis=-1)
    q = rms_norm(q, q_g)
    k = rms_norm(k, k_g)
    q = jnp.concatenate([q[..., :MLA_NOPE], apply_rope(q[..., MLA_NOPE:], pos)], axis=-1)
    k = jnp.concatenate([k[..., :MLA_NOPE], apply_rope(k[..., MLA_NOPE:], pos)], axis=-1)
    k_t = k.transpose(0, 2, 1, 3)
    v_t = v.transpose(0, 2, 1, 3)
    scale = MLA_QK ** -0.5

    def one_block(args):
        qb, blk = args
        q_idx, k_idx = block_indices(blk, s)
        allowed = chunk_allowed(q_idx, k_idx)
        logits = jnp.einsum('bhqd,bhkd->bhqk', qb, k_t).astype(jnp.float32) * scale
        logits = jnp.where(allowed[None, None], logits, NEG_INF)
        probs = jax.nn.softmax(logits, axis=-1).astype(v_t.dtype)
        return jnp.einsum('bhqk,bhkd->bhqd', probs, v_t)

    o = lax.map(one_block, (to_blocks(q), jnp.arange(s // Q_BLOCK, dtype=jnp.int32)))
    return from_blocks(o).reshape(b, s, MLA_HEADS * MLA_V)


def diff_group(dq, dk, dv, q_g, k_g, lq1, lk1, lq2, lk2, out_g, rel_bias, lam_init):
    b, s, _ = dq.shape
    q = rms_norm(dq.reshape(b, s, DIFF_HEADS, 2, DIFF_QK), q_g)
    k = rms_norm(dk.reshape(b, s, DIFF_HEADS, 2, DIFF_QK), k_g)
    k_t = k.transpose(0, 2, 3, 1, 4)
    v_t = dv.reshape(b, s, DIFF_HEADS, DIFF_V).transpose(0, 2, 1, 3)
    f32 = jnp.float32
    lam = (jnp.exp(jnp.sum(lq1.astype(f32) * lk1.astype(f32)))
           - jnp.exp(jnp.sum(lq2.astype(f32) * lk2.astype(f32))) + lam_init)
    scale = DIFF_QK ** -0.5
    table = rel_bias.astype(f32)

    def one_block(args):
        qb, blk = args
        qb = qb.reshape(b, DIFF_HEADS, Q_BLOCK, 2, DIFF_QK)
        q_idx, k_idx = block_indices(blk, s)
        allowed = chunk_allowed(q_idx, k_idx)
        bias = table[t5_bucket(k_idx[None, :] - q_idx[:, None])].transpose(2, 0, 1)
        logits = jnp.einsum('bhqmd,bhmkd->bhmqk', qb, k_t).astype(f32) * scale
        logits = logits + bias[None, :, None]
        logits = jnp.where(allowed[None, None, None], logits, NEG_INF)
        probs = jax.nn.softmax(logits, axis=-1)
        attn = (probs[:, :, 0] - lam * probs[:, :, 1]).astype(v_t.dtype)
        return jnp.einsum('bhqk,bhkd->bhqd', attn, v_t)

    qf = q.reshape(b, s, DIFF_HEADS, 2 * DIFF_QK)
    o = lax.map(one_block, (to_blocks(qf), jnp.arange(s // Q_BLOCK, dtype=jnp.int32)))
    o = rms_norm(from_blocks(o), out_g) * (1.0 - lam_init)
    return o.reshape(b, s, DIFF_HEADS * DIFF_V)


def conv_ffn(h, w_gate, w_up, conv_w, conv_b, w_down):
    s = h.shape[1]
    g = h @ w_gate
    gp = jnp.pad(g, ((0, 0), (CONV_WIDTH - 1, 0), (0, 0)))
    conv = conv_b
    for j in range(CONV_WIDTH):
        conv = conv + gp[:, j:j + s, :] * conv_w[j]
    return (jax.nn.silu(conv) * (h @ w_up)) @ w_down


def setup_inputs(seed: int = 0) -> dict:
    key = jax.random.key(seed)
    ks = jax.random.split(key, 28)
    f32 = jnp.float32

    def nrm(k, shape, scale):
        return jax.random.normal(k, shape, dtype=f32) * scale

    def gain(k, shape):
        return 1.0 + 0.1 * jax.random.normal(k, shape, dtype=f32)

    return {
        'x': nrm(ks[0], (BATCH, SEQ, D_MODEL), 1.0),
        'p': nrm(ks[1], (DEPTH, BATCH, SEQ, PLE_DIM), 1.0),
        'attn_norm_g': gain(ks[2], (DEPTH, D_MODEL)),
        'w_in': nrm(ks[3], (DEPTH, D_MODEL, IN_COLS), D_MODEL ** -0.5),
        'q_lat_norm_g': gain(ks[4], (DEPTH, MLA_Q_RANK)),
        'w_uq': nrm(ks[5], (DEPTH, MLA_Q_RANK, MLA_HEADS * MLA_QK), MLA_Q_RANK ** -0.5),
        'kv_lat_norm_g': gain(ks[6], (DEPTH, MLA_KV_RANK)),
        'w_ukv': nrm(ks[7], (DEPTH, MLA_KV_RANK, MLA_HEADS * (MLA_NOPE + MLA_V)), MLA_KV_RANK ** -0.5),
        'mla_q_norm_g': gain(ks[8], (DEPTH, MLA_QK)),
        'mla_k_norm_g': gain(ks[9], (DEPTH, MLA_QK)),
        'diff_q_norm_g': gain(ks[10], (DEPTH, DIFF_QK)),
        'diff_k_norm_g': gain(ks[11], (DEPTH, DIFF_QK)),
        'lambda_q1': nrm(ks[12], (DEPTH, DIFF_QK), 0.1),
        'lambda_k1': nrm(ks[13], (DEPTH, DIFF_QK), 0.1),
        'lambda_q2': nrm(ks[14], (DEPTH, DIFF_QK), 0.1),
        'lambda_k2': nrm(ks[15], (DEPTH, DIFF_QK), 0.1),
        'diff_out_norm_g': gain(ks[16], (DEPTH, DIFF_V)),
        'rel_bias': nrm(ks[17], (NUM_BUCKETS, DIFF_HEADS), 0.5),
        'w_out': nrm(ks[18], (DEPTH, MIX_WIDTH, D_MODEL), MIX_WIDTH ** -0.5),
        'ffn_norm_g': gain(ks[19], (DEPTH, D_MODEL)),
        'w_gate': nrm(ks[20], (DEPTH, D_MODEL, D_FF), D_MODEL ** -0.5),
        'w_up': nrm(ks[21], (DEPTH, D_MODEL, D_FF), D_MODEL ** -0.5),
        'conv_w': nrm(ks[22], (DEPTH, CONV_WIDTH, D_FF), CONV_WIDTH ** -0.5),
        'conv_b': nrm(ks[23], (DEPTH, D_FF), 0.02),
        'w_down': nrm(ks[24], (DEPTH, D_FF, D_MODEL), D_FF ** -0.5),
        'ple_norm_g': gain(ks[25], (DEPTH, D_MODEL)),
        'w_ple_gate': nrm(ks[26], (DEPTH, D_MODEL, D_MODEL), D_MODEL ** -0.5),
        'w_ple_proj': nrm(ks[27], (DEPTH, PLE_DIM, D_MODEL), PLE_DIM ** -0.5),
    }


def reference(x, p, attn_norm_g, w_in, q_lat_norm_g, w_uq, kv_lat_norm_g, w_ukv,
              mla_q_norm_g, mla_k_norm_g, diff_q_norm_g, diff_k_norm_g,
              lambda_q1, lambda_k1, lambda_q2, lambda_k2, diff_out_norm_g, rel_bias,
              w_out, ffn_norm_g, w_gate, w_up, conv_w, conv_b, w_down,
              ple_norm_g, w_ple_gate, w_ple_proj):
    s = x.shape[1]
    pos = jnp.arange(s, dtype=jnp.int32)
    for i in range(DEPTH):
        h = rms_norm(x, attn_norm_g[i])
        z = h @ w_in[i]
        y_mla = mla_group(z[..., OFF_Q_LAT:OFF_KV_LAT], z[..., OFF_KV_LAT:OFF_K_ROPE],
                          z[..., OFF_K_ROPE:OFF_DIFF_Q], pos,
                          q_lat_norm_g[i], w_uq[i], kv_lat_norm_g[i], w_ukv[i],
                          mla_q_norm_g[i], mla_k_norm_g[i])
        y_diff = diff_group(z[..., OFF_DIFF_Q:OFF_DIFF_K], z[..., OFF_DIFF_K:OFF_DIFF_V],
                            z[..., OFF_DIFF_V:IN_COLS],
                            diff_q_norm_g[i], diff_k_norm_g[i],
                            lambda_q1[i], lambda_k1[i], lambda_q2[i], lambda_k2[i],
                            diff_out_norm_g[i], rel_bias, lambda_init(i))
        x = x + jnp.concatenate([y_mla, y_diff], axis=-1) @ w_out[i]
        x = x + conv_ffn(rms_norm(x, ffn_norm_g[i]), w_gate[i], w_up[i], conv_w[i], conv_b[i], w_down[i])
        gate = jax.nn.sigmoid(rms_norm(x, ple_norm_g[i]) @ w_ple_gate[i])
        x = x + gate * (p[i] @ w_ple_proj[i])
    return x
```

```python
import contextlib
import numpy as np
import concourse.bass as bass
import concourse.mybir as mybir
from concourse.bass_utils import run_bass_kernel_spmd

F32 = mybir.dt.float32
BF16 = mybir.dt.bfloat16
ALU = mybir.AluOpType
AF = mybir.ActivationFunctionType
AX = mybir.AxisListType

import os
SES_ENGS = set(os.environ.get("K_SES", "act,dve").split(","))
MAXOPS = int(os.environ.get("K_MAXOPS", "100000000"))
VERBOSE_OPS = bool(int(os.environ.get("K_VERBOSE", "0")))


class TRef:
    def __init__(self, obj, name, handle=None, psum=False):
        self.o = obj
        self.name = name
        self.t = handle if handle is not None else obj
        self.psum = psum

    def __getitem__(self, k):
        return self.o[k]

    def rearrange(self, s, **kw):
        return self.o[:].rearrange(s, **kw) if not hasattr(self.o, "rearrange") else self.o.rearrange(s, **kw)


class _Op:
    __slots__ = ("eng", "fn", "reads", "writes", "is_dma", "chan", "deps", "needed", "ms", "idx")


class Prog:
    ENGS = ("pe", "act", "dve", "pool", "sp")

    def __init__(self, nc):
        self.nc = nc
        self.stack = contextlib.ExitStack()
        self.ops = []
        self.nchan = 0
        self.pending_barrier = {}

    def dram(self, name, shape, dtype, kind="Internal"):
        h = self.nc.dram_tensor(name, list(shape), dtype, kind=kind)
        return TRef(h.ap(), name, h)

    def sbuf(self, name, shape, dtype):
        t = self.stack.enter_context(self.nc.sbuf_tensor(name, list(shape), dtype))
        return TRef(t, name)

    def psum(self, name, shape, dtype):
        t = self.stack.enter_context(self.nc.psum_tensor(name, list(shape), dtype))
        return TRef(t, name, psum=True)

    def op(self, eng, fn, reads=(), writes=()):
        o = _Op()
        reads, writes = tuple(reads), tuple(writes)
        writes = writes + tuple(r for r in reads if getattr(r, "psum", False) and r not in writes)
        o.eng, o.fn, o.reads, o.writes = eng, fn, reads, writes
        o.is_dma, o.chan = False, None
        o.ms = None
        o.idx = None
        o.deps = set()
        cap = getattr(self, "_cap", None)
        if cap is not None:
            cap.append(o)
        else:
            self._commit(o)
        return o

    def _commit(self, o):
        if len(self.ops) >= MAXOPS:
            return
        o.idx = len(self.ops)
        o.deps = set(self.pending_barrier.pop(o.eng, ()))
        self.ops.append(o)

    def capture(self):
        self._cap = []

    def end_capture(self):
        lst, self._cap = self._cap, None
        return lst

    def emit_zip(self, lists):
        lists = [l for l in lists if l]
        pos = [0] * len(lists)
        total = sum(len(l) for l in lists)
        for _ in range(total):
            k = min((i for i in range(len(lists)) if pos[i] < len(lists[i])),
                    key=lambda i: pos[i] / len(lists[i]))
            self._commit(lists[k][pos[k]])
            pos[k] += 1

    def barrier(self):
        last = {}
        for o in self.ops:
            if o.is_dma:
                last[("c", o.chan)] = o.idx
            else:
                last[("e", o.eng)] = o.idx
        allidx = set(last.values())
        for e in self.ENGS:
            self.pending_barrier[e] = set(allidx) | set(self.pending_barrier.get(e, ()))

    def phase_scope(self):
        old = self.stack
        self.stack = contextlib.ExitStack()
        return old

    def end_scope(self, old):
        self.stack.close()
        self.stack = old

    def dma(self, eng, out_ap, in_ap, reads=(), writes=(), chan=None, **kw):
        o = self.op(eng, lambda e: e.dma_start(out=out_ap, in_=in_ap, **kw), reads, writes)
        o.is_dma = True
        o.chan = chan if chan is not None else ("auto", writes[0].name)
        return o

    def act(self, out, in_, func, R, W, **kw):
        return self.op("act", lambda e: e.activation(out, in_, func, **kw), R, W)

    def tt(self, eng, out, a, b, op, R, W):
        return self.op(eng, lambda e: e.tensor_tensor(out, a, b, op), R, W)

    def ts(self, eng, out, a, s1, s2, op0, op1, R, W):
        if s2 is None:
            return self.op(eng, lambda e: e.tensor_scalar(out, a, s1, None, op0), R, W)
        return self.op(eng, lambda e: e.tensor_scalar(out, a, s1, s2, op0, op1), R, W)

    def stt(self, eng, out, in0, scalar, in1, op0, op1, R, W):
        return self.op(eng, lambda e: e.scalar_tensor_tensor(out, in0, scalar, in1, op0, op1), R, W)

    def copy(self, eng, out, in_, R, W):
        if eng == "act":
            return self.op("act", lambda e: e.activation(out, in_, AF.Copy), R, W)
        return self.op(eng, lambda e: e.tensor_copy(out, in_), R, W)

    def red(self, out, in_, R, W):
        return self.op("dve", lambda e: e.tensor_reduce(out, in_, AX.X, ALU.add), R, W)

    def recip(self, out, in_, R, W):
        return self.op("dve", lambda e: e.reciprocal(out, in_), R, W)

    def mm(self, out, lhsT, rhs, start, stop, R, W):
        return self.op("pe", lambda e: e.matmul(out, lhsT, rhs, start=start, stop=stop), R, W)

    def tr(self, out, in_, ident, R, W):
        return self.op("pe", lambda e: e.transpose(out, in_, ident), R, W)

    def memset(self, eng, out, val, R, W):
        return self.op(eng, lambda e: e.memset(out, val), R, W)

    def _init_lower(self):
        nc = self.nc
        self.esem = {e: nc.alloc_semaphore(name="ms_" + e) for e in self.ENGS}
        self.csem = {}
        self.ecount = {e: 0 for e in self.ENGS}
        self.ccount = {}
        self.known = {e: {} for e in self.ENGS}
        self.last_w = {}
        self.readers = {}
        self.chan_last = {}
        self.flushed = 0
        self.n_instr = {e: 0 for e in self.ENGS}
        self._lower_ready = True

    def flush(self, final=False):
        if not getattr(self, "_lower_ready", False):
            self._init_lower()
        nc = self.nc
        ops = self.ops
        new = ops[self.flushed:]
        phase_start = self.flushed
        last_w, readers, chan_last = self.last_w, self.readers, self.chan_last
        for o in new:
            deps = set(o.deps)
            for r in o.reads:
                k = r.name
                if k in last_w:
                    deps.add(last_w[k])
            for w in o.writes:
                k = w.name
                if k in last_w:
                    deps.add(last_w[k])
                for rd in readers.get(k, ()):
                    deps.add(rd)
            if o.is_dma and o.chan in chan_last:
                deps.add(chan_last[o.chan])
            deps.discard(o.idx)
            o.deps = deps
            for w in o.writes:
                last_w[w.name] = o.idx
                readers[w.name] = []
            for r in o.reads:
                readers.setdefault(r.name, []).append(o.idx)
            if o.is_dma:
                chan_last[o.chan] = o.idx
        for o in new:
            o.needed = False
        for o in new:
            for d in o.deps:
                p = ops[d]
                if p.is_dma or p.idx < self.flushed:
                    continue
                if p.eng != o.eng or (o.eng in SES_ENGS):
                    p.needed = True
        lastc = {}
        for o in new:
            if not o.is_dma:
                lastc[o.eng] = o
        for o in lastc.values():
            o.needed = True
        for o in new:
            if o.is_dma and o.chan not in self.csem:
                self.csem[o.chan] = nc.alloc_semaphore(name="ch%d" % len(self.csem))
                self.ccount[o.chan] = 0
        esem, csem, ecount, ccount = self.esem, self.csem, self.ecount, self.ccount
        streams = {e: [] for e in self.ENGS}
        for o in new:
            need = {}
            for d in o.deps:
                p = ops[d]
                if p.is_dma:
                    key = ("c", p.chan)
                else:
                    if p.eng == o.eng and (o.eng not in SES_ENGS):
                        continue
                    key = ("e", p.eng)
                    if p.ms is None:
                        assert p.idx < phase_start, (p.idx, p.eng, o.idx, o.eng)
                        continue
                val = p.ms
                if need.get(key, 0) < val:
                    need[key] = val
            kn = self.known[o.eng]
            waits = []
            for key, val in need.items():
                if kn.get(key, 0) >= val:
                    continue
                kn[key] = val
                sem = csem[key[1]] if key[0] == "c" else esem[key[1]]
                waits.append((sem, val))
            if o.is_dma:
                ccount[o.chan] += 16
                o.ms = ccount[o.chan]
                inc = (csem[o.chan], 16)
            elif o.needed:
                ecount[o.eng] += 1
                o.ms = ecount[o.eng]
                inc = (esem[o.eng], 1)
            else:
                o.ms = None
                inc = None
            streams[o.eng].append((waits, o.fn, inc))
            o.fn = None
        self.flushed = len(ops)
        for e in self.ENGS:
            self.n_instr[e] += len(streams[e])
        fin = None
        if final:
            fin = [(csem[c], ccount[c]) for c in csem if ccount[c] > 0]
        else:
            self.barrier()

        def run(eng_handle, items, final=None):
            for waits, fn, inc in items:
                for sem, val in waits:
                    eng_handle.wait_ge(sem, val)
                ins = fn(eng_handle)
                if inc is not None:
                    ins.then_inc(inc[0], inc[1])
            if final:
                for sem, val in final:
                    eng_handle.wait_ge(sem, val)

        with nc.Block() as block:
            @block.tensor
            def _(e):
                run(e, streams["pe"])

            @block.scalar
            def _(e):
                run(e, streams["act"])

            @block.vector
            def _(e):
                run(e, streams["dve"])

            @block.gpsimd
            def _(e):
                run(e, streams["pool"])

            @block.sync
            def _(e):
                run(e, streams["sp"], fin)

    def emit(self):
        self.flush(final=True)
        self.stack.close()


D = 1024
NH, QK, NOPE, ROPE, VD = 8, 96, 64, 32, 64
QR, KVR = 256, 128
DH, DQK, DV = 4, 64, 128
DFF, NF = 2816, 22
PLE = 256
EPS = 1e-6
IN_COLS = 1952
OQ, OKV, OKR, ODQ, ODK, ODV = 0, 256, 384, 416, 928, 1440
SM_MLA = float(96 ** -0.5)
SM_DIFF = float(64 ** -0.5)
LAM_INIT = 0.8 - 0.6 * 1.0
NUM_BUCKETS = 32
RMAX, RMIN = 127, -1151
LV = RMAX - RMIN + 1
NEAR = 5


class Cfg:
    def __init__(self, S=8192, L=8):
        self.S, self.L = S, L
        self.NBLK = S // 512
        self.NT = S // 128
        self.own_pos = [p for p in range(self.NBLK) if (p // L) % 2 == 1]
        self.seg_starts = [p for p in self.own_pos if p % L == 0]
        self.NOWN = len(self.own_pos) * 512
        self.NQ = self.NOWN + 64 * len(self.seg_starts)

    def qcol(self, pos):
        return self.own_pos.index(pos) * 512

    def halocol(self, seg_pos):
        return self.NOWN + 64 * self.seg_starts.index(seg_pos)


def t5_bucket_np(rel):
    nb = NUM_BUCKETS // 2
    max_exact = nb // 2
    rel = np.asarray(rel, dtype=np.int64)
    sign_off = (rel > 0).astype(np.int64) * nb
    n = np.abs(rel)
    nf = np.maximum(n, 1).astype(np.float32)
    large = max_exact + (np.log(nf / np.float32(max_exact)) / np.float32(np.log(1024 / max_exact))
                         * np.float32(nb - max_exact)).astype(np.int32)
    large = np.minimum(large, nb - 1)
    return sign_off + np.where(n < max_exact, n, large)


def host_inputs(inputs, core, C):
    b, role = core // 2, core % 2
    S, L = C.S, C.L
    x = np.asarray(inputs["x"], dtype=np.float32)[b, :S]
    p = np.asarray(inputs["p"], dtype=np.float32)[0, b, :S]
    shift = 512 * L if role == 0 else 0
    real = np.arange(S) - shift
    ok = real >= 0
    xbuf = np.zeros((S, D), np.float32)
    xbuf[ok] = x[real[ok]]
    own_tok = np.concatenate([np.arange(pp * 512, (pp + 1) * 512) for pp in C.own_pos])
    p_own = np.ascontiguousarray(p[real[own_tok]])
    valid = ok.astype(np.float32).reshape(C.NT, 128).T.copy()
    pos = np.where(ok, real, 0).astype(np.float32)
    inv_freq = (10000.0 ** (-np.arange(16, dtype=np.float32) / np.float32(16))).astype(np.float32)
    ang = pos[:, None] * inv_freq[None, :]
    cos, sin = np.cos(ang).astype(np.float32), np.sin(ang).astype(np.float32)
    cc = np.concatenate([cos, cos], 1).reshape(C.NT, 128, 32).transpose(1, 0, 2).copy()
    sn = np.concatenate([-sin, sin], 1).reshape(C.NT, 128, 32).transpose(1, 0, 2).copy()
    rel = RMAX - np.arange(LV)
    oh = (t5_bucket_np(rel)[None, :] == np.arange(NUM_BUCKETS)[:, None]).astype(np.float32)
    m = {"xbuf": xbuf, "p_own": p_own, "valid": valid, "rope_cc": cc, "rope_sn": sn, "t5_onehot": oh}
    for k, v in inputs.items():
        if k in ("x", "p"):
            continue
        m[k] = np.ascontiguousarray(np.asarray(v, dtype=np.float32))
    return m, own_tok, real


class NS:
    pass


def bc_ap(t, n, off=0):
    return bass.AP(t.t, off, [[0, 128], [1, n]])


def declare_dram(P, C, debug=()):
    G = NS()
    S, NT, NQ = C.S, C.NT, C.NQ
    ei = lambda n, shp: P.dram(n, shp, F32, kind="ExternalInput")
    G.xbuf = ei("xbuf", [S, D])
    G.p_own = ei("p_own", [C.NOWN, PLE])
    G.valid = ei("valid", [128, NT])
    G.rope_cc = ei("rope_cc", [128, NT, 32])
    G.rope_sn = ei("rope_sn", [128, NT, 32])
    G.t5_onehot = ei("t5_onehot", [NUM_BUCKETS, LV])
    shapes = {
        "attn_norm_g": [1, D], "w_in": [1, D, IN_COLS], "q_lat_norm_g": [1, QR], "w_uq": [1, QR, NH * QK],
        "kv_lat_norm_g": [1, KVR], "w_ukv": [1, KVR, NH * 128], "mla_q_norm_g": [1, QK], "mla_k_norm_g": [1, QK],
        "diff_q_norm_g": [1, DQK], "diff_k_norm_g": [1, DQK], "lambda_q1": [1, DQK], "lambda_k1": [1, DQK],
        "lambda_q2": [1, DQK], "lambda_k2": [1, DQK], "diff_out_norm_g": [1, DV], "rel_bias": [NUM_BUCKETS, DH],
        "w_out": [1, D, D], "ffn_norm_g": [1, D], "w_gate": [1, D, DFF], "w_up": [1, D, DFF],
        "conv_w": [1, 3, DFF], "conv_b": [1, DFF], "w_down": [1, DFF, D], "ple_norm_g": [1, D],
        "w_ple_gate": [1, D, D], "w_ple_proj": [1, PLE, D],
    }
    for k, shp in shapes.items():
        setattr(G, k, ei(k, shp))
    G.out = P.dram("out", [C.NOWN, D], F32, kind="ExternalOutput")

    def scr(name, shp, dt=BF16):
        kind = "ExternalOutput" if name in debug else "Internal"
        t = P.dram(name, shp, dt, kind=kind)
        setattr(G, name, t)
        return t
    scr("KTm", [NH, QK, S])
    scr("Vm", [NH, 128, NT, 128])
    scr("QTm", [NH, QK, NQ])
    scr("KTd", [DH, 128, S])
    scr("Vd", [DH, 128, NT, 128])
    scr("QTd", [DH, 128, NQ])
    scr("yT", [128, 8, NQ])
    scr("wgu", [NF, 128, 2, 8, 128])
    scr("wdn", [128, NF, D])
    scr("Rs", [DH, 128, LV], F32)
    return G


def phase_A(P, C, G):
    old = P.phase_scope()
    sb, ps = P.sbuf, P.psum
    NT = C.NT
    win = sb("win", [128, 8, IN_COLS], BF16)
    wuq = sb("wuq", [128, 2, NH * QK], BF16)
    wukv = sb("wukv", [128, NH * 128], BF16)
    P.dma("pool", win[:], G.w_in[0].rearrange("(c p) n -> p c n", p=128), [G.w_in], [win])
    P.dma("pool", wuq[:], G.w_uq[0].rearrange("(c p) n -> p c n", p=128), [G.w_uq], [wuq])
    P.dma("pool", wukv[:], G.w_ukv[0], [G.w_ukv], [wukv])
    def bload(name, src, n):
        t = sb(name, [128, n], F32)
        P.dma("sp", t[:], bc_ap(src, n), [src], [t])
        return t
    gA = bload("gA", G.attn_norm_g, D)
    gql = bload("gql", G.q_lat_norm_g, QR)
    gkvl = bload("gkvl", G.kv_lat_norm_g, KVR)
    qg = bload("qg", G.mla_q_norm_g, QK)
    kg = bload("kg", G.mla_k_norm_g, QK)
    dqg = bload("dqg", G.diff_q_norm_g, DQK)
    dkg = bload("dkg", G.diff_k_norm_g, DQK)
    ccs = [sb("cc%d" % k, [128, 4, 32], F32) for k in range(2)]
    sns = [sb("sn%d" % k, [128, 4, 32], F32) for k in range(2)]
    valid = sb("validA", [128, NT], F32)
    P.dma("sp", valid[:], G.valid[:, :], [G.valid], [valid])
    ident = sb("identA", [128, 128], BF16)
    P.memset("pool", ident[:], 1.0, [], [ident])
    P.op("pool", lambda e: e.affine_select(ident[:], ident[:], [[-1, 128]], ALU.is_equal, 0.0,
                                           base=0, channel_multiplier=1), [ident], [ident])
    pT = ps("pT", [128, 8, 128], BF16)
    pTx = ps("pTx", [128, 8, 128], BF16)
    pZ0 = ps("pZ0", [128, 512], F32)
    pZ1 = ps("pZ1", [128, 512], F32)
    pZ2 = ps("pZ2", [128, 512], F32)
    pZ3 = ps("pZ3", [128, 512], F32)
    pK = ps("pK", [128, 1024], F32)
    NXS = 2
    xts = [sb("xt%d" % i, [128, D], F32) for i in range(NXS)]
    NSL = 2

    def slot_tiles(i):
        t = NS()
        t.junk = sb("junk%d" % i, [128, D], BF16)
        t.xg = sb("xg%d" % i, [128, D], BF16)
        t.xT = sb("xT%d" % i, [128, 8, 128], BF16)
        t.zsb = sb("zsb%d" % i, [128, IN_COLS], F32)
        t.st = sb("st%d" % i, [128, 16], F32)
        t.stk = sb("stk%d" % i, [128, 16], F32)
        t.stq = sb("stq%d" % i, [128, 16], F32)
        t.shk = sb("shk%d" % i, [128, 8, 8], F32)
        t.shq = sb("shq%d" % i, [128, 8, 8], F32)
        t.shd = sb("shd%d" % i, [128, 8, 2], F32)
        t.she = sb("she%d" % i, [128, 8, 2], F32)
        t.sqd = sb("sqd%d" % i, [128, 512], F32)
        t.tmpd = sb("tmpd%d" % i, [128, 512], F32)
        t.sqe = sb("sqe%d" % i, [128, 512], F32)
        t.tmpe = sb("tmpe%d" % i, [128, 512], F32)
        t.dfin2 = sb("dfin2%d" % i, [128, 512], BF16)
        t.jf = sb("jf%d" % i, [128, 256], F32)
        t.ukv = sb("ukv%d" % i, [128, 128], BF16)
        t.ukvT = sb("ukvT%d" % i, [128, 128], BF16)
        t.sq = sb("sq%d" % i, [128, 768], F32)
        t.tmp = sb("tmp%d" % i, [128, 768], F32)
        t.tmp2 = sb("tmp2%d" % i, [128, 768], F32)
        t.kfin = sb("kfin%d" % i, [128, 8, QK], BF16)
        t.krg = sb("krg%d" % i, [128, 32], F32)
        t.r1 = sb("r1%d" % i, [128, 8, 32], F32)
        t.r2 = sb("r2%d" % i, [128, 8, 32], F32)
        t.dfin = sb("dfin%d" % i, [128, 512], BF16)
        t.uq = sb("uq%d" % i, [128, 256], BF16)
        t.uqT = sb("uqT%d" % i, [128, 2, 128], BF16)
        return t
    slots = [slot_tiles(i) for i in range(NSL)]

    def blk_tiles(i):
        t = NS()
        t.KT = sb("KTblk%d" % i, [QK, NH, 512], BF16)
        t.V = sb("Vblk%d" % i, [128, NH, 4, 128], BF16)
        t.dKT = sb("dKTblk%d" % i, [128, DH, 512], BF16)
        t.dV = sb("dVblk%d" % i, [128, DH, 4, 128], BF16)
        t.QT = sb("QTblk%d" % i, [QK, NH, 512], BF16)
        t.dQT = sb("dQTblk%d" % i, [128, DH, 512], BF16)
        return t
    blks = [blk_tiles(i) for i in range(2)]

    def rsqrt_inplace(v, n, R):
        P.act(v, v, AF.Sqrt, R, R)
        P.recip(v, v, R, R)

    ntiles = C.NBLK * 4
    ENG_EW = "dve"

    def load_x(t):
        xt = xts[t % NXS]
        P.dma("sp", xt[:], G.xbuf[t * 128:(t + 1) * 128, :], [G.xbuf], [xt])
    load_x(0)
    if ntiles > 1:
        load_x(1)

    ENG_EW = "dve"

    def tile_info(t):
        pos, sub = t // 4, t % 4
        own = pos in C.own_pos
        halo_next = (pos + 1) in C.seg_starts
        need_q = own or (halo_next and sub == 3)
        return pos, sub, own, halo_next, need_q

    def front(t):
        xt = xts[t % NXS]
        T = slots[t % NSL]
        st = T.st
        P.act(T.junk[:], xt[:], AF.Square, [xt], [T.junk, st], accum_out=st[:, 0:1])
        P.ts("dve", st[:, 0:1], st[:, 0:1], 1.0 / D, EPS, ALU.mult, ALU.add, [st], [st])
        rsqrt_inplace(st[:, 0:1], 1, [st])
        P.tt("dve", st[:, 1:2], st[:, 0:1], st[:, 0:1], ALU.mult, [st], [st])
        P.tt("dve", T.xg[:], xt[:], gA[:], ALU.mult, [xt, gA], [T.xg])
        for c in range(8):
            P.tr(pTx[:, c, :], T.xg[:, c * 128:(c + 1) * 128], ident[:], [T.xg, ident], [pTx])
        P.copy("act", T.xT[:], pTx[:], [pTx], [T.xT])

    def zgroups(need_q):
        if need_q:
            g = [(pZ0, 0, 0, OKR + ROPE), (pZ1, 0, ODQ, ODK)]
        else:
            g = [(pZ0, OKV, OKV, OKR + ROPE)]
        return g + [(pZ2, 0, ODK, ODV), (pZ3, 0, ODV, IN_COLS)]

    def zmm(t):
        T = slots[t % NSL]
        need_q = tile_info(t)[4]
        for (pz, o0, c0, c1) in zgroups(need_q):
            for c in range(8):
                P.mm(pz[:, o0:o0 + (c1 - c0)], T.xT[:, c, :], win[:, c, c0:c1], c == 0, c == 7, [T.xT, win], [pz])

    def evac(t):
        T = slots[t % NSL]
        need_q = tile_info(t)[4]
        for gi, (pz, o0, c0, c1) in enumerate(zgroups(need_q)):
            P.copy("act" if gi % 2 == 0 else "dve", T.zsb[:, c0:c1], pz[:, o0:o0 + (c1 - c0)], [pz], [T.zsb])

    def kv_head(t):
        T = slots[t % NSL]
        st, stk, zsb = T.st, T.stk, T.zsb
        zkv = zsb[:, OKV:OKV + KVR]
        P.act(T.jf[:, 0:KVR], zkv, AF.Square, [zsb], [T.jf, stk], accum_out=stk[:, 2:3])
        P.tt("dve", T.ukv[:], zkv, gkvl[:], ALU.mult, [zsb, gkvl], [T.ukv])
        P.tr(pT[:, 0, :], T.ukv[:], ident[:], [T.ukv, ident], [pT])
        P.copy("act", T.ukvT[:], pT[:, 0, :], [pT], [T.ukvT])
        for hf in range(2):
            P.mm(pK[:, hf * 512:(hf + 1) * 512], T.ukvT[:], wukv[:, hf * 512:(hf + 1) * 512], True, True,
                 [T.ukvT, wukv], [pK])

    def k_pre(t):
        pos, sub, own, halo_next, need_q = tile_info(t)
        B = blks[pos % 2]
        T = slots[t % NSL]
        st, stk, sh, zsb = T.st, T.stk, T.shk, T.zsb
        P.tt("dve", stk[:, 3:4], stk[:, 2:3], st[:, 1:2], ALU.mult, [stk, st], [stk])
        P.ts("dve", stk[:, 3:4], stk[:, 3:4], 1.0 / KVR, EPS, ALU.mult, ALU.add, [stk], [stk])
        rsqrt_inplace(stk[:, 3:4], 1, [stk])
        P.tt("dve", stk[:, 3:4], stk[:, 3:4], st[:, 0:1], ALU.mult, [stk, st], [stk])
        P.tt("dve", stk[:, 4:5], stk[:, 3:4], stk[:, 3:4], ALU.mult, [stk], [stk])
        zkr = zsb[:, OKR:OKR + ROPE]
        P.act(T.jf[:, 128:160], zkr, AF.Square, [zsb], [T.jf, stk], accum_out=stk[:, 5:6])
        P.tt("dve", stk[:, 5:6], stk[:, 5:6], st[:, 1:2], ALU.mult, [stk, st], [stk])
        pK3 = pK[:].rearrange("p (h c) -> p h c", h=NH)
        sqk3 = T.sq[:, 0:512].rearrange("p (h d) -> p h d", h=NH)
        P.act(sqk3, pK3[:, :, 0:NOPE], AF.Square, [pK], [T.sq])
        P.red(sh[:, :, 0], sqk3, [T.sq], [sh])
        P.ts("dve", sh[:, :, 0], sh[:, :, 0], stk[:, 4:5], stk[:, 5:6], ALU.mult, ALU.add, [sh, stk], [sh])
        P.ts("dve", sh[:, :, 0], sh[:, :, 0], 1.0 / QK, EPS, ALU.mult, ALU.add, [sh], [sh])
        rsqrt_inplace(sh[:, :, 0], 8, [sh])
        P.ts("dve", sh[:, :, 1], sh[:, :, 0], stk[:, 3:4], None, ALU.mult, None, [sh, stk], [sh])
        P.ts("dve", sh[:, :, 2], sh[:, :, 0], st[:, 0:1], None, ALU.mult, None, [sh, st], [sh])
        tk3 = T.tmp[:, 0:512].rearrange("p (h d) -> p h d", h=NH)
        P.tt("dve", tk3, pK3[:, :, 0:NOPE], sh[:, :, 1:2].to_broadcast([128, NH, NOPE]), ALU.mult, [pK, sh], [T.tmp])
        P.act(B.V[:, :, sub, 0:VD], pK3[:, :, NOPE:128], AF.Copy, [pK, stk], [B.V], scale=stk[:, 3:4])
        P.copy("pool", B.V[:, :, sub, VD:128], valid[:, t:t + 1].unsqueeze(2).to_broadcast([128, NH, 64]),
               [valid], [B.V])
        P.tt(ENG_EW, T.kfin[:, :, 0:NOPE], tk3, kg[:, 0:NOPE].unsqueeze(1).to_broadcast([128, NH, NOPE]), ALU.mult,
             [T.tmp, kg], [T.kfin])
        P.tt("dve", T.krg[:], zkr, kg[:, NOPE:QK], ALU.mult, [zsb, kg], [T.krg])
        cc, sn = ccs[pos % 2], sns[pos % 2]
        P.tt("pool", T.r1[:, 0, :], T.krg[:], cc[:, sub, :], ALU.mult, [T.krg, cc], [T.r1])
        P.tt("pool", T.r2[:, 0, 0:16], T.krg[:, 16:32], sn[:, sub, 0:16], ALU.mult, [T.krg, sn], [T.r2])
        P.tt("pool", T.r2[:, 0, 16:32], T.krg[:, 0:16], sn[:, sub, 16:32], ALU.mult, [T.krg, sn], [T.r2])
        P.tt("pool", T.r1[:, 0, :], T.r1[:, 0, :], T.r2[:, 0, :], ALU.add, [T.r1, T.r2], [T.r1])
        P.tt("dve", T.kfin[:, :, NOPE:QK], T.r1[:, 0:1, :].to_broadcast([128, NH, ROPE]),
             sh[:, :, 2:3].to_broadcast([128, NH, ROPE]), ALU.mult, [T.r1, sh], [T.kfin])

    def k_tail(t):
        pos, sub = tile_info(t)[0:2]
        B = blks[pos % 2]
        T = slots[t % NSL]
        for h in range(NH):
            P.tr(pT[0:QK, h, :], T.kfin[:, h, :], ident[:], [T.kfin, ident], [pT])
        P.copy("act", B.KT[:, :, sub * 128:(sub + 1) * 128], pT[0:QK, :, :], [pT], [B.KT])

    def dqk_pre(t, which):
        T = slots[t % NSL]
        st, zsb = T.st, T.zsb
        sqx, tmpx, shx, dfx = (T.sqd, T.tmpd, T.shd, T.dfin) if which == 0 else (T.sqe, T.tmpe, T.she, T.dfin2)
        zap = zsb[:, ODK:ODV] if which == 0 else zsb[:, ODQ:ODK]
        gain = dkg if which == 0 else dqg
        sq3 = sqx[:].rearrange("p (m d) -> p m d", m=8)
        P.act(sqx[:], zap, AF.Square, [zsb], [sqx])
        P.red(shx[:, :, 0], sq3, [sqx], [shx])
        P.ts("dve", shx[:, :, 0], shx[:, :, 0], st[:, 1:2], None, ALU.mult, None, [shx, st], [shx])
        P.ts("dve", shx[:, :, 0], shx[:, :, 0], 1.0 / DQK, EPS, ALU.mult, ALU.add, [shx], [shx])
        rsqrt_inplace(shx[:, :, 0], 8, [shx])
        P.ts("dve", shx[:, :, 0], shx[:, :, 0], st[:, 0:1], None, ALU.mult, None, [shx, st], [shx])
        tmp3 = tmpx[:].rearrange("p (m d) -> p m d", m=8)
        P.tt("dve", tmp3, zap.rearrange("p (m d) -> p m d", m=8),
             shx[:, :, 0:1].to_broadcast([128, 8, DQK]), ALU.mult, [zsb, shx], [tmpx])
        P.tt(ENG_EW, dfx[:].rearrange("p (m d) -> p m d", m=8), tmp3,
             gain[:].unsqueeze(1).to_broadcast([128, 8, DQK]), ALU.mult, [tmpx, gain], [dfx])

    def dqk_tail(t, which):
        pos, sub, own = tile_info(t)[0:3]
        B = blks[pos % 2]
        T = slots[t % NSL]
        dfx = T.dfin if which == 0 else T.dfin2
        for h in range(DH):
            P.tr(pT[:, h, :], dfx[:, h * 128:(h + 1) * 128], ident[:], [dfx, ident], [pT])
        if which == 0:
            P.copy("act", B.dKT[:, :, sub * 128:(sub + 1) * 128], pT[:, 0:DH, :], [pT], [B.dKT])
        elif own:
            P.copy("act", B.dQT[:, :, sub * 128:(sub + 1) * 128], pT[:, 0:DH, :], [pT], [B.dQT])
        else:
            P.copy("act", B.dQT[:, :, 0:64], pT[:, 0:DH, 64:128], [pT], [B.dQT])

    def dv_copy(t):
        pos, sub = tile_info(t)[0:2]
        B = blks[pos % 2]
        T = slots[t % NSL]
        P.act(B.dV[:, :, sub, :], T.zsb[:, ODV:IN_COLS].rearrange("p (h c) -> p h c", h=DH), AF.Copy, [T.zsb, T.st], [B.dV],
              scale=T.st[:, 0:1])

    def q_head(t):
        T = slots[t % NSL]
        st, stq, zsb = T.st, T.stq, T.zsb
        zq = zsb[:, OQ:OQ + QR]
        P.act(T.jf[:, 0:QR], zq, AF.Square, [zsb], [T.jf, stq], accum_out=stq[:, 6:7])
        P.tt("dve", T.uq[:], zq, gql[:], ALU.mult, [zsb, gql], [T.uq])
        for c in range(2):
            P.tr(pT[:, 1 + c, :], T.uq[:, c * 128:(c + 1) * 128], ident[:], [T.uq, ident], [pT])
        P.copy("act", T.uqT[:], pT[:, 1:3, :], [pT], [T.uqT])
        for (c0, c1) in ((0, 512), (512, 768)):
            for c in range(2):
                P.mm(pK[:, c0:c1], T.uqT[:, c, :], wuq[:, c, c0:c1], c == 0, c == 1, [T.uqT, wuq], [pK])

    def q_pre(t):
        pos, sub = tile_info(t)[0:2]
        cc, sn = ccs[pos % 2], sns[pos % 2]
        T = slots[t % NSL]
        st, stq, sh = T.st, T.stq, T.shq
        P.tt("dve", stq[:, 7:8], stq[:, 6:7], st[:, 1:2], ALU.mult, [stq, st], [stq])
        P.ts("dve", stq[:, 7:8], stq[:, 7:8], 1.0 / QR, EPS, ALU.mult, ALU.add, [stq], [stq])
        rsqrt_inplace(stq[:, 7:8], 1, [stq])
        P.tt("dve", stq[:, 7:8], stq[:, 7:8], st[:, 0:1], ALU.mult, [stq, st], [stq])
        P.tt("dve", stq[:, 8:9], stq[:, 7:8], stq[:, 7:8], ALU.mult, [stq], [stq])
        pQ3 = pK[:, 0:NH * QK].rearrange("p (h d) -> p h d", h=NH)
        sqq3 = T.sq[:, 0:NH * QK].rearrange("p (h d) -> p h d", h=NH)
        P.act(T.sq[:, 0:NH * QK], pK[:, 0:NH * QK], AF.Square, [pK], [T.sq])
        P.red(sh[:, :, 4], sqq3, [T.sq], [sh])
        P.ts("dve", sh[:, :, 4], sh[:, :, 4], stq[:, 8:9], None, ALU.mult, None, [sh, stq], [sh])
        P.ts("dve", sh[:, :, 4], sh[:, :, 4], 1.0 / QK, EPS, ALU.mult, ALU.add, [sh], [sh])
        rsqrt_inplace(sh[:, :, 4], 8, [sh])
        P.ts("dve", sh[:, :, 4], sh[:, :, 4], stq[:, 7:8], None, ALU.mult, None, [sh, stq], [sh])
        tq3 = T.tmp[:, 0:NH * QK].rearrange("p (h d) -> p h d", h=NH)
        tg3 = T.tmp2[:, 0:NH * QK].rearrange("p (h d) -> p h d", h=NH)
        P.tt("dve", tq3, pQ3, sh[:, :, 4:5].to_broadcast([128, NH, QK]), ALU.mult, [pK, sh], [T.tmp])
        P.tt(ENG_EW, tg3, tq3, qg[:].unsqueeze(1).to_broadcast([128, NH, QK]), ALU.mult, [T.tmp, qg], [T.tmp2])
        qfin = T.kfin
        P.copy("pool", qfin[:, :, 0:NOPE], tg3[:, :, 0:NOPE], [T.tmp2], [qfin])
        P.tt(ENG_EW, T.r1[:], tg3[:, :, NOPE:QK], cc[:, sub:sub + 1, :].to_broadcast([128, NH, ROPE]), ALU.mult,
             [T.tmp2, cc], [T.r1])
        P.tt(ENG_EW, T.r2[:, :, 0:16], tg3[:, :, NOPE + 16:QK], sn[:, sub:sub + 1, 0:16].to_broadcast([128, NH, 16]),
             ALU.mult, [T.tmp2, sn], [T.r2])
        P.tt(ENG_EW, T.r2[:, :, 16:32], tg3[:, :, NOPE:NOPE + 16], sn[:, sub:sub + 1, 16:32].to_broadcast([128, NH, 16]),
             ALU.mult, [T.tmp2, sn], [T.r2])
        P.tt(ENG_EW, qfin[:, :, NOPE:QK], T.r1[:], T.r2[:], ALU.add, [T.r1, T.r2], [qfin])

    def q_tail(t):
        pos, sub, own = tile_info(t)[0:3]
        B = blks[pos % 2]
        T = slots[t % NSL]
        qfin = T.kfin
        for h in range(NH):
            P.tr(pT[0:QK, h, :], qfin[:, h, :], ident[:], [qfin, ident], [pT])
        if own:
            P.copy("act", B.QT[:, :, sub * 128:(sub + 1) * 128], pT[0:QK, :, :], [pT], [B.QT])
        else:
            P.copy("act", B.QT[:, :, 0:64], pT[0:QK, :, 64:128], [pT], [B.QT])

    def captured(fn, *a):
        P.capture()
        fn(*a)
        return P.end_capture()

    def back_kv(t):
        kv_head(t)
        P.emit_zip([captured(k_pre, t), captured(dqk_pre, t, 0)])
        k_tail(t)
        dqk_tail(t, 0)
        dv_copy(t)

    def back_rest(t):
        if not tile_info(t)[4]:
            return
        q_head(t)
        P.emit_zip([captured(q_pre, t), captured(dqk_pre, t, 1)])
        q_tail(t)
        dqk_tail(t, 1)

    front(0)
    zmm(0)
    evac(0)
    for pos in range(C.NBLK):
        own = pos in C.own_pos
        halo_next = (pos + 1) in C.seg_starts
        B = blks[pos % 2]
        P.dma("sp", ccs[pos % 2][:], G.rope_cc[:, pos * 4:pos * 4 + 4, :], [G.rope_cc], [ccs[pos % 2]])
        P.dma("sp", sns[pos % 2][:], G.rope_sn[:, pos * 4:pos * 4 + 4, :], [G.rope_sn], [sns[pos % 2]])
        for sub in range(4):
            t = pos * 4 + sub
            if t + 2 < ntiles:
                load_x(t + 2)
            if t + 1 < ntiles:
                front(t + 1)
            back_kv(t)
            if t + 1 < ntiles:
                zmm(t + 1)
            back_rest(t)
            if t + 1 < ntiles:
                evac(t + 1)
            if int(os.environ.get("K_SERIAL", "0")):
                P.flush()
        c0 = pos * 512
        P.dma("sp", G.KTm[:, :, c0:c0 + 512].rearrange("h p s -> p h s"), B.KT[:], [B.KT], [G.KTm], chan=("st", "KT", pos % 2))
        P.dma("sp", G.Vm[:, :, pos * 4:pos * 4 + 4, :].rearrange("h p t c -> p h t c"), B.V[:], [B.V], [G.Vm],
              chan=("st", "V", pos % 2))
        P.dma("sp", G.KTd[:, :, c0:c0 + 512].rearrange("h p s -> p h s"), B.dKT[:], [B.dKT], [G.KTd],
              chan=("st", "dKT", pos % 2))
        P.dma("sp", G.Vd[:, :, pos * 4:pos * 4 + 4, :].rearrange("h p t c -> p h t c"), B.dV[:], [B.dV], [G.Vd],
              chan=("st", "dV", pos % 2))
        if own:
            q0 = C.qcol(pos)
            P.dma("sp", G.QTm[:, :, q0:q0 + 512].rearrange("h p s -> p h s"), B.QT[:], [B.QT], [G.QTm],
                  chan=("st", "QT", pos % 2))
            P.dma("sp", G.QTd[:, :, q0:q0 + 512].rearrange("h p s -> p h s"), B.dQT[:], [B.dQT], [G.QTd],
                  chan=("st", "dQT", pos % 2))
        elif halo_next:
            q0 = C.halocol(pos + 1)
            P.dma("sp", G.QTm[:, :, q0:q0 + 64].rearrange("h p s -> p h s"), B.QT[:, :, 0:64], [B.QT], [G.QTm],
                  chan=("st", "QT", pos % 2))
            P.dma("sp", G.QTd[:, :, q0:q0 + 64].rearrange("h p s -> p h s"), B.dQT[:, :, 0:64], [B.dQT], [G.QTd],
                  chan=("st", "dQT", pos % 2))
    P.flush()
    P.end_scope(old)


def phase_B(P, C, G):
    old = P.phase_scope()
    sb, ps = P.sbuf, P.psum
    NT, NQ, S = C.NT, C.NQ, C.S
    Sm = [ps("Sm%d" % k, [128, 1024], F32) for k in range(2)]
    Ab = [ps("A%d" % k, [128, 512], F32) for k in range(4)]
    ones_bf = sb("ones_bf", [128, 128], BF16)
    P.memset("pool", ones_bf[:], 1.0, [], [ones_bf])
    epsb = sb("epsb", [128, 1], F32)
    P.memset("pool", epsb[:], EPS, [], [epsb])
    valid = sb("validB", [128, NT], F32)
    P.dma("sp", valid[:], G.valid[:, :], [G.valid], [valid])
    validT = sb("validT", [128, NT, 128], BF16)
    P.copy("pool", validT[:], valid[:].unsqueeze(2).to_broadcast([128, NT, 128]), [valid], [validT])
    lam4 = sb("lam4", [128, 4, DQK], F32)
    for k, nm in enumerate(("lambda_q1", "lambda_k1", "lambda_q2", "lambda_k2")):
        P.dma("sp", lam4[:, k, :], bc_ap(getattr(G, nm), DQK), [getattr(G, nm)], [lam4], chan=("lam", k))
    lamt = sb("lamt", [128, 8], F32)
    lamp = sb("lamp", [128, 2, DQK], F32)
    P.tt("dve", lamp[:, 0, :], lam4[:, 0, :], lam4[:, 1, :], ALU.mult, [lam4], [lamp])
    P.tt("dve", lamp[:, 1, :], lam4[:, 2, :], lam4[:, 3, :], ALU.mult, [lam4], [lamp])
    P.red(lamt[:, 0:2], lamp[:], [lamp], [lamt])
    P.act(lamt[:, 2:4], lamt[:, 0:2], AF.Exp, [lamt], [lamt])
    P.tt("dve", lamt[:, 4:5], lamt[:, 3:4], lamt[:, 2:3], ALU.subtract, [lamt], [lamt])
    P.ts("dve", lamt[:, 5:6], lamt[:, 4:5], -LAM_INIT, None, ALU.add, None, [lamt], [lamt])
    neglam = lamt[:, 5:6]
    gsc = sb("gsc", [128, 1], F32)
    P.dma("sp", gsc[:], bass.AP(G.diff_out_norm_g.t, 0, [[1, 128], [1, 1]]), [G.diff_out_norm_g], [gsc])
    P.ts("dve", gsc[:], gsc[:], 1.0 - LAM_INIT, None, ALU.mult, None, [gsc], [gsc])
    KTs = [sb("KTs%d" % k, [128, S], BF16) for k in range(2)]
    Vs = [sb("Vs%d" % k, [128, NT, 128], BF16) for k in range(2)]
    Qs = [sb("Qs%d" % k, [128, NQ], BF16) for k in range(2)]
    NPT = 3
    Pts = [sb("Pt%d" % k, [128, 1024], BF16) for k in range(NPT)]
    densb = [sb("densb%d" % k, [128, 512], F32) for k in range(2)]
    tnum = [sb("tnum%d" % k, [128, 512], F32) for k in range(2)]
    osb = sb("osb", [128, 512], F32)
    sqb = sb("sqb", [128, 512], BF16)
    rstd = sb("rstdB", [128, 512], F32)
    ystage = [sb("ystage%d" % k, [128, 512], BF16) for k in range(2)]

    heads = [("m", h) for h in range(NH)] + [("d", h) for h in range(DH)]

    def load_head(idx):
        kind, h = heads[idx]
        sl = idx % 2
        if kind == "m":
            P.dma("sp", KTs[sl][0:QK, :], G.KTm[h], [G.KTm], [KTs[sl]])
            P.dma("sp", Vs[sl][:], G.Vm[h], [G.Vm], [Vs[sl]])
            P.dma("sp", Qs[sl][0:QK, :], G.QTm[h], [G.QTm], [Qs[sl]])
        else:
            P.dma("sp", KTs[sl][:], G.KTd[h], [G.KTd], [KTs[sl]])
            P.dma("sp", Vs[sl][:], G.Vd[h], [G.Vd], [Vs[sl]])
            P.dma("sp", Qs[sl][:], G.QTd[h], [G.QTd], [Qs[sl]])

    qtiles = []
    for pos in C.own_pos:
        if pos in C.seg_starts:
            qtiles.append((C.halocol(pos), 64, 4 * pos, False))
        qtiles.append((C.qcol(pos), 512, 4 * pos, True))

    def batches_of(kind, w, nfull, diag):
        bl = []
        nmap = 1 if kind == "m" else 2
        if diag:
            for j in range(4):
                ww = 512 - 128 * j
                exps = [(0, ww)] if nmap == 1 else [(0, ww), (512, 512 + ww)]
                bl.append(([(nfull + j, 0, 128 * j, ww, True, 0)], exps))
        if w == 512:
            if nmap == 1:
                for kt in range(0, nfull, 2):
                    bl.append(([(kt, 0, 0, 512, False, None), (kt + 1, 512, 0, 512, False, None)], [(0, 1024)]))
            else:
                for kt in range(nfull):
                    d = nfull - kt
                    bl.append(([(kt, 0, 0, 512, False, d if d <= NEAR else None)], [(0, 1024)]))
        else:
            per = 16 if nmap == 1 else 8
            for k0 in range(0, nfull, per):
                n_ = min(per, nfull - k0)
                ents = []
                for i_ in range(n_):
                    d = nfull - (k0 + i_)
                    ents.append((k0 + i_, 64 * i_, 0, 64, False, (d - 1) if d <= 6 else None))
                exps = [(0, 64 * n_)] if nmap == 1 else ([(0, 1024)] if n_ == 8 else [(0, 64 * n_), (512, 512 + 64 * n_)])
                bl.append((ents, exps))
        return bl

    load_head(0)
    late = {}

    def late_setup():
        tab = sb("tab", [NUM_BUCKETS, DH], F32)
        P.dma("sp", tab[:], G.rel_bias[:, :], [G.rel_bias], [tab])
        tabh = sb("tabh", [NUM_BUCKETS, DH], BF16)
        tabl = sb("tabl", [NUM_BUCKETS, DH], BF16)
        tabr = sb("tabr", [NUM_BUCKETS, DH], F32)
        P.copy("dve", tabh[:], tab[:], [tab], [tabh])
        P.copy("dve", tabr[:], tabh[:], [tabh], [tabr])
        P.tt("dve", tabr[:], tab[:], tabr[:], ALU.subtract, [tab, tabr], [tabr])
        P.copy("dve", tabl[:], tabr[:], [tabr], [tabl])
        ohf = sb("ohf", [NUM_BUCKETS, LV], F32)
        P.dma("sp", ohf[:], G.t5_onehot[:, :], [G.t5_onehot], [ohf])
        ohb = sb("ohb", [NUM_BUCKETS, LV], BF16)
        P.copy("dve", ohb[:], ohf[:], [ohf], [ohb])
        tbh = sb("tbh", [NUM_BUCKETS, DH, 128], BF16)
        tbl = sb("tbl", [NUM_BUCKETS, DH, 128], BF16)
        P.copy("dve", tbh[:], tabh[:].unsqueeze(2).to_broadcast([NUM_BUCKETS, DH, 128]), [tabh], [tbh])
        P.copy("dve", tbl[:], tabl[:].unsqueeze(2).to_broadcast([NUM_BUCKETS, DH, 128]), [tabl], [tbl])
        Rsbs = [sb("Rsb%d" % k, [128, LV], F32) for k in range(DH)]
        cfar = sb("cfar", [128, DH], F32)
        ncfar = sb("ncfar", [128, DH], F32)
        Et = sb("Et", [128, DH * 6, 512], BF16)
        Etmp = [sb("Etmp%d" % k, [128, 512], F32) for k in range(2)]
        cgs = [(0, 512), (512, 1024), (1024, LV)]
        for h in range(DH):
            Rsb = Rsbs[h]
            for k, (c0, c1) in enumerate(cgs):
                P.mm(Ab[k][:, 0:c1 - c0], tbh[:, h, :], ohb[:, c0:c1], True, False, [tbh, ohb], [Ab[k]])
                P.mm(Ab[k][:, 0:c1 - c0], tbl[:, h, :], ohb[:, c0:c1], False, True, [tbl, ohb], [Ab[k]])
                P.copy("act", Rsb[:, c0:c1], Ab[k][:, 0:c1 - c0], [Ab[k]], [Rsb])
        for h in range(DH):
            Rsb = Rsbs[h]
            P.copy("dve", cfar[:, h:h + 1], Rsb[:, LV - 1:LV], [Rsb], [cfar])
            P.ts("dve", ncfar[:, h:h + 1], cfar[:, h:h + 1], -1.0, None, ALU.mult, None, [cfar], [ncfar])
            P.act(Rsb[:], Rsb[:], AF.Exp, [Rsb, ncfar], [Rsb], bias=ncfar[:, h:h + 1])
            P.dma("sp", G.Rs[h], Rsb[:], [Rsb], [G.Rs], chan=("Rs", h))
        for h in range(DH):
            for d in range(6):
                et = Etmp[(h * 6 + d) % 2]
                src = bass.AP(G.Rs.t, h * 128 * LV + RMAX + 128 * d, [[LV - 1, 128], [1, 512]])
                P.dma("sp", et[:], src, [G.Rs], [et])
                P.copy("pool", Et[:, h * 6 + d, :], et[:], [et], [Et])
        for f in range(NF):
            P.dma("pool", G.wgu[f, :, 0, :, :], G.w_gate[0][:, f * 128:(f + 1) * 128].rearrange("(c p) j -> p c j", p=128),
                  [G.w_gate], [G.wgu], chan=("wprep", (3 * f) % 6))
            P.dma("pool", G.wgu[f, :, 1, :, :], G.w_up[0][:, f * 128:(f + 1) * 128].rearrange("(c p) j -> p c j", p=128),
                  [G.w_up], [G.wgu], chan=("wprep", (3 * f + 1) % 6))
            P.dma("pool", G.wdn[:, f, :], G.w_down[0][f * 128:(f + 1) * 128, :], [G.w_down], [G.wdn],
                  chan=("wprep", (3 * f + 2) % 6))


        late["Et"], late["cfar"] = Et, cfar

    ycount = 0
    bcount = 0
    for idx, (kind, h) in enumerate(heads):
        if idx + 1 < len(heads):
            load_head(idx + 1)
        if idx == 1:
            late_setup()
        Et, cfar = late.get("Et"), late.get("cfar")
        sl = idx % 2
        KT, V, Q = KTs[sl], Vs[sl], Qs[sl]
        nmap = 1 if kind == "m" else 2
        kdim = QK if kind == "m" else DQK
        sm_scale = SM_MLA if kind == "m" else SM_DIFF
        for ti, (qc, w, nfull, diag) in enumerate(qtiles):
            bl = batches_of(kind, w, nfull, diag)
            nb = len(bl)
            if kind == "m":
                accs = [(Ab[ti % 2], V)]
            else:
                accs = None
            total_pv = sum(len(b[0]) for b in bl)

            def s_mm(bi_):
                S_ = Sm[(bcount + bi_) % 2]
                for (kt, so, qo, ww, _, _) in bl[bi_][0]:
                    for m in range(nmap):
                        P.mm(S_[:, m * 512 + so:m * 512 + so + ww], KT[64 * m:64 * m + kdim, kt * 128:(kt + 1) * 128],
                             Q[64 * m:64 * m + kdim, qc + qo:qc + qo + ww], True, True, [KT, Q], [S_])
            s_mm(0)
            pv = 0
            for bi_ in range(nb):
                ents, exps = bl[bi_]
                if bi_ + 1 < nb:
                    s_mm(bi_ + 1)
                S_ = Sm[(bcount + bi_) % 2]
                pt = Pts[(bcount + bi_) % NPT]
                for (c0, c1) in exps:
                    if kind == "m":
                        P.act(pt[:, c0:c1], S_[:, c0:c1], AF.Exp, [S_], [pt], scale=sm_scale)
                    else:
                        P.act(pt[:, c0:c1], S_[:, c0:c1], AF.Exp, [S_, cfar], [pt], scale=sm_scale, bias=cfar[:, h:h + 1])
                for (kt, so, qo, ww, mask, near) in ents:
                    if kind == "d" and near is not None:
                        e0 = 64 if w == 64 else 0
                        pv3 = pt[:].rearrange("p (m c) -> p m c", m=2)[:, :, so:so + ww]
                        P.tt("dve", pv3, pv3, Et[:, h * 6 + near, e0:e0 + ww].unsqueeze(1).to_broadcast([128, 2, ww]),
                             ALU.mult, [pt, Et], [pt])
                    if mask:
                        for m in range(nmap):
                            P.memset("dve", pt[64:128, m * 512 + so:m * 512 + so + 64], 0.0, [pt], [pt])
                    first, last = (pv == 0), (pv == total_pv - 1)
                    pv += 1
                    if kind == "m":
                        O = Ab[ti % 2]
                        P.mm(O[:, qo:qo + ww], V[:, kt, :], pt[:, so:so + ww], first, last, [V, pt], [O])
                    else:
                        for m in range(2):
                            P.mm(Ab[2 * m][:, qo:qo + ww], V[:, kt, :], pt[:, m * 512 + so:m * 512 + so + ww], first, last,
                                 [V, pt], [Ab[2 * m]])
                            P.mm(Ab[2 * m + 1][:, qo:qo + ww], validT[:, kt, :], pt[:, m * 512 + so:m * 512 + so + ww],
                                 first, last, [validT, pt], [Ab[2 * m + 1]])
            bcount += nb
            if kind == "m":
                O = Ab[ti % 2]
                ds = densb[ti % 2]
                P.copy("act", ds[0:64, 0:w], O[64:128, 0:w], [O], [ds])
                P.ts("dve", ds[0:64, 0:w], ds[0:64, 0:w], 1e-30, None, ALU.max, None, [ds], [ds])
                P.recip(ds[0:64, 0:w], ds[0:64, 0:w], [ds], [ds])
                ys = ystage[ycount % 2]
                ycount += 1
                P.tt("dve", ys[0:64, 0:w], O[0:64, 0:w], ds[0:64, 0:w], ALU.mult, [O, ds], [ys])
                r0 = (h % 2) * 64
                P.dma("sp", G.yT[r0:r0 + 64, h // 2, qc:qc + w], ys[0:64, 0:w], [ys], [G.yT],
                      chan=("yst", ycount % 2))
            else:
                for m in range(2):
                    P.copy("act", densb[m][:, 0:w], Ab[2 * m + 1][:, 0:w], [Ab[2 * m + 1]], [densb[m]])
                    P.copy("act", tnum[m][:, 0:w], Ab[2 * m][:, 0:w], [Ab[2 * m]], [tnum[m]])
                for m in range(2):
                    ds = densb[m]
                    P.ts("dve", ds[:, 0:w], ds[:, 0:w], 1e-30, None, ALU.max, None, [ds], [ds])
                    P.recip(ds[:, 0:w], ds[:, 0:w], [ds], [ds])
                    P.tt("dve", tnum[m][:, 0:w], tnum[m][:, 0:w], ds[:, 0:w], ALU.mult, [tnum[m], ds], [tnum[m]])
                ys = ystage[ycount % 2]
                ycount += 1
                P.stt("dve", ys[:, 0:w], tnum[1][:, 0:w], neglam, tnum[0][:, 0:w], ALU.mult, ALU.add,
                      [tnum[0], tnum[1], lamt], [ys])
                P.dma("sp", G.yT[:, 4 + h, qc:qc + w], ys[:, 0:w], [ys], [G.yT], chan=("yst", ycount % 2))
    P.flush()
    P.end_scope(old)


def phase_C(P, C, G):
    old = P.phase_scope()
    sb, ps = P.sbuf, P.psum
    pX = ps("pX", [128, D], F32)
    pTt = ps("pTt", [128, 8, 128], BF16)
    pG = [ps("pG%d" % k, [128, 512], F32) for k in range(2)]
    pU = [ps("pU%d" % k, [128, 512], F32) for k in range(2)]
    pH = ps("pH", [128, 512], F32)
    wout = sb("wout", [128, 8, D], BF16)
    wpg = sb("wpg", [128, 8, D], BF16)
    wpp = sb("wpp", [128, 2, D], BF16)
    wd = sb("wd", [128, NF, D], BF16)
    P.dma("pool", wout[:], G.w_out[0].rearrange("(c p) n -> p c n", p=128), [G.w_out], [wout])
    P.dma("pool", wpg[:], G.w_ple_gate[0].rearrange("(c p) n -> p c n", p=128), [G.w_ple_gate], [wpg])
    P.dma("pool", wpp[:], G.w_ple_proj[0].rearrange("(c p) n -> p c n", p=128), [G.w_ple_proj], [wpp])
    P.dma("sp", wd[:], G.wdn[:, :, :], [G.wdn], [wd])
    gF = sb("gF", [128, D], F32)
    gP = sb("gP", [128, D], F32)
    P.dma("sp", gF[:], bc_ap(G.ffn_norm_g, D), [G.ffn_norm_g], [gF])
    P.dma("sp", gP[:], bc_ap(G.ple_norm_g, D), [G.ple_norm_g], [gP])
    ident = sb("identC", [128, 128], BF16)
    P.memset("pool", ident[:], 1.0, [], [ident])
    P.op("pool", lambda e: e.affine_select(ident[:], ident[:], [[-1, 128]], ALU.is_equal, 0.0,
                                           base=0, channel_multiplier=1), [ident], [ident])
    identf = sb("identf", [128, 128], F32)
    P.copy("pool", identf[:], ident[:], [ident], [identf])
    cwr = sb("cwr", [NF, 4, 128], F32)
    for jj in range(3):
        P.dma("sp", cwr[:, jj, :], G.conv_w[0, jj].rearrange("(f p) -> f p", p=128), [G.conv_w], [cwr], chan=("cw", jj))
    P.dma("sp", cwr[:, 3, :], G.conv_b[0].rearrange("(f p) -> f p", p=128), [G.conv_b], [cwr], chan=("cw", 3))
    cw = sb("cw", [128, 4, NF], F32)
    pHc = pH[:, 0:4 * NF].rearrange("p (j f) -> p j f", j=4)
    for jj in range(4):
        P.op("pe", lambda e, jj=jj: e.transpose(pHc[:, jj, :], cwr[:, jj, :], identf[0:NF, 0:NF]), [cwr, identf], [pH])
    P.copy("act", cw[:], pHc, [pH], [cw])
    yTb = [sb("yTb%d" % k, [128, 8, 512], BF16) for k in range(1)]
    yTh = sb("yTh", [128, 8, 64], BF16)
    xin = [sb("xin%d" % k, [128, D], F32) for k in range(2)]
    x1b = sb("x1b", [128, 4, D], F32)
    h2 = [sb("h2_%d" % k, [128, D], BF16) for k in range(3)]
    junk = sb("junkC", [128, D], BF16)
    stc = [sb("stc%d" % k, [128, 4], F32) for k in range(3)]
    h2T = sb("h2T", [128, 8, 512], BF16)
    h2Th = sb("h2Th", [128, 8, 64], BF16)
    actT = sb("actT", [128, NF, 512], BF16)
    wgu = [sb("wgu%d" % k, [128, 2, 8, 128], BF16) for k in range(3)]
    gext = [sb("gext%d" % k, [128, 514], F32) for k in range(2)]
    carry = sb("carry", [128, NF, 2], F32)
    cacc = [sb("cacc%d" % k, [128, 512], F32) for k in range(2)]
    sl = [sb("sl%d" % k, [128, 512], F32) for k in range(2)]
    h3T = sb("h3T", [128, 8, 128], BF16)
    pin = [sb("pin%d" % k, [128, PLE], F32) for k in range(2)]
    pb = sb("pb", [128, PLE], BF16)
    pTT = sb("pTT", [128, 2, 128], BF16)
    sig = sb("sig", [128, D], F32)
    outt = [sb("outt%d" % k, [128, D], F32) for k in range(2)]
    P.memset("pool", carry[:], 0.0, [], [carry])
    ones_c = sb("ones_c", [128, 128], BF16)
    P.memset("pool", ones_c[:], 1.0, [], [ones_c])
    epsc = sb("epsc", [128, 1], F32)
    P.memset("pool", epsc[:], EPS, [], [epsc])
    gscc = sb("gscc", [128, 1], F32)
    P.dma("sp", gscc[:], bass.AP(G.diff_out_norm_g.t, 0, [[1, 128], [1, 1]]), [G.diff_out_norm_g], [gscc])
    P.ts("dve", gscc[:], gscc[:], 1.0 - LAM_INIT, None, ALU.mult, None, [gscc], [gscc])
    sqc = sb("sqc", [128, 512], BF16)
    rsc = sig

    def diff_out_norm(yt, w):
        for hh in range(DH):
            ych = yt[:, 4 + hh, 0:w]
            P.act(sqc[:, 0:w], ych, AF.Square, [yt], [sqc])
            P.mm(pH[:, 0:w], ones_c[:], sqc[:, 0:w], True, True, [ones_c, sqc], [pH])
            P.act(rsc[:, 0:w], pH[:, 0:w], AF.Sqrt, [pH, epsc], [rsc], scale=1.0 / DV, bias=epsc[:])
            P.recip(rsc[:, 0:w], rsc[:, 0:w], [rsc], [rsc])
            P.stt("dve", ych, ych, gscc[:], rsc[:, 0:w], ALU.mult, ALU.mult, [yt, gscc, rsc], [yt])

    def rsqrt_inplace(v, R):
        P.act(v, v, AF.Sqrt, R, R)
        P.recip(v, v, R, R)

    xcount = [0]

    def wout_mm(np_, ylhs, yref, xrows, x1dst, x1ref):
        xt = xin[xcount[0] % 2]
        xcount[0] += 1
        P.dma("sp", xt[0:np_, :], xrows, [G.xbuf], [xt])
        for hf in range(2):
            for c in range(8):
                P.mm(pX[0:np_, hf * 512:(hf + 1) * 512], ylhs(c), wout[:, c, hf * 512:(hf + 1) * 512], c == 0, c == 7,
                     [wout, yref], [pX])
        return xt

    def wout_add(xt, np_, ylhs, yref, xrows, x1dst, x1ref):
        P.tt("dve", x1dst, pX[0:np_, :], xt[0:np_, :], ALU.add, [pX, xt], [x1ref])

    def wout_p2(np_, x1dst, x1ref, stt_, h2t, h2T_dst, h2Tref):
        P.act(junk[0:np_, :], x1dst, AF.Square, [x1ref], [junk, stt_], accum_out=stt_[0:np_, 0:1])
        P.ts("dve", stt_[0:np_, 0:1], stt_[0:np_, 0:1], 1.0 / D, EPS, ALU.mult, ALU.add, [stt_], [stt_])
        rsqrt_inplace(stt_[0:np_, 0:1], [stt_])
        P.stt("dve", h2t[0:np_, :], x1dst, stt_[0:np_, 0:1], gF[0:np_, :], ALU.mult, ALU.mult, [x1ref, stt_, gF], [h2t])
        for c in range(8):
            P.tr(pTt[:, c, 0:np_], h2t[0:np_, c * 128:(c + 1) * 128], ident[0:np_, 0:np_], [h2t, ident], [pTt])
        P.copy("act", h2T_dst, pTt[:, :, 0:np_], [pTt], [h2Tref])

    def load_w(f):
        w = wgu[f % 3]
        P.dma("sp", w[:], G.wgu[f], [G.wgu], [w])

    for bi, pos in enumerate(C.own_pos):
        qc = C.qcol(pos)
        yb = yTb[0]
        P.dma("sp", yb[:], G.yT[:, :, qc:qc + 512], [G.yT], [yb])
        diff_out_norm(yb, 512)
        first = pos in C.seg_starts
        stages = []
        if first:
            hc = C.halocol(pos)
            P.dma("sp", yTh[:], G.yT[:, :, hc:hc + 64], [G.yT], [yTh])
            diff_out_norm(yTh, 64)

            def ylhs_h(c):
                return yTh[:, c, :]
            x1h = outt[0]
            stages.append(((64, ylhs_h, yTh, G.xbuf[pos * 512 - 64:pos * 512, :], x1h[0:64, :], x1h),
                           (64, x1h[0:64, :], x1h, stc[2], h2[2], h2Th[:], h2Th)))
        for sub in range(4):
            def ylhs(c, sub=sub):
                return yb[:, c, sub * 128:(sub + 1) * 128]
            r0 = pos * 512 + sub * 128
            stages.append(((128, ylhs, yb, G.xbuf[r0:r0 + 128, :], x1b[:, sub, :], x1b),
                           (128, x1b[:, sub, :], x1b, stc[sub % 2], h2[sub % 2],
                            h2T[:, :, sub * 128:(sub + 1) * 128], h2T)))
        xt0 = wout_mm(*stages[0][0])
        wout_add(xt0, *stages[0][0])
        for k in range(len(stages)):
            if k + 1 < len(stages):
                xt1 = wout_mm(*stages[k + 1][0])
            wout_p2(*stages[k][1])
            if k + 1 < len(stages):
                wout_add(xt1, *stages[k + 1][0])
        load_w(0)
        load_w(1)

        def ffn_front(f):
            if f + 2 < NF:
                load_w(f + 2)
            w = wgu[f % 3]
            if first:
                for c in range(8):
                    P.mm(pH[:, 0:64], w[:, 0, c, :], h2Th[:, c, :], c == 0, c == 7, [w, h2Th], [pH])
                P.copy("act", carry[:, f, :], pH[:, 62:64], [pH], [carry])
            g_ps, u_ps = pG[f % 2], pU[f % 2]
            for c in range(8):
                P.mm(g_ps[:], w[:, 0, c, :], h2T[:, c, :], c == 0, c == 7, [w, h2T], [g_ps])
            for c in range(8):
                P.mm(u_ps[:], w[:, 1, c, :], h2T[:, c, :], c == 0, c == 7, [w, h2T], [u_ps])
            ge = gext[f % 2]
            P.copy("act", ge[:, 2:514], g_ps[:], [g_ps], [ge])
            P.copy("pool", ge[:, 0:2], carry[:, f, :], [carry], [ge])
            P.copy("pool", carry[:, f, :], ge[:, 512:514], [ge], [carry])

        def ffn_back(f):
            u_ps = pU[f % 2]
            ge, ca, s_ = gext[f % 2], cacc[f % 2], sl[f % 2]
            P.ts("dve", ca[:], ge[:, 2:514], cw[:, 2, f:f + 1], cw[:, 3, f:f + 1], ALU.mult, ALU.add, [ge, cw], [ca])
            P.stt("dve", ca[:], ge[:, 1:513], cw[:, 1, f:f + 1], ca[:], ALU.mult, ALU.add, [ge, cw, ca], [ca])
            P.stt("dve", ca[:], ge[:, 0:512], cw[:, 0, f:f + 1], ca[:], ALU.mult, ALU.add, [ge, cw, ca], [ca])
            P.act(s_[:], ca[:], AF.Silu, [ca], [s_])
            P.tt("dve", actT[:, f, :], u_ps[:], s_[:], ALU.mult, [u_ps, s_], [actT])

        ffn_front(0)
        for f in range(NF):
            if f + 1 < NF:
                ffn_front(f + 1)
            ffn_back(f)
        def down_mm(sub):
            for hf in range(2):
                for f in range(NF):
                    P.mm(pX[:, hf * 512:(hf + 1) * 512], actT[:, f, sub * 128:(sub + 1) * 128],
                         wd[:, f, hf * 512:(hf + 1) * 512], f == 0, f == NF - 1, [actT, wd], [pX])

        def down_add(sub):
            x2_ = x1b[:, sub, :]
            P.tt("dve", x2_, pX[:], x2_, ALU.add, [pX, x1b], [x1b])

        def ple(sub):
            x2 = x1b[:, sub, :]
            stt_ = stc[sub % 2]
            P.act(junk[:], x2, AF.Square, [x1b], [junk, stt_], accum_out=stt_[:, 1:2])
            P.ts("dve", stt_[:, 1:2], stt_[:, 1:2], 1.0 / D, EPS, ALU.mult, ALU.add, [stt_], [stt_])
            rsqrt_inplace(stt_[:, 1:2], [stt_])
            h3 = h2[sub % 2]
            P.stt("dve", h3[:], x2, stt_[:, 1:2], gP[:], ALU.mult, ALU.mult, [x1b, stt_, gP], [h3])
            pi = pin[sub % 2]
            orow = qc + sub * 128
            P.dma("sp", pi[:], G.p_own[orow:orow + 128, :], [G.p_own], [pi])
            P.copy("pool", pb[:], pi[:], [pi], [pb])
            for c in range(2):
                P.tr(pH[:].bitcast(BF16)[:, c * 128:(c + 1) * 128], pb[:, c * 128:(c + 1) * 128], ident[:], [pb, ident], [pH])
            P.copy("act", pTT[:], pH[:].bitcast(BF16)[:, 0:256].rearrange("p (c t) -> p c t", c=2), [pH], [pTT])
            for c in range(8):
                P.tr(pTt[:, c, :], h3[:, c * 128:(c + 1) * 128], ident[:], [h3, ident], [pTt])
            P.copy("act", h3T[:], pTt[:], [pTt], [h3T])
            ot = outt[sub % 2]
            gl = (pG[0], pU[0])
            pj = (pG[1], pU[1])
            for hf in range(2):
                for c in range(2):
                    P.mm(pj[hf][:], pTT[:, c, :], wpp[:, c, hf * 512:(hf + 1) * 512], c == 0, c == 1, [pTT, wpp], [pj[hf]])
            for hf in range(2):
                for c in range(8):
                    P.mm(gl[hf][:], h3T[:, c, :], wpg[:, c, hf * 512:(hf + 1) * 512], c == 0, c == 7, [h3T, wpg], [gl[hf]])
                P.act(sig[:, hf * 512:(hf + 1) * 512], gl[hf][:], AF.Sigmoid, [gl[hf]], [sig])
                P.tt("dve", ot[:, hf * 512:(hf + 1) * 512], pj[hf][:], sig[:, hf * 512:(hf + 1) * 512], ALU.mult,
                     [pj[hf], sig], [ot])
            P.tt("pool", ot[:], ot[:], x2, ALU.add, [ot, x1b], [ot])
            P.dma("pool", G.out[orow:orow + 128, :], ot[:], [ot], [G.out], chan=("out", sub % 2))

        down_mm(0)
        down_add(0)
        for sub in range(4):
            if sub + 1 < 4:
                down_mm(sub + 1)
            ple(sub)
            if sub + 1 < 4:
                down_add(sub + 1)
    P.end_scope_deferred = old


def build_program(C, debug=(), phases="ABC"):
    nc = bass.Bass("TRN2", target_bir_lowering=False)
    P = Prog(nc)
    G = declare_dram(P, C, debug)
    if "A" in phases:
        phase_A(P, C, G)
    if "B" in phases:
        phase_B(P, C, G)
    if "C" in phases:
        phase_C(P, C, G)
    P.flush(final=True)
    if getattr(P, "end_scope_deferred", None) is not None:
        P.end_scope(P.end_scope_deferred)
    return nc, P


SEQ_FULL = 8192
SEG_L = 2
N_CORES = 8
_CACHE = {}


def kernel(**inputs):
    C = Cfg(SEQ_FULL, SEG_L)
    if "prog" not in _CACHE:
        _CACHE["prog"] = build_program(C)
    nc, _ = _CACHE["prog"]
    in_maps, metas = [], []
    for core in range(N_CORES):
        m, own_tok, real = host_inputs(inputs, core, C)
        in_maps.append(m)
        metas.append((own_tok, real))
    res = run_bass_kernel_spmd(nc, in_maps, core_ids=list(range(N_CORES)))
    out = np.zeros((4, SEQ_FULL, D), np.float32)
    for core in range(N_CORES):
        own_tok, real = metas[core]
        out[core // 2, real[own_tok]] = np.asarray(res.results[core]["out"], dtype=np.float32)
    return out
```

```python
import contextlib
import numpy as np
import concourse.bass as bass
import concourse.mybir as mybir
from concourse.bass_utils import run_bass_kernel_spmd

F32 = mybir.dt.float32
BF16 = mybir.dt.bfloat16
ALU = mybir.AluOpType
AF = mybir.ActivationFunctionType
AX = mybir.AxisListType

import os
SES_ENGS = set(os.environ.get("K_SES", "act,dve").split(","))
MAXOPS = int(os.environ.get("K_MAXOPS", "100000000"))
VERBOSE_OPS = bool(int(os.environ.get("K_VERBOSE", "0")))


class TRef:
    def __init__(self, obj, name, handle=None, psum=False):
        self.o = obj
        self.name = name
        self.t = handle if handle is not None else obj
        self.psum = psum

    def __getitem__(self, k):
        return self.o[k]

    def rearrange(self, s, **kw):
        return self.o[:].rearrange(s, **kw) if not hasattr(self.o, "rearrange") else self.o.rearrange(s, **kw)


class _Op:
    __slots__ = ("eng", "fn", "reads", "writes", "is_dma", "chan", "deps", "needed", "ms", "idx")


class Prog:
    ENGS = ("pe", "act", "dve", "pool", "sp")

    def __init__(self, nc):
        self.nc = nc
        self.stack = contextlib.ExitStack()
        self.ops = []
        self.nchan = 0
        self.pending_barrier = {}

    def dram(self, name, shape, dtype, kind="Internal"):
        h = self.nc.dram_tensor(name, list(shape), dtype, kind=kind)
        return TRef(h.ap(), name, h)

    def sbuf(self, name, shape, dtype):
        t = self.stack.enter_context(self.nc.sbuf_tensor(name, list(shape), dtype))
        return TRef(t, name)

    def psum(self, name, shape, dtype):
        t = self.stack.enter_context(self.nc.psum_tensor(name, list(shape), dtype))
        return TRef(t, name, psum=True)

    def op(self, eng, fn, reads=(), writes=()):
        o = _Op()
        reads, writes = tuple(reads), tuple(writes)
        writes = writes + tuple(r for r in reads if getattr(r, "psum", False) and r not in writes)
        o.eng, o.fn, o.reads, o.writes = eng, fn, reads, writes
        o.is_dma, o.chan = False, None
        o.ms = None
        o.idx = None
        o.deps = set()
        cap = getattr(self, "_cap", None)
        if cap is not None:
            cap.append(o)
        else:
            self._commit(o)
        return o

    def _commit(self, o):
        if len(self.ops) >= MAXOPS:
            return
        o.idx = len(self.ops)
        o.deps = set(self.pending_barrier.pop(o.eng, ()))
        self.ops.append(o)

    def capture(self):
        self._cap = []

    def end_capture(self):
        lst, self._cap = self._cap, None
        return lst

    def emit_zip(self, lists):
        lists = [l for l in lists if l]
        pos = [0] * len(lists)
        total = sum(len(l) for l in lists)
        for _ in range(total):
            k = min((i for i in range(len(lists)) if pos[i] < len(lists[i])),
                    key=lambda i: pos[i] / len(lists[i]))
            self._commit(lists[k][pos[k]])
            pos[k] += 1

    def barrier(self):
        last = {}
        for o in self.ops:
            if o.is_dma:
                last[("c", o.chan)] = o.idx
            else:
                last[("e", o.eng)] = o.idx
        allidx = set(last.values())
        for e in self.ENGS:
            self.pending_barrier[e] = set(allidx) | set(self.pending_barrier.get(e, ()))

    def phase_scope(self):
        old = self.stack
        self.stack = contextlib.ExitStack()
        return old

    def end_scope(self, old):
        self.stack.close()
        self.stack = old

    def dma(self, eng, out_ap, in_ap, reads=(), writes=(), chan=None, **kw):
        o = self.op(eng, lambda e: e.dma_start(out=out_ap, in_=in_ap, **kw), reads, writes)
        o.is_dma = True
        o.chan = chan if chan is not None else ("auto", writes[0].name)
        return o

    def act(self, out, in_, func, R, W, **kw):
        return self.op("act", lambda e: e.activation(out, in_, func, **kw), R, W)

    def tt(self, eng, out, a, b, op, R, W):
        return self.op(eng, lambda e: e.tensor_tensor(out, a, b, op), R, W)

    def ts(self, eng, out, a, s1, s2, op0, op1, R, W):
        if s2 is None:
            return self.op(eng, lambda e: e.tensor_scalar(out, a, s1, None, op0), R, W)
        return self.op(eng, lambda e: e.tensor_scalar(out, a, s1, s2, op0, op1), R, W)

    def stt(self, eng, out, in0, scalar, in1, op0, op1, R, W):
        return self.op(eng, lambda e: e.scalar_tensor_tensor(out, in0, scalar, in1, op0, op1), R, W)

    def copy(self, eng, out, in_, R, W):
        if eng == "act":
            return self.op("act", lambda e: e.activation(out, in_, AF.Copy), R, W)
        return self.op(eng, lambda e: e.tensor_copy(out, in_), R, W)

    def red(self, out, in_, R, W):
        return self.op("dve", lambda e: e.tensor_reduce(out, in_, AX.X, ALU.add), R, W)

    def recip(self, out, in_, R, W):
        return self.op("dve", lambda e: e.reciprocal(out, in_), R, W)

    def mm(self, out, lhsT, rhs, start, stop, R, W):
        return self.op("pe", lambda e: e.matmul(out, lhsT, rhs, start=start, stop=stop), R, W)

    def tr(self, out, in_, ident, R, W):
        return self.op("pe", lambda e: e.transpose(out, in_, ident), R, W)

    def memset(self, eng, out, val, R, W):
        return self.op(eng, lambda e: e.memset(out, val), R, W)

    def _init_lower(self):
        nc = self.nc
        self.esem = {e: nc.alloc_semaphore(name="ms_" + e) for e in self.ENGS}
        self.csem = {}
        self.ecount = {e: 0 for e in self.ENGS}
        self.ccount = {}
        self.known = {e: {} for e in self.ENGS}
        self.last_w = {}
        self.readers = {}
        self.chan_last = {}
        self.flushed = 0
        self.n_instr = {e: 0 for e in self.ENGS}
        self._lower_ready = True

    def flush(self, final=False):
        if not getattr(self, "_lower_ready", False):
            self._init_lower()
        nc = self.nc
        ops = self.ops
        new = ops[self.flushed:]
        phase_start = self.flushed
        last_w, readers, chan_last = self.last_w, self.readers, self.chan_last
        for o in new:
            deps = set(o.deps)
            for r in o.reads:
                k = r.name
                if k in last_w:
                    deps.add(last_w[k])
            for w in o.writes:
                k = w.name
                if k in last_w:
                    deps.add(last_w[k])
                for rd in readers.get(k, ()):
                    deps.add(rd)
            if o.is_dma and o.chan in chan_last:
                deps.add(chan_last[o.chan])
            deps.discard(o.idx)
            o.deps = deps
            for w in o.writes:
                last_w[w.name] = o.idx
                readers[w.name] = []
            for r in o.reads:
                readers.setdefault(r.name, []).append(o.idx)
            if o.is_dma:
                chan_last[o.chan] = o.idx
        for o in new:
            o.needed = False
        for o in new:
            for d in o.deps:
                p = ops[d]
                if p.is_dma or p.idx < self.flushed:
                    continue
                if p.eng != o.eng or (o.eng in SES_ENGS):
                    p.needed = True
        lastc = {}
        for o in new:
            if not o.is_dma:
                lastc[o.eng] = o
        for o in lastc.values():
            o.needed = True
        for o in new:
            if o.is_dma and o.chan not in self.csem:
                self.csem[o.chan] = nc.alloc_semaphore(name="ch%d" % len(self.csem))
                self.ccount[o.chan] = 0
        esem, csem, ecount, ccount = self.esem, self.csem, self.ecount, self.ccount
        streams = {e: [] for e in self.ENGS}
        for o in new:
            need = {}
            for d in o.deps:
                p = ops[d]
                if p.is_dma:
                    key = ("c", p.chan)
                else:
                    if p.eng == o.eng and (o.eng not in SES_ENGS):
                        continue
                    key = ("e", p.eng)
                    if p.ms is None:
                        assert p.idx < phase_start, (p.idx, p.eng, o.idx, o.eng)
                        continue
                val = p.ms
                if need.get(key, 0) < val:
                    need[key] = val
            kn = self.known[o.eng]
            waits = []
            for key, val in need.items():
                if kn.get(key, 0) >= val:
                    continue
                kn[key] = val
                sem = csem[key[1]] if key[0] == "c" else esem[key[1]]
                waits.append((sem, val))
            if o.is_dma:
                ccount[o.chan] += 16
                o.ms = ccount[o.chan]
                inc = (csem[o.chan], 16)
            elif o.needed:
                ecount[o.eng] += 1
                o.ms = ecount[o.eng]
                inc = (esem[o.eng], 1)
            else:
                o.ms = None
                inc = None
            streams[o.eng].append((waits, o.fn, inc))
            o.fn = None
        self.flushed = len(ops)
        for e in self.ENGS:
            self.n_instr[e] += len(streams[e])
        fin = None
        if final:
            fin = [(csem[c], ccount[c]) for c in csem if ccount[c] > 0]
        else:
            self.barrier()

        def run(eng_handle, items, final=None):
            for waits, fn, inc in items:
                for sem, val in waits:
                    eng_handle.wait_ge(sem, val)
                ins = fn(eng_handle)
                if inc is not None:
                    ins.then_inc(inc[0], inc[1])
            if final:
                for sem, val in final:
                    eng_handle.wait_ge(sem, val)

        with nc.Block() as block:
            @block.tensor
            def _(e):
                run(e, streams["pe"])

            @block.scalar
            def _(e):
                run(e, streams["act"])

            @block.vector
            def _(e):
                run(e, streams["dve"])

            @block.gpsimd
            def _(e):
                run(e, streams["pool"])

            @block.sync
            def _(e):
                run(e, streams["sp"], fin)

    def emit(self):
        self.flush(final=True)
        self.stack.close()


D = 1024
NH, QK, NOPE, ROPE, VD = 8, 96, 64, 32, 64
QR, KVR = 256, 128
DH, DQK, DV = 4, 64, 128
DFF, NF = 2816, 22
PLE = 256
EPS = 1e-6
IN_COLS = 1952
OQ, OKV, OKR, ODQ, ODK, ODV = 0, 256, 384, 416, 928, 1440
SM_MLA = float(96 ** -0.5)
SM_DIFF = float(64 ** -0.5)
LAM_INIT = 0.8 - 0.6 * 1.0
NUM_BUCKETS = 32
RMAX, RMIN = 127, -1151
LV = RMAX - RMIN + 1
NEAR = 5


class Cfg:
    def __init__(self, S=8192, L=8):
        self.S, self.L = S, L
        self.NBLK = S // 512
        self.NT = S // 128
        self.own_pos = [p for p in range(self.NBLK) if (p // L) % 2 == 1]
        self.seg_starts = [p for p in self.own_pos if p % L == 0]
        self.NOWN = len(self.own_pos) * 512
        self.NQ = self.NOWN + 64 * len(self.seg_starts)

    def qcol(self, pos):
        return self.own_pos.index(pos) * 512

    def halocol(self, seg_pos):
        return self.NOWN + 64 * self.seg_starts.index(seg_pos)


def t5_bucket_np(rel):
    nb = NUM_BUCKETS // 2
    max_exact = nb // 2
    rel = np.asarray(rel, dtype=np.int64)
    sign_off = (rel > 0).astype(np.int64) * nb
    n = np.abs(rel)
    nf = np.maximum(n, 1).astype(np.float32)
    large = max_exact + (np.log(nf / np.float32(max_exact)) / np.float32(np.log(1024 / max_exact))
                         * np.float32(nb - max_exact)).astype(np.int32)
    large = np.minimum(large, nb - 1)
    return sign_off + np.where(n < max_exact, n, large)


def host_inputs(inputs, core, C):
    b, role = core // 2, core % 2
    S, L = C.S, C.L
    x = np.asarray(inputs["x"], dtype=np.float32)[b, :S]
    p = np.asarray(inputs["p"], dtype=np.float32)[0, b, :S]
    shift = 512 * L if role == 0 else 0
    real = np.arange(S) - shift
    ok = real >= 0
    xbuf = np.zeros((S, D), np.float32)
    xbuf[ok] = x[real[ok]]
    own_tok = np.concatenate([np.arange(pp * 512, (pp + 1) * 512) for pp in C.own_pos])
    p_own = np.ascontiguousarray(p[real[own_tok]])
    valid = ok.astype(np.float32).reshape(C.NT, 128).T.copy()
    pos = np.where(ok, real, 0).astype(np.float32)
    inv_freq = (10000.0 ** (-np.arange(16, dtype=np.float32) / np.float32(16))).astype(np.float32)
    ang = pos[:, None] * inv_freq[None, :]
    cos, sin = np.cos(ang).astype(np.float32), np.sin(ang).astype(np.float32)
    cc = np.concatenate([cos, cos], 1).reshape(C.NT, 128, 32).transpose(1, 0, 2).copy()
    sn = np.concatenate([-sin, sin], 1).reshape(C.NT, 128, 32).transpose(1, 0, 2).copy()
    rel = RMAX - np.arange(LV)
    oh = (t5_bucket_np(rel)[None, :] == np.arange(NUM_BUCKETS)[:, None]).astype(np.float32)
    m = {"xbuf": xbuf, "p_own": p_own, "valid": valid, "rope_cc": cc, "rope_sn": sn, "t5_onehot": oh}
    for k, v in inputs.items():
        if k in ("x", "p"):
            continue
        m[k] = np.ascontiguousarray(np.asarray(v, dtype=np.float32))
    return m, own_tok, real


class NS:
    pass


def bc_ap(t, n, off=0):
    return bass.AP(t.t, off, [[0, 128], [1, n]])


def declare_dram(P, C, debug=()):
    G = NS()
    S, NT, NQ = C.S, C.NT, C.NQ
    ei = lambda n, shp: P.dram(n, shp, F32, kind="ExternalInput")
    G.xbuf = ei("xbuf", [S, D])
    G.p_own = ei("p_own", [C.NOWN, PLE])
    G.valid = ei("valid", [128, NT])
    G.rope_cc = ei("rope_cc", [128, NT, 32])
    G.rope_sn = ei("rope_sn", [128, NT, 32])
    G.t5_onehot = ei("t5_onehot", [NUM_BUCKETS, LV])
    shapes = {
        "attn_norm_g": [1, D], "w_in": [1, D, IN_COLS], "q_lat_norm_g": [1, QR], "w_uq": [1, QR, NH * QK],
        "kv_lat_norm_g": [1, KVR], "w_ukv": [1, KVR, NH * 128], "mla_q_norm_g": [1, QK], "mla_k_norm_g": [1, QK],
        "diff_q_norm_g": [1, DQK], "diff_k_norm_g": [1, DQK], "lambda_q1": [1, DQK], "lambda_k1": [1, DQK],
        "lambda_q2": [1, DQK], "lambda_k2": [1, DQK], "diff_out_norm_g": [1, DV], "rel_bias": [NUM_BUCKETS, DH],
        "w_out": [1, D, D], "ffn_norm_g": [1, D], "w_gate": [1, D, DFF], "w_up": [1, D, DFF],
        "conv_w": [1, 3, DFF], "conv_b": [1, DFF], "w_down": [1, DFF, D], "ple_norm_g": [1, D],
        "w_ple_gate": [1, D, D], "w_ple_proj": [1, PLE, D],
    }
    for k, shp in shapes.items():
        setattr(G, k, ei(k, shp))
    G.out = P.dram("out", [C.NOWN, D], F32, kind="ExternalOutput")

    def scr(name, shp, dt=BF16):
        kind = "ExternalOutput" if name in debug else "Internal"
        t = P.dram(name, shp, dt, kind=kind)
        setattr(G, name, t)
        return t
    scr("KTm", [NH, QK, S])
    scr("Vm", [NH, 128, NT, 128])
    scr("QTm", [NH, QK, NQ])
    scr("KTd", [DH, 128, S])
    scr("Vd", [DH, 128, NT, 128])
    scr("QTd", [DH, 128, NQ])
    scr("yT", [128, 8, NQ])
    scr("wgu", [NF, 128, 2, 8, 128])
    scr("wdn", [128, NF, D])
    scr("Rs", [DH, 128, LV], F32)
    return G


def phase_A(P, C, G):
    old = P.phase_scope()
    sb, ps = P.sbuf, P.psum
    NT = C.NT
    win = sb("win", [128, 8, IN_COLS], BF16)
    wuq = sb("wuq", [128, 2, NH * QK], BF16)
    wukv = sb("wukv", [128, NH * 128], BF16)
    P.dma("pool", win[:], G.w_in[0].rearrange("(c p) n -> p c n", p=128), [G.w_in], [win])
    P.dma("pool", wuq[:], G.w_uq[0].rearrange("(c p) n -> p c n", p=128), [G.w_uq], [wuq])
    P.dma("pool", wukv[:], G.w_ukv[0], [G.w_ukv], [wukv])
    def bload(name, src, n):
        t = sb(name, [128, n], F32)
        P.dma("sp", t[:], bc_ap(src, n), [src], [t])
        return t
    gA = bload("gA", G.attn_norm_g, D)
    gql = bload("gql", G.q_lat_norm_g, QR)
    gkvl = bload("gkvl", G.kv_lat_norm_g, KVR)
    qg = bload("qg", G.mla_q_norm_g, QK)
    kg = bload("kg", G.mla_k_norm_g, QK)
    dqg = bload("dqg", G.diff_q_norm_g, DQK)
    dkg = bload("dkg", G.diff_k_norm_g, DQK)
    ccs = [sb("cc%d" % k, [128, 4, 32], F32) for k in range(2)]
    sns = [sb("sn%d" % k, [128, 4, 32], F32) for k in range(2)]
    valid = sb("validA", [128, NT], F32)
    P.dma("sp", valid[:], G.valid[:, :], [G.valid], [valid])
    ident = sb("identA", [128, 128], BF16)
    P.memset("pool", ident[:], 1.0, [], [ident])
    P.op("pool", lambda e: e.affine_select(ident[:], ident[:], [[-1, 128]], ALU.is_equal, 0.0,
                                           base=0, channel_multiplier=1), [ident], [ident])
    pT = ps("pT", [128, 8, 128], BF16)
    pTx = ps("pTx", [128, 8, 128], BF16)
    pZ0 = ps("pZ0", [128, 512], F32)
    pZ1 = ps("pZ1", [128, 512], F32)
    pZ2 = ps("pZ2", [128, 512], F32)
    pZ3 = ps("pZ3", [128, 512], F32)
    pK = ps("pK", [128, 1024], F32)
    NXS = 2
    xts = [sb("xt%d" % i, [128, D], F32) for i in range(NXS)]
    NSL = 2

    def slot_tiles(i):
        t = NS()
        t.junk = sb("junk%d" % i, [128, D], BF16)
        t.xg = sb("xg%d" % i, [128, D], BF16)
        t.xT = sb("xT%d" % i, [128, 8, 128], BF16)
        t.zsb = sb("zsb%d" % i, [128, IN_COLS], F32)
        t.st = sb("st%d" % i, [128, 16], F32)
        t.stk = sb("stk%d" % i, [128, 16], F32)
        t.stq = sb("stq%d" % i, [128, 16], F32)
        t.shk = sb("shk%d" % i, [128, 8, 8], F32)
        t.shq = sb("shq%d" % i, [128, 8, 8], F32)
        t.shd = sb("shd%d" % i, [128, 8, 2], F32)
        t.she = sb("she%d" % i, [128, 8, 2], F32)
        t.sqd = sb("sqd%d" % i, [128, 512], F32)
        t.tmpd = sb("tmpd%d" % i, [128, 512], F32)
        t.sqe = sb("sqe%d" % i, [128, 512], F32)
        t.tmpe = sb("tmpe%d" % i, [128, 512], F32)
        t.dfin2 = sb("dfin2%d" % i, [128, 512], BF16)
        t.jf = sb("jf%d" % i, [128, 256], F32)
        t.ukv = sb("ukv%d" % i, [128, 128], BF16)
        t.ukvT = sb("ukvT%d" % i, [128, 128], BF16)
        t.sq = sb("sq%d" % i, [128, 768], F32)
        t.tmp = sb("tmp%d" % i, [128, 768], F32)
        t.tmp2 = sb("tmp2%d" % i, [128, 768], F32)
        t.kfin = sb("kfin%d" % i, [128, 8, QK], BF16)
        t.krg = sb("krg%d" % i, [128, 32], F32)
        t.r1 = sb("r1%d" % i, [128, 8, 32], F32)
        t.r2 = sb("r2%d" % i, [128, 8, 32], F32)
        t.dfin = sb("dfin%d" % i, [128, 512], BF16)
        t.uq = sb("uq%d" % i, [128, 256], BF16)
        t.uqT = sb("uqT%d" % i, [128, 2, 128], BF16)
        return t
    slots = [slot_tiles(i) for i in range(NSL)]

    def blk_tiles(i):
        t = NS()
        t.KT = sb("KTblk%d" % i, [QK, NH, 512], BF16)
        t.V = sb("Vblk%d" % i, [128, NH, 4, 128], BF16)
        t.dKT = sb("dKTblk%d" % i, [128, DH, 512], BF16)
        t.dV = sb("dVblk%d" % i, [128, DH, 4, 128], BF16)
        t.QT = sb("QTblk%d" % i, [QK, NH, 512], BF16)
        t.dQT = sb("dQTblk%d" % i, [128, DH, 512], BF16)
        return t
    blks = [blk_tiles(i) for i in range(2)]

    def rsqrt_inplace(v, n, R):
        P.act(v, v, AF.Sqrt, R, R)
        P.recip(v, v, R, R)

    ntiles = C.NBLK * 4
    ENG_EW = "dve"

    def load_x(t):
        xt = xts[t % NXS]
        P.dma("sp", xt[:], G.xbuf[t * 128:(t + 1) * 128, :], [G.xbuf], [xt])
    load_x(0)
    if ntiles > 1:
        load_x(1)

    ENG_EW = "dve"

    def tile_info(t):
        pos, sub = t // 4, t % 4
        own = pos in C.own_pos
        halo_next = (pos + 1) in C.seg_starts
        need_q = own or (halo_next and sub == 3)
        return pos, sub, own, halo_next, need_q

    def front(t):
        xt = xts[t % NXS]
        T = slots[t % NSL]
        st = T.st
        P.act(T.junk[:], xt[:], AF.Square, [xt], [T.junk, st], accum_out=st[:, 0:1])
        P.ts("dve", st[:, 0:1], st[:, 0:1], 1.0 / D, EPS, ALU.mult, ALU.add, [st], [st])
        rsqrt_inplace(st[:, 0:1], 1, [st])
        P.tt("dve", st[:, 1:2], st[:, 0:1], st[:, 0:1], ALU.mult, [st], [st])
        P.tt("dve", T.xg[:], xt[:], gA[:], ALU.mult, [xt, gA], [T.xg])
        for c in range(8):
            P.tr(pTx[:, c, :], T.xg[:, c * 128:(c + 1) * 128], ident[:], [T.xg, ident], [pTx])
        P.copy("act", T.xT[:], pTx[:], [pTx], [T.xT])

    def zgroups(need_q):
        if need_q:
            g = [(pZ0, 0, 0, OKR + ROPE), (pZ1, 0, ODQ, ODK)]
        else:
            g = [(pZ0, OKV, OKV, OKR + ROPE)]
        return g + [(pZ2, 0, ODK, ODV), (pZ3, 0, ODV, IN_COLS)]

    def zmm(t):
        T = slots[t % NSL]
        need_q = tile_info(t)[4]
        for (pz, o0, c0, c1) in zgroups(need_q):
            for c in range(8):
                P.mm(pz[:, o0:o0 + (c1 - c0)], T.xT[:, c, :], win[:, c, c0:c1], c == 0, c == 7, [T.xT, win], [pz])

    def evac(t):
        T = slots[t % NSL]
        need_q = tile_info(t)[4]
        for gi, (pz, o0, c0, c1) in enumerate(zgroups(need_q)):
            P.copy("act" if gi % 2 == 0 else "dve", T.zsb[:, c0:c1], pz[:, o0:o0 + (c1 - c0)], [pz], [T.zsb])

    def kv_head(t):
        T = slots[t % NSL]
        st, stk, zsb = T.st, T.stk, T.zsb
        zkv = zsb[:, OKV:OKV + KVR]
        P.act(T.jf[:, 0:KVR], zkv, AF.Square, [zsb], [T.jf, stk], accum_out=stk[:, 2:3])
        P.tt("dve", T.ukv[:], zkv, gkvl[:], ALU.mult, [zsb, gkvl], [T.ukv])
        P.tr(pT[:, 0, :], T.ukv[:], ident[:], [T.ukv, ident], [pT])
        P.copy("act", T.ukvT[:], pT[:, 0, :], [pT], [T.ukvT])
        for hf in range(2):
            P.mm(pK[:, hf * 512:(hf + 1) * 512], T.ukvT[:], wukv[:, hf * 512:(hf + 1) * 512], True, True,
                 [T.ukvT, wukv], [pK])

    def k_pre(t):
        pos, sub, own, halo_next, need_q = tile_info(t)
        B = blks[pos % 2]
        T = slots[t % NSL]
        st, stk, sh, zsb = T.st, T.stk, T.shk, T.zsb
        P.tt("dve", stk[:, 3:4], stk[:, 2:3], st[:, 1:2], ALU.mult, [stk, st], [stk])
        P.ts("dve", stk[:, 3:4], stk[:, 3:4], 1.0 / KVR, EPS, ALU.mult, ALU.add, [stk], [stk])
        rsqrt_inplace(stk[:, 3:4], 1, [stk])
        P.tt("dve", stk[:, 3:4], stk[:, 3:4], st[:, 0:1], ALU.mult, [stk, st], [stk])
        P.tt("dve", stk[:, 4:5], stk[:, 3:4], stk[:, 3:4], ALU.mult, [stk], [stk])
        zkr = zsb[:, OKR:OKR + ROPE]
        P.act(T.jf[:, 128:160], zkr, AF.Square, [zsb], [T.jf, stk], accum_out=stk[:, 5:6])
        P.tt("dve", stk[:, 5:6], stk[:, 5:6], st[:, 1:2], ALU.mult, [stk, st], [stk])
        pK3 = pK[:].rearrange("p (h c) -> p h c", h=NH)
        sqk3 = T.sq[:, 0:512].rearrange("p (h d) -> p h d", h=NH)
        P.act(sqk3, pK3[:, :, 0:NOPE], AF.Square, [pK], [T.sq])
        P.red(sh[:, :, 0], sqk3, [T.sq], [sh])
        P.ts("dve", sh[:, :, 0], sh[:, :, 0], stk[:, 4:5], stk[:, 5:6], ALU.mult, ALU.add, [sh, stk], [sh])
        P.ts("dve", sh[:, :, 0], sh[:, :, 0], 1.0 / QK, EPS, ALU.mult, ALU.add, [sh], [sh])
        rsqrt_inplace(sh[:, :, 0], 8, [sh])
        P.ts("dve", sh[:, :, 1], sh[:, :, 0], stk[:, 3:4], None, ALU.mult, None, [sh, stk], [sh])
        P.ts("dve", sh[:, :, 2], sh[:, :, 0], st[:, 0:1], None, ALU.mult, None, [sh, st], [sh])
        tk3 = T.tmp[:, 0:512].rearrange("p (h d) -> p h d", h=NH)
        P.tt("dve", tk3, pK3[:, :, 0:NOPE], sh[:, :, 1:2].to_broadcast([128, NH, NOPE]), ALU.mult, [pK, sh], [T.tmp])
        P.act(B.V[:, :, sub, 0:VD], pK3[:, :, NOPE:128], AF.Copy, [pK, stk], [B.V], scale=stk[:, 3:4])
        P.copy("pool", B.V[:, :, sub, VD:128], valid[:, t:t + 1].unsqueeze(2).to_broadcast([128, NH, 64]),
               [valid], [B.V])
        P.tt(ENG_EW, T.kfin[:, :, 0:NOPE], tk3, kg[:, 0:NOPE].unsqueeze(1).to_broadcast([128, NH, NOPE]), ALU.mult,
             [T.tmp, kg], [T.kfin])
        P.tt("dve", T.krg[:], zkr, kg[:, NOPE:QK], ALU.mult, [zsb, kg], [T.krg])
        cc, sn = ccs[pos % 2], sns[pos % 2]
        P.tt("pool", T.r1[:, 0, :], T.krg[:], cc[:, sub, :], ALU.mult, [T.krg, cc], [T.r1])
        P.tt("pool", T.r2[:, 0, 0:16], T.krg[:, 16:32], sn[:, sub, 0:16], ALU.mult, [T.krg, sn], [T.r2])
        P.tt("pool", T.r2[:, 0, 16:32], T.krg[:, 0:16], sn[:, sub, 16:32], ALU.mult, [T.krg, sn], [T.r2])
        P.tt("pool", T.r1[:, 0, :], T.r1[:, 0, :], T.r2[:, 0, :], ALU.add, [T.r1, T.r2], [T.r1])
        P.tt("dve", T.kfin[:, :, NOPE:QK], T.r1[:, 0:1, :].to_broadcast([128, NH, ROPE]),
             sh[:, :, 2:3].to_broadcast([128, NH, ROPE]), ALU.mult, [T.r1, sh], [T.kfin])

    def k_tail(t):
        pos, sub = tile_info(t)[0:2]
        B = blks[pos % 2]
        T = slots[t % NSL]
        for h in range(NH):
            P.tr(pT[0:QK, h, :], T.kfin[:, h, :], ident[:], [T.kfin, ident], [pT])
        P.copy("act", B.KT[:, :, sub * 128:(sub + 1) * 128], pT[0:QK, :, :], [pT], [B.KT])

    def dqk_pre(t, which):
        T = slots[t % NSL]
        st, zsb = T.st, T.zsb
        sqx, tmpx, shx, dfx = (T.sqd, T.tmpd, T.shd, T.dfin) if which == 0 else (T.sqe, T.tmpe, T.she, T.dfin2)
        zap = zsb[:, ODK:ODV] if which == 0 else zsb[:, ODQ:ODK]
        gain = dkg if which == 0 else dqg
        sq3 = sqx[:].rearrange("p (m d) -> p m d", m=8)
        P.act(sqx[:], zap, AF.Square, [zsb], [sqx])
        P.red(shx[:, :, 0], sq3, [sqx], [shx])
        P.ts("dve", shx[:, :, 0], shx[:, :, 0], st[:, 1:2], None, ALU.mult, None, [shx, st], [shx])
        P.ts("dve", shx[:, :, 0], shx[:, :, 0], 1.0 / DQK, EPS, ALU.mult, ALU.add, [shx], [shx])
        rsqrt_inplace(shx[:, :, 0], 8, [shx])
        P.ts("dve", shx[:, :, 0], shx[:, :, 0], st[:, 0:1], None, ALU.mult, None, [shx, st], [shx])
        tmp3 = tmpx[:].rearrange("p (m d) -> p m d", m=8)
        P.tt("dve", tmp3, zap.rearrange("p (m d) -> p m d", m=8),
             shx[:, :, 0:1].to_broadcast([128, 8, DQK]), ALU.mult, [zsb, shx], [tmpx])
        P.tt(ENG_EW, dfx[:].rearrange("p (m d) -> p m d", m=8), tmp3,
             gain[:].unsqueeze(1).to_broadcast([128, 8, DQK]), ALU.mult, [tmpx, gain], [dfx])

    def dqk_tail(t, which):
        pos, sub, own = tile_info(t)[0:3]
        B = blks[pos % 2]
        T = slots[t % NSL]
        dfx = T.dfin if which == 0 else T.dfin2
        for h in range(DH):
            P.tr(pT[:, h, :], dfx[:, h * 128:(h + 1) * 128], ident[:], [dfx, ident], [pT])
        if which == 0:
            P.copy("act", B.dKT[:, :, sub * 128:(sub + 1) * 128], pT[:, 0:DH, :], [pT], [B.dKT])
        elif own:
            P.copy("act", B.dQT[:, :, sub * 128:(sub + 1) * 128], pT[:, 0:DH, :], [pT], [B.dQT])
        else:
            P.copy("act", B.dQT[:, :, 0:64], pT[:, 0:DH, 64:128], [pT], [B.dQT])

    def dv_copy(t):
        pos, sub = tile_info(t)[0:2]
        B = blks[pos % 2]
        T = slots[t % NSL]
        P.act(B.dV[:, :, sub, :], T.zsb[:, ODV:IN_COLS].rearrange("p (h c) -> p h c", h=DH), AF.Copy, [T.zsb, T.st], [B.dV],
              scale=T.st[:, 0:1])

    def q_head(t):
        T = slots[t % NSL]
        st, stq, zsb = T.st, T.stq, T.zsb
        zq = zsb[:, OQ:OQ + QR]
        P.act(T.jf[:, 0:QR], zq, AF.Square, [zsb], [T.jf, stq], accum_out=stq[:, 6:7])
        P.tt("dve", T.uq[:], zq, gql[:], ALU.mult, [zsb, gql], [T.uq])
        for c in range(2):
            P.tr(pT[:, 1 + c, :], T.uq[:, c * 128:(c + 1) * 128], ident[:], [T.uq, ident], [pT])
        P.copy("act", T.uqT[:], pT[:, 1:3, :], [pT], [T.uqT])
        for (c0, c1) in ((0, 512), (512, 768)):
            for c in range(2):
                P.mm(pK[:, c0:c1], T.uqT[:, c, :], wuq[:, c, c0:c1], c == 0, c == 1, [T.uqT, wuq], [pK])

    def q_pre(t):
        pos, sub = tile_info(t)[0:2]
        cc, sn = ccs[pos % 2], sns[pos % 2]
        T = slots[t % NSL]
        st, stq, sh = T.st, T.stq, T.shq
        P.tt("dve", stq[:, 7:8], stq[:, 6:7], st[:, 1:2], ALU.mult, [stq, st], [stq])
        P.ts("dve", stq[:, 7:8], stq[:, 7:8], 1.0 / QR, EPS, ALU.mult, ALU.add, [stq], [stq])
        rsqrt_inplace(stq[:, 7:8], 1, [stq])
        P.tt("dve", stq[:, 7:8], stq[:, 7:8], st[:, 0:1], ALU.mult, [stq, st], [stq])
        P.tt("dve", stq[:, 8:9], stq[:, 7:8], stq[:, 7:8], ALU.mult, [stq], [stq])
        pQ3 = pK[:, 0:NH * QK].rearrange("p (h d) -> p h d", h=NH)
        sqq3 = T.sq[:, 0:NH * QK].rearrange("p (h d) -> p h d", h=NH)
        P.act(T.sq[:, 0:NH * QK], pK[:, 0:NH * QK], AF.Square, [pK], [T.sq])
        P.red(sh[:, :, 4], sqq3, [T.sq], [sh])
        P.ts("dve", sh[:, :, 4], sh[:, :, 4], stq[:, 8:9], None, ALU.mult, None, [sh, stq], [sh])
        P.ts("dve", sh[:, :, 4], sh[:, :, 4], 1.0 / QK, EPS, ALU.mult, ALU.add, [sh], [sh])
        rsqrt_inplace(sh[:, :, 4], 8, [sh])
        P.ts("dve", sh[:, :, 4], sh[:, :, 4], stq[:, 7:8], None, ALU.mult, None, [sh, stq], [sh])
        tq3 = T.tmp[:, 0:NH * QK].rearrange("p (h d) -> p h d", h=NH)
        tg3 = T.tmp2[:, 0:NH * QK].rearrange("p (h d) -> p h d", h=NH)
        P.tt("dve", tq3, pQ3, sh[:, :, 4:5].to_broadcast([128, NH, QK]), ALU.mult, [pK, sh], [T.tmp])
        P.tt(ENG_EW, tg3, tq3, qg[:].unsqueeze(1).to_broadcast([128, NH, QK]), ALU.mult, [T.tmp, qg], [T.tmp2])
        qfin = T.kfin
        P.copy("pool", qfin[:, :, 0:NOPE], tg3[:, :, 0:NOPE], [T.tmp2], [qfin])
        P.tt(ENG_EW, T.r1[:], tg3[:, :, NOPE:QK], cc[:, sub:sub + 1, :].to_broadcast([128, NH, ROPE]), ALU.mult,
             [T.tmp2, cc], [T.r1])
        P.tt(ENG_EW, T.r2[:, :, 0:16], tg3[:, :, NOPE + 16:QK], sn[:, sub:sub + 1, 0:16].to_broadcast([128, NH, 16]),
             ALU.mult, [T.tmp2, sn], [T.r2])
        P.tt(ENG_EW, T.r2[:, :, 16:32], tg3[:, :, NOPE:NOPE + 16], sn[:, sub:sub + 1, 16:32].to_broadcast([128, NH, 16]),
             ALU.mult, [T.tmp2, sn], [T.r2])
        P.tt(ENG_EW, qfin[:, :, NOPE:QK], T.r1[:], T.r2[:], ALU.add, [T.r1, T.r2], [qfin])

    def q_tail(t):
        pos, sub, own = tile_info(t)[0:3]
        B = blks[pos % 2]
        T = slots[t % NSL]
        qfin = T.kfin
        for h in range(NH):
            P.tr(pT[0:QK, h, :], qfin[:, h, :], ident[:], [qfin, ident], [pT])
        if own:
            P.copy("act", B.QT[:, :, sub * 128:(sub + 1) * 128], pT[0:QK, :, :], [pT], [B.QT])
        else:
            P.copy("act", B.QT[:, :, 0:64], pT[0:QK, :, 64:128], [pT], [B.QT])

    def captured(fn, *a):
        P.capture()
        fn(*a)
        return P.end_capture()

    def back_kv(t):
        kv_head(t)
        P.emit_zip([captured(k_pre, t), captured(dqk_pre, t, 0)])
        k_tail(t)
        dqk_tail(t, 0)
        dv_copy(t)

    def back_rest(t):
        if not tile_info(t)[4]:
            return
        q_head(t)
        P.emit_zip([captured(q_pre, t), captured(dqk_pre, t, 1)])
        q_tail(t)
        dqk_tail(t, 1)

    front(0)
    zmm(0)
    evac(0)
    for pos in range(C.NBLK):
        own = pos in C.own_pos
        halo_next = (pos + 1) in C.seg_starts
        B = blks[pos % 2]
        P.dma("sp", ccs[pos % 2][:], G.rope_cc[:, pos * 4:pos * 4 + 4, :], [G.rope_cc], [ccs[pos % 2]])
        P.dma("sp", sns[pos % 2][:], G.rope_sn[:, pos * 4:pos * 4 + 4, :], [G.rope_sn], [sns[pos % 2]])
        for sub in range(4):
            t = pos * 4 + sub
            if t + 2 < ntiles:
                load_x(t + 2)
            if t + 1 < ntiles:
                front(t + 1)
            back_kv(t)
            if t + 1 < ntiles:
                zmm(t + 1)
            back_rest(t)
            if t + 1 < ntiles:
                evac(t + 1)
            if int(os.environ.get("K_SERIAL", "0")):
                P.flush()
        c0 = pos * 512
        P.dma("sp", G.KTm[:, :, c0:c0 + 512].rearrange("h p s -> p h s"), B.KT[:], [B.KT], [G.KTm], chan=("st", "KT", pos % 2))
        P.dma("sp", G.Vm[:, :, pos * 4:pos * 4 + 4, :].rearrange("h p t c -> p h t c"), B.V[:], [B.V], [G.Vm],
              chan=("st", "V", pos % 2))
        P.dma("sp", G.KTd[:, :, c0:c0 + 512].rearrange("h p s -> p h s"), B.dKT[:], [B.dKT], [G.KTd],
              chan=("st", "dKT", pos % 2))
        P.dma("sp", G.Vd[:, :, pos * 4:pos * 4 + 4, :].rearrange("h p t c -> p h t c"), B.dV[:], [B.dV], [G.Vd],
              chan=("st", "dV", pos % 2))
        if own:
            q0 = C.qcol(pos)
            P.dma("sp", G.QTm[:, :, q0:q0 + 512].rearrange("h p s -> p h s"), B.QT[:], [B.QT], [G.QTm],
                  chan=("st", "QT", pos % 2))
            P.dma("sp", G.QTd[:, :, q0:q0 + 512].rearrange("h p s -> p h s"), B.dQT[:], [B.dQT], [G.QTd],
                  chan=("st", "dQT", pos % 2))
        elif halo_next:
            q0 = C.halocol(pos + 1)
            P.dma("sp", G.QTm[:, :, q0:q0 + 64].rearrange("h p s -> p h s"), B.QT[:, :, 0:64], [B.QT], [G.QTm],
                  chan=("st", "QT", pos % 2))
            P.dma("sp", G.QTd[:, :, q0:q0 + 64].rearrange("h p s -> p h s"), B.dQT[:, :, 0:64], [B.dQT], [G.QTd],
                  chan=("st", "dQT", pos % 2))
    P.flush()
    P.end_scope(old)


def phase_B(P, C, G):
    old = P.phase_scope()
    sb, ps = P.sbuf, P.psum
    NT, NQ, S = C.NT, C.NQ, C.S
    Sm = [ps("Sm%d" % k, [128, 1024], F32) for k in range(2)]
    Ab = [ps("A%d" % k, [128, 512], F32) for k in range(4)]
    ones_bf = sb("ones_bf", [128, 128], BF16)
    P.memset("pool", ones_bf[:], 1.0, [], [ones_bf])
    epsb = sb("epsb", [128, 1], F32)
    P.memset("pool", epsb[:], EPS, [], [epsb])
    valid = sb("validB", [128, NT], F32)
    P.dma("sp", valid[:], G.valid[:, :], [G.valid], [valid])
    validT = sb("validT", [128, NT, 128], BF16)
    P.copy("pool", validT[:], valid[:].unsqueeze(2).to_broadcast([128, NT, 128]), [valid], [validT])
    lam4 = sb("lam4", [128, 4, DQK], F32)
    for k, nm in enumerate(("lambda_q1", "lambda_k1", "lambda_q2", "lambda_k2")):
        P.dma("sp", lam4[:, k, :], bc_ap(getattr(G, nm), DQK), [getattr(G, nm)], [lam4], chan=("lam", k))
    lamt = sb("lamt", [128, 8], F32)
    lamp = sb("lamp", [128, 2, DQK], F32)
    P.tt("dve", lamp[:, 0, :], lam4[:, 0, :], lam4[:, 1, :], ALU.mult, [lam4], [lamp])
    P.tt("dve", lamp[:, 1, :], lam4[:, 2, :], lam4[:, 3, :], ALU.mult, [lam4], [lamp])
    P.red(lamt[:, 0:2], lamp[:], [lamp], [lamt])
    P.act(lamt[:, 2:4], lamt[:, 0:2], AF.Exp, [lamt], [lamt])
    P.tt("dve", lamt[:, 4:5], lamt[:, 3:4], lamt[:, 2:3], ALU.subtract, [lamt], [lamt])
    P.ts("dve", lamt[:, 5:6], lamt[:, 4:5], -LAM_INIT, None, ALU.add, None, [lamt], [lamt])
    neglam = lamt[:, 5:6]
    gsc = sb("gsc", [128, 1], F32)
    P.dma("sp", gsc[:], bass.AP(G.diff_out_norm_g.t, 0, [[1, 128], [1, 1]]), [G.diff_out_norm_g], [gsc])
    P.ts("dve", gsc[:], gsc[:], 1.0 - LAM_INIT, None, ALU.mult, None, [gsc], [gsc])
    KTs = [sb("KTs%d" % k, [128, S], BF16) for k in range(2)]
    Vs = [sb("Vs%d" % k, [128, NT, 128], BF16) for k in range(2)]
    Qs = [sb("Qs%d" % k, [128, NQ], BF16) for k in range(2)]
    NPT = 3
    Pts = [sb("Pt%d" % k, [128, 1024], BF16) for k in range(NPT)]
    densb = [sb("densb%d" % k, [128, 512], F32) for k in range(2)]
    tnum = [sb("tnum%d" % k, [128, 512], F32) for k in range(2)]
    osb = sb("osb", [128, 512], F32)
    sqb = sb("sqb", [128, 512], BF16)
    rstd = sb("rstdB", [128, 512], F32)
    ystage = [sb("ystage%d" % k, [128, 512], BF16) for k in range(2)]

    heads = [("m", h) for h in range(NH)] + [("d", h) for h in range(DH)]

    def load_head(idx):
        kind, h = heads[idx]
        sl = idx % 2
        if kind == "m":
            P.dma("sp", KTs[sl][0:QK, :], G.KTm[h], [G.KTm], [KTs[sl]])
            P.dma("sp", Vs[sl][:], G.Vm[h], [G.Vm], [Vs[sl]])
            P.dma("sp", Qs[sl][0:QK, :], G.QTm[h], [G.QTm], [Qs[sl]])
        else:
            P.dma("sp", KTs[sl][:], G.KTd[h], [G.KTd], [KTs[sl]])
            P.dma("sp", Vs[sl][:], G.Vd[h], [G.Vd], [Vs[sl]])
            P.dma("sp", Qs[sl][:], G.QTd[h], [G.QTd], [Qs[sl]])

    qtiles = []
    for pos in C.own_pos:
        if pos in C.seg_starts:
            qtiles.append((C.halocol(pos), 64, 4 * pos, False))
        qtiles.append((C.qcol(pos), 512, 4 * pos, True))

    def batches_of(kind, w, nfull, diag):
        bl = []
        nmap = 1 if kind == "m" else 2
        if diag:
            for j in range(4):
                ww = 512 - 128 * j
                exps = [(0, ww)] if nmap == 1 else [(0, ww), (512, 512 + ww)]
                bl.append(([(nfull + j, 0, 128 * j, ww, True, 0)], exps))
        if w == 512:
            if nmap == 1:
                for kt in range(0, nfull, 2):
                    bl.append(([(kt, 0, 0, 512, False, None), (kt + 1, 512, 0, 512, False, None)], [(0, 1024)]))
            else:
                for kt in range(nfull):
                    d = nfull - kt
                    bl.append(([(kt, 0, 0, 512, False, d if d <= NEAR else None)], [(0, 1024)]))
        else:
            per = 16 if nmap == 1 else 8
            for k0 in range(0, nfull, per):
                n_ = min(per, nfull - k0)
                ents = []
                for i_ in range(n_):
                    d = nfull - (k0 + i_)
                    ents.append((k0 + i_, 64 * i_, 0, 64, False, (d - 1) if d <= 6 else None))
                exps = [(0, 64 * n_)] if nmap == 1 else ([(0, 1024)] if n_ == 8 else [(0, 64 * n_), (512, 512 + 64 * n_)])
                bl.append((ents, exps))
        return bl

    load_head(0)
    late = {}

    def late_setup():
        tab = sb("tab", [NUM_BUCKETS, DH], F32)
        P.dma("sp", tab[:], G.rel_bias[:, :], [G.rel_bias], [tab])
        tabh = sb("tabh", [NUM_BUCKETS, DH], BF16)
        tabl = sb("tabl", [NUM_BUCKETS, DH], BF16)
        tabr = sb("tabr", [NUM_BUCKETS, DH], F32)
        P.copy("dve", tabh[:], tab[:], [tab], [tabh])
        P.copy("dve", tabr[:], tabh[:], [tabh], [tabr])
        P.tt("dve", tabr[:], tab[:], tabr[:], ALU.subtract, [tab, tabr], [tabr])
        P.copy("dve", tabl[:], tabr[:], [tabr], [tabl])
        ohf = sb("ohf", [NUM_BUCKETS, LV], F32)
        P.dma("sp", ohf[:], G.t5_onehot[:, :], [G.t5_onehot], [ohf])
        ohb = sb("ohb", [NUM_BUCKETS, LV], BF16)
        P.copy("dve", ohb[:], ohf[:], [ohf], [ohb])
        tbh = sb("tbh", [NUM_BUCKETS, DH, 128], BF16)
        tbl = sb("tbl", [NUM_BUCKETS, DH, 128], BF16)
        P.copy("dve", tbh[:], tabh[:].unsqueeze(2).to_broadcast([NUM_BUCKETS, DH, 128]), [tabh], [tbh])
        P.copy("dve", tbl[:], tabl[:].unsqueeze(2).to_broadcast([NUM_BUCKETS, DH, 128]), [tabl], [tbl])
        Rsb = sb("Rsb", [128, LV], F32)
        cfar = sb("cfar", [128, DH], F32)
        ncfar = sb("ncfar", [128, DH], F32)
        Et = sb("Et", [128, DH * 6, 512], BF16)
        Etmp = [sb("Etmp%d" % k, [128, 512], F32) for k in range(2)]
        cgs = [(0, 512), (512, 1024), (1024, LV)]
        for h in range(DH):
            for k, (c0, c1) in enumerate(cgs):
                P.mm(Ab[k][:, 0:c1 - c0], tbh[:, h, :], ohb[:, c0:c1], True, False, [tbh, ohb], [Ab[k]])
                P.mm(Ab[k][:, 0:c1 - c0], tbl[:, h, :], ohb[:, c0:c1], False, True, [tbl, ohb], [Ab[k]])
                P.copy("act", Rsb[:, c0:c1], Ab[k][:, 0:c1 - c0], [Ab[k]], [Rsb])
            P.copy("dve", cfar[:, h:h + 1], Rsb[:, LV - 1:LV], [Rsb], [cfar])
            P.ts("dve", ncfar[:, h:h + 1], cfar[:, h:h + 1], -1.0, None, ALU.mult, None, [cfar], [ncfar])
            P.act(Rsb[:], Rsb[:], AF.Exp, [Rsb, ncfar], [Rsb], bias=ncfar[:, h:h + 1])
            P.dma("sp", G.Rs[h], Rsb[:], [Rsb], [G.Rs], chan="Rs")
            for d in range(6):
                et = Etmp[(h * 6 + d) % 2]
                src = bass.AP(G.Rs.t, h * 128 * LV + RMAX + 128 * d, [[LV - 1, 128], [1, 512]])
                P.dma("sp", et[:], src, [G.Rs], [et])
                P.copy("pool", Et[:, h * 6 + d, :], et[:], [et], [Et])
        for f in range(NF):
            P.dma("pool", G.wgu[f, :, 0, :, :], G.w_gate[0][:, f * 128:(f + 1) * 128].rearrange("(c p) j -> p c j", p=128),
                  [G.w_gate], [G.wgu], chan=("wprep", (3 * f) % 6))
            P.dma("pool", G.wgu[f, :, 1, :, :], G.w_up[0][:, f * 128:(f + 1) * 128].rearrange("(c p) j -> p c j", p=128),
                  [G.w_up], [G.wgu], chan=("wprep", (3 * f + 1) % 6))
            P.dma("pool", G.wdn[:, f, :], G.w_down[0][f * 128:(f + 1) * 128, :], [G.w_down], [G.wdn],
                  chan=("wprep", (3 * f + 2) % 6))


        late["Et"], late["cfar"] = Et, cfar

    ycount = 0
    bcount = 0
    for idx, (kind, h) in enumerate(heads):
        if idx + 1 < len(heads):
            load_head(idx + 1)
        if idx == 1:
            late_setup()
        Et, cfar = late.get("Et"), late.get("cfar")
        sl = idx % 2
        KT, V, Q = KTs[sl], Vs[sl], Qs[sl]
        nmap = 1 if kind == "m" else 2
        kdim = QK if kind == "m" else DQK
        sm_scale = SM_MLA if kind == "m" else SM_DIFF
        for ti, (qc, w, nfull, diag) in enumerate(qtiles):
            bl = batches_of(kind, w, nfull, diag)
            nb = len(bl)
            if kind == "m":
                accs = [(Ab[ti % 2], V)]
            else:
                accs = None
            total_pv = sum(len(b[0]) for b in bl)

            def s_mm(bi_):
                S_ = Sm[(bcount + bi_) % 2]
                for (kt, so, qo, ww, _, _) in bl[bi_][0]:
                    for m in range(nmap):
                        P.mm(S_[:, m * 512 + so:m * 512 + so + ww], KT[64 * m:64 * m + kdim, kt * 128:(kt + 1) * 128],
                             Q[64 * m:64 * m + kdim, qc + qo:qc + qo + ww], True, True, [KT, Q], [S_])
            s_mm(0)
            pv = 0
            for bi_ in range(nb):
                ents, exps = bl[bi_]
                if bi_ + 1 < nb:
                    s_mm(bi_ + 1)
                S_ = Sm[(bcount + bi_) % 2]
                pt = Pts[(bcount + bi_) % NPT]
                for (c0, c1) in exps:
                    if kind == "m":
                        P.act(pt[:, c0:c1], S_[:, c0:c1], AF.Exp, [S_], [pt], scale=sm_scale)
                    else:
                        P.act(pt[:, c0:c1], S_[:, c0:c1], AF.Exp, [S_, cfar], [pt], scale=sm_scale, bias=cfar[:, h:h + 1])
                for (kt, so, qo, ww, mask, near) in ents:
                    if kind == "d" and near is not None:
                        e0 = 64 if w == 64 else 0
                        pv3 = pt[:].rearrange("p (m c) -> p m c", m=2)[:, :, so:so + ww]
                        P.tt("dve", pv3, pv3, Et[:, h * 6 + near, e0:e0 + ww].unsqueeze(1).to_broadcast([128, 2, ww]),
                             ALU.mult, [pt, Et], [pt])
                    first, last = (pv == 0), (pv == total_pv - 1)
                    pv += 1
                    parts = [(0, 128, 0, ww)] if not mask else [(0, 128, 64, ww), (0, 64, 0, 64)]
                    for pi_, (k0, k1, c0, c1) in enumerate(parts):
                        if c1 <= c0:
                            continue
                        if pi_ > 0:
                            first = False
                        if kind == "m":
                            O = Ab[ti % 2]
                            P.mm(O[:, qo + c0:qo + c1], V[k0:k1, kt, :], pt[k0:k1, so + c0:so + c1], first, last, [V, pt], [O])
                        else:
                            for m in range(2):
                                P.mm(Ab[2 * m][:, qo + c0:qo + c1], V[k0:k1, kt, :],
                                     pt[k0:k1, m * 512 + so + c0:m * 512 + so + c1], first, last, [V, pt], [Ab[2 * m]])
                                P.mm(Ab[2 * m + 1][:, qo + c0:qo + c1], validT[k0:k1, kt, :],
                                     pt[k0:k1, m * 512 + so + c0:m * 512 + so + c1], first, last,
                                     [validT, pt], [Ab[2 * m + 1]])
            bcount += nb
            if kind == "m":
                O = Ab[ti % 2]
                ds = densb[ti % 2]
                P.copy("act", ds[0:64, 0:w], O[64:128, 0:w], [O], [ds])
                P.ts("dve", ds[0:64, 0:w], ds[0:64, 0:w], 1e-30, None, ALU.max, None, [ds], [ds])
                P.recip(ds[0:64, 0:w], ds[0:64, 0:w], [ds], [ds])
                ys = ystage[ycount % 2]
                ycount += 1
                P.tt("dve", ys[0:64, 0:w], O[0:64, 0:w], ds[0:64, 0:w], ALU.mult, [O, ds], [ys])
                r0 = (h % 2) * 64
                P.dma("sp", G.yT[r0:r0 + 64, h // 2, qc:qc + w], ys[0:64, 0:w], [ys], [G.yT],
                      chan=("yst", ycount % 2))
            else:
                for m in range(2):
                    P.copy("act", densb[m][:, 0:w], Ab[2 * m + 1][:, 0:w], [Ab[2 * m + 1]], [densb[m]])
                    P.copy("act", tnum[m][:, 0:w], Ab[2 * m][:, 0:w], [Ab[2 * m]], [tnum[m]])
                for m in range(2):
                    ds = densb[m]
                    P.ts("dve", ds[:, 0:w], ds[:, 0:w], 1e-30, None, ALU.max, None, [ds], [ds])
                    P.recip(ds[:, 0:w], ds[:, 0:w], [ds], [ds])
                    P.tt("dve", tnum[m][:, 0:w], tnum[m][:, 0:w], ds[:, 0:w], ALU.mult, [tnum[m], ds], [tnum[m]])
                ys = ystage[ycount % 2]
                ycount += 1
                P.stt("dve", ys[:, 0:w], tnum[1][:, 0:w], neglam, tnum[0][:, 0:w], ALU.mult, ALU.add,
                      [tnum[0], tnum[1], lamt], [ys])
                P.dma("sp", G.yT[:, 4 + h, qc:qc + w], ys[:, 0:w], [ys], [G.yT], chan=("yst", ycount % 2))
    P.flush()
    P.end_scope(old)


def phase_C(P, C, G):
    old = P.phase_scope()
    sb, ps = P.sbuf, P.psum
    pX = ps("pX", [128, D], F32)
    pTt = ps("pTt", [128, 8, 128], BF16)
    pG = [ps("pG%d" % k, [128, 512], F32) for k in range(2)]
    pU = [ps("pU%d" % k, [128, 512], F32) for k in range(2)]
    pH = ps("pH", [128, 512], F32)
    wout = sb("wout", [128, 8, D], BF16)
    wpg = sb("wpg", [128, 8, D], BF16)
    wpp = sb("wpp", [128, 2, D], BF16)
    wd = sb("wd", [128, NF, D], BF16)
    P.dma("pool", wout[:], G.w_out[0].rearrange("(c p) n -> p c n", p=128), [G.w_out], [wout])
    P.dma("pool", wpg[:], G.w_ple_gate[0].rearrange("(c p) n -> p c n", p=128), [G.w_ple_gate], [wpg])
    P.dma("pool", wpp[:], G.w_ple_proj[0].rearrange("(c p) n -> p c n", p=128), [G.w_ple_proj], [wpp])
    P.dma("sp", wd[:], G.wdn[:, :, :], [G.wdn], [wd])
    gF = sb("gF", [128, D], F32)
    gP = sb("gP", [128, D], F32)
    P.dma("sp", gF[:], bc_ap(G.ffn_norm_g, D), [G.ffn_norm_g], [gF])
    P.dma("sp", gP[:], bc_ap(G.ple_norm_g, D), [G.ple_norm_g], [gP])
    ident = sb("identC", [128, 128], BF16)
    P.memset("pool", ident[:], 1.0, [], [ident])
    P.op("pool", lambda e: e.affine_select(ident[:], ident[:], [[-1, 128]], ALU.is_equal, 0.0,
                                           base=0, channel_multiplier=1), [ident], [ident])
    identf = sb("identf", [128, 128], F32)
    P.copy("pool", identf[:], ident[:], [ident], [identf])
    cwr = sb("cwr", [NF, 4, 128], F32)
    for jj in range(3):
        P.dma("sp", cwr[:, jj, :], G.conv_w[0, jj].rearrange("(f p) -> f p", p=128), [G.conv_w], [cwr], chan=("cw", jj))
    P.dma("sp", cwr[:, 3, :], G.conv_b[0].rearrange("(f p) -> f p", p=128), [G.conv_b], [cwr], chan=("cw", 3))
    cw = sb("cw", [128, 4, NF], F32)
    pHc = pH[:, 0:4 * NF].rearrange("p (j f) -> p j f", j=4)
    for jj in range(4):
        P.op("pe", lambda e, jj=jj: e.transpose(pHc[:, jj, :], cwr[:, jj, :], identf[0:NF, 0:NF]), [cwr, identf], [pH])
    P.copy("act", cw[:], pHc, [pH], [cw])
    yTb = [sb("yTb%d" % k, [128, 8, 512], BF16) for k in range(1)]
    yTh = sb("yTh", [128, 8, 64], BF16)
    xin = [sb("xin%d" % k, [128, D], F32) for k in range(2)]
    x1b = sb("x1b", [128, 4, D], F32)
    h2 = [sb("h2_%d" % k, [128, D], BF16) for k in range(3)]
    junk = sb("junkC", [128, D], BF16)
    stc = [sb("stc%d" % k, [128, 4], F32) for k in range(3)]
    h2T = sb("h2T", [128, 8, 512], BF16)
    h2Th = sb("h2Th", [128, 8, 64], BF16)
    actT = sb("actT", [128, NF, 512], BF16)
    wgu = [sb("wgu%d" % k, [128, 2, 8, 128], BF16) for k in range(3)]
    gext = [sb("gext%d" % k, [128, 514], F32) for k in range(2)]
    carry = sb("carry", [128, NF, 2], F32)
    cacc = [sb("cacc%d" % k, [128, 512], F32) for k in range(2)]
    sl = [sb("sl%d" % k, [128, 512], F32) for k in range(2)]
    h3T = sb("h3T", [128, 8, 128], BF16)
    pin = [sb("pin%d" % k, [128, PLE], F32) for k in range(2)]
    pb = sb("pb", [128, PLE], BF16)
    pTT = sb("pTT", [128, 2, 128], BF16)
    sig = sb("sig", [128, D], F32)
    outt = [sb("outt%d" % k, [128, D], F32) for k in range(2)]
    P.memset("pool", carry[:], 0.0, [], [carry])
    ones_c = sb("ones_c", [128, 128], BF16)
    P.memset("pool", ones_c[:], 1.0, [], [ones_c])
    epsc = sb("epsc", [128, 1], F32)
    P.memset("pool", epsc[:], EPS, [], [epsc])
    gscc = sb("gscc", [128, 1], F32)
    P.dma("sp", gscc[:], bass.AP(G.diff_out_norm_g.t, 0, [[1, 128], [1, 1]]), [G.diff_out_norm_g], [gscc])
    P.ts("dve", gscc[:], gscc[:], 1.0 - LAM_INIT, None, ALU.mult, None, [gscc], [gscc])
    sqc = sb("sqc", [128, 512], BF16)
    rsc = sig

    def diff_out_norm(yt, w):
        for hh in range(DH):
            ych = yt[:, 4 + hh, 0:w]
            P.act(sqc[:, 0:w], ych, AF.Square, [yt], [sqc])
            P.mm(pH[:, 0:w], ones_c[:], sqc[:, 0:w], True, True, [ones_c, sqc], [pH])
            P.act(rsc[:, 0:w], pH[:, 0:w], AF.Sqrt, [pH, epsc], [rsc], scale=1.0 / DV, bias=epsc[:])
            P.recip(rsc[:, 0:w], rsc[:, 0:w], [rsc], [rsc])
            P.stt("dve", ych, ych, gscc[:], rsc[:, 0:w], ALU.mult, ALU.mult, [yt, gscc, rsc], [yt])

    def rsqrt_inplace(v, R):
        P.act(v, v, AF.Sqrt, R, R)
        P.recip(v, v, R, R)

    xcount = [0]

    def wout_mm(np_, ylhs, yref, xrows, x1dst, x1ref):
        xt = xin[xcount[0] % 2]
        xcount[0] += 1
        P.dma("sp", xt[0:np_, :], xrows, [G.xbuf], [xt])
        for hf in range(2):
            for c in range(8):
                P.mm(pX[0:np_, hf * 512:(hf + 1) * 512], ylhs(c), wout[:, c, hf * 512:(hf + 1) * 512], c == 0, c == 7,
                     [wout, yref], [pX])
        return xt

    def wout_add(xt, np_, ylhs, yref, xrows, x1dst, x1ref):
        P.tt("dve", x1dst, pX[0:np_, :], xt[0:np_, :], ALU.add, [pX, xt], [x1ref])

    def wout_p2(np_, x1dst, x1ref, stt_, h2t, h2T_dst, h2Tref):
        P.act(junk[0:np_, :], x1dst, AF.Square, [x1ref], [junk, stt_], accum_out=stt_[0:np_, 0:1])
        P.ts("dve", stt_[0:np_, 0:1], stt_[0:np_, 0:1], 1.0 / D, EPS, ALU.mult, ALU.add, [stt_], [stt_])
        rsqrt_inplace(stt_[0:np_, 0:1], [stt_])
        P.stt("dve", h2t[0:np_, :], x1dst, stt_[0:np_, 0:1], gF[0:np_, :], ALU.mult, ALU.mult, [x1ref, stt_, gF], [h2t])
        for c in range(8):
            P.tr(pTt[:, c, 0:np_], h2t[0:np_, c * 128:(c + 1) * 128], ident[0:np_, 0:np_], [h2t, ident], [pTt])
        P.copy("act", h2T_dst, pTt[:, :, 0:np_], [pTt], [h2Tref])

    def load_w(f):
        w = wgu[f % 3]
        P.dma("sp", w[:], G.wgu[f], [G.wgu], [w])

    for bi, pos in enumerate(C.own_pos):
        qc = C.qcol(pos)
        yb = yTb[0]
        P.dma("sp", yb[:], G.yT[:, :, qc:qc + 512], [G.yT], [yb])
        diff_out_norm(yb, 512)
        first = pos in C.seg_starts
        stages = []
        if first:
            hc = C.halocol(pos)
            P.dma("sp", yTh[:], G.yT[:, :, hc:hc + 64], [G.yT], [yTh])
            diff_out_norm(yTh, 64)

            def ylhs_h(c):
                return yTh[:, c, :]
            x1h = outt[0]
            stages.append(((64, ylhs_h, yTh, G.xbuf[pos * 512 - 64:pos * 512, :], x1h[0:64, :], x1h),
                           (64, x1h[0:64, :], x1h, stc[2], h2[2], h2Th[:], h2Th)))
        for sub in range(4):
            def ylhs(c, sub=sub):
                return yb[:, c, sub * 128:(sub + 1) * 128]
            r0 = pos * 512 + sub * 128
            stages.append(((128, ylhs, yb, G.xbuf[r0:r0 + 128, :], x1b[:, sub, :], x1b),
                           (128, x1b[:, sub, :], x1b, stc[sub % 2], h2[sub % 2],
                            h2T[:, :, sub * 128:(sub + 1) * 128], h2T)))
        xt0 = wout_mm(*stages[0][0])
        wout_add(xt0, *stages[0][0])
        for k in range(len(stages)):
            if k + 1 < len(stages):
                xt1 = wout_mm(*stages[k + 1][0])
            wout_p2(*stages[k][1])
            if k + 1 < len(stages):
                wout_add(xt1, *stages[k + 1][0])
        load_w(0)
        load_w(1)

        def ffn_front(f):
            if f + 2 < NF:
                load_w(f + 2)
            w = wgu[f % 3]
            if first:
                for c in range(8):
                    P.mm(pH[:, 0:64], w[:, 0, c, :], h2Th[:, c, :], c == 0, c == 7, [w, h2Th], [pH])
                P.copy("act", carry[:, f, :], pH[:, 62:64], [pH], [carry])
            g_ps, u_ps = pG[f % 2], pU[f % 2]
            for c in range(8):
                P.mm(g_ps[:], w[:, 0, c, :], h2T[:, c, :], c == 0, c == 7, [w, h2T], [g_ps])
            for c in range(8):
                P.mm(u_ps[:], w[:, 1, c, :], h2T[:, c, :], c == 0, c == 7, [w, h2T], [u_ps])
            ge = gext[f % 2]
            P.copy("act", ge[:, 2:514], g_ps[:], [g_ps], [ge])
            P.copy("pool", ge[:, 0:2], carry[:, f, :], [carry], [ge])
            P.copy("pool", carry[:, f, :], ge[:, 512:514], [ge], [carry])

        def ffn_back(f):
            u_ps = pU[f % 2]
            ge, ca, s_ = gext[f % 2], cacc[f % 2], sl[f % 2]
            P.ts("dve", ca[:], ge[:, 2:514], cw[:, 2, f:f + 1], cw[:, 3, f:f + 1], ALU.mult, ALU.add, [ge, cw], [ca])
            P.stt("dve", ca[:], ge[:, 1:513], cw[:, 1, f:f + 1], ca[:], ALU.mult, ALU.add, [ge, cw, ca], [ca])
            P.stt("dve", ca[:], ge[:, 0:512], cw[:, 0, f:f + 1], ca[:], ALU.mult, ALU.add, [ge, cw, ca], [ca])
            P.act(s_[:], ca[:], AF.Silu, [ca], [s_])
            P.tt("dve", actT[:, f, :], u_ps[:], s_[:], ALU.mult, [u_ps, s_], [actT])

        ffn_front(0)
        for f in range(NF):
            if f + 1 < NF:
                ffn_front(f + 1)
            ffn_back(f)
        def down_mm(sub):
            for hf in range(2):
                for f in range(NF):
                    P.mm(pX[:, hf * 512:(hf + 1) * 512], actT[:, f, sub * 128:(sub + 1) * 128],
                         wd[:, f, hf * 512:(hf + 1) * 512], f == 0, f == NF - 1, [actT, wd], [pX])

        def down_add(sub):
            x2_ = x1b[:, sub, :]
            P.tt("dve", x2_, pX[:], x2_, ALU.add, [pX, x1b], [x1b])

        def ple(sub):
            x2 = x1b[:, sub, :]
            stt_ = stc[sub % 2]
            P.act(junk[:], x2, AF.Square, [x1b], [junk, stt_], accum_out=stt_[:, 1:2])
            P.ts("dve", stt_[:, 1:2], stt_[:, 1:2], 1.0 / D, EPS, ALU.mult, ALU.add, [stt_], [stt_])
            rsqrt_inplace(stt_[:, 1:2], [stt_])
            h3 = h2[sub % 2]
            P.stt("dve", h3[:], x2, stt_[:, 1:2], gP[:], ALU.mult, ALU.mult, [x1b, stt_, gP], [h3])
            pi = pin[sub % 2]
            orow = qc + sub * 128
            P.dma("sp", pi[:], G.p_own[orow:orow + 128, :], [G.p_own], [pi])
            P.copy("pool", pb[:], pi[:], [pi], [pb])
            for c in range(2):
                P.tr(pH[:].bitcast(BF16)[:, c * 128:(c + 1) * 128], pb[:, c * 128:(c + 1) * 128], ident[:], [pb, ident], [pH])
            P.copy("act", pTT[:], pH[:].bitcast(BF16)[:, 0:256].rearrange("p (c t) -> p c t", c=2), [pH], [pTT])
            for c in range(8):
                P.tr(pTt[:, c, :], h3[:, c * 128:(c + 1) * 128], ident[:], [h3, ident], [pTt])
            P.copy("act", h3T[:], pTt[:], [pTt], [h3T])
            ot = outt[sub % 2]
            gl = (pG[0], pU[0])
            pj = (pG[1], pU[1])
            for hf in range(2):
                for c in range(2):
                    P.mm(pj[hf][:], pTT[:, c, :], wpp[:, c, hf * 512:(hf + 1) * 512], c == 0, c == 1, [pTT, wpp], [pj[hf]])
            for hf in range(2):
                for c in range(8):
                    P.mm(gl[hf][:], h3T[:, c, :], wpg[:, c, hf * 512:(hf + 1) * 512], c == 0, c == 7, [h3T, wpg], [gl[hf]])
                P.act(sig[:, hf * 512:(hf + 1) * 512], gl[hf][:], AF.Sigmoid, [gl[hf]], [sig])
                P.tt("dve", ot[:, hf * 512:(hf + 1) * 512], pj[hf][:], sig[:, hf * 512:(hf + 1) * 512], ALU.mult,
                     [pj[hf], sig], [ot])
            P.tt("pool", ot[:], ot[:], x2, ALU.add, [ot, x1b], [ot])
            P.dma("pool", G.out[orow:orow + 128, :], ot[:], [ot], [G.out], chan=("out", sub % 2))

        down_mm(0)
        down_add(0)
        for sub in range(4):
            if sub + 1 < 4:
                down_mm(sub + 1)
            ple(sub)
            if sub + 1 < 4:
                down_add(sub + 1)
    P.end_scope_deferred = old


def build_program(C, debug=(), phases="ABC"):
    nc = bass.Bass("TRN2", target_bir_lowering=False)
    P = Prog(nc)
    G = declare_dram(P, C, debug)
    if "A" in phases:
        phase_A(P, C, G)
    if "B" in phases:
        phase_B(P, C, G)
    if "C" in phases:
        phase_C(P, C, G)
    P.flush(final=True)
    if getattr(P, "end_scope_deferred", None) is not None:
        P.end_scope(P.end_scope_deferred)
    return nc, P


SEQ_FULL = 8192
SEG_L = 2
N_CORES = 8
_CACHE = {}


def kernel(**inputs):
    C = Cfg(SEQ_FULL, SEG_L)
    if "prog" not in _CACHE:
        _CACHE["prog"] = build_program(C)
    nc, _ = _CACHE["prog"]
    in_maps, metas = [], []
    for core in range(N_CORES):
        m, own_tok, real = host_inputs(inputs, core, C)
        in_maps.append(m)
        metas.append((own_tok, real))
    res = run_bass_kernel_spmd(nc, in_maps, core_ids=list(range(N_CORES)))
    out = np.zeros((4, SEQ_FULL, D), np.float32)
    for core in range(N_CORES):
        own_tok, real = metas[core]
        out[core // 2, real[own_tok]] = np.asarray(res.results[core]["out"], dtype=np.float32)
    return out
```

```python
import contextlib
import numpy as np
import concourse.bass as bass
import concourse.mybir as mybir
from concourse.bass_utils import run_bass_kernel_spmd

F32 = mybir.dt.float32
BF16 = mybir.dt.bfloat16
ALU = mybir.AluOpType
AF = mybir.ActivationFunctionType
AX = mybir.AxisListType

import os
SES_ENGS = set(os.environ.get("K_SES", "act,dve,pool,sp").split(","))
MAXOPS = int(os.environ.get("K_MAXOPS", "100000000"))
VERBOSE_OPS = bool(int(os.environ.get("K_VERBOSE", "0")))


class TRef:
    def __init__(self, obj, name, handle=None, psum=False):
        self.o = obj
        self.name = name
        self.t = handle if handle is not None else obj
        self.psum = psum

    def __getitem__(self, k):
        return self.o[k]

    def rearrange(self, s, **kw):
        return self.o[:].rearrange(s, **kw) if not hasattr(self.o, "rearrange") else self.o.rearrange(s, **kw)


class _Op:
    __slots__ = ("eng", "fn", "reads", "writes", "is_dma", "chan", "deps", "needed", "ms", "idx")


class Prog:
    ENGS = ("pe", "act", "dve", "pool", "sp")

    def __init__(self, nc):
        self.nc = nc
        self.stack = contextlib.ExitStack()
        self.ops = []
        self.nchan = 0
        self.pending_barrier = {}

    def dram(self, name, shape, dtype, kind="Internal"):
        h = self.nc.dram_tensor(name, list(shape), dtype, kind=kind)
        return TRef(h.ap(), name, h)

    def sbuf(self, name, shape, dtype):
        t = self.stack.enter_context(self.nc.sbuf_tensor(name, list(shape), dtype))
        return TRef(t, name)

    def psum(self, name, shape, dtype):
        t = self.stack.enter_context(self.nc.psum_tensor(name, list(shape), dtype))
        return TRef(t, name, psum=True)

    def op(self, eng, fn, reads=(), writes=()):
        o = _Op()
        reads, writes = tuple(reads), tuple(writes)
        writes = writes + tuple(r for r in reads if getattr(r, "psum", False) and r not in writes)
        o.eng, o.fn, o.reads, o.writes = eng, fn, reads, writes
        o.is_dma, o.chan = False, None
        o.ms = None
        o.idx = None
        o.deps = set()
        cap = getattr(self, "_cap", None)
        if cap is not None:
            cap.append(o)
        else:
            self._commit(o)
        return o

    def _commit(self, o):
        if len(self.ops) >= MAXOPS:
            return
        o.idx = len(self.ops)
        o.deps = set(self.pending_barrier.pop(o.eng, ()))
        self.ops.append(o)

    def capture(self):
        self._cap = []

    def end_capture(self):
        lst, self._cap = self._cap, None
        return lst

    def emit_zip(self, lists):
        lists = [l for l in lists if l]
        pos = [0] * len(lists)
        total = sum(len(l) for l in lists)
        for _ in range(total):
            k = min((i for i in range(len(lists)) if pos[i] < len(lists[i])),
                    key=lambda i: pos[i] / len(lists[i]))
            self._commit(lists[k][pos[k]])
            pos[k] += 1

    def barrier(self):
        last = {}
        for o in self.ops:
            if o.is_dma:
                last[("c", o.chan)] = o.idx
            else:
                last[("e", o.eng)] = o.idx
        allidx = set(last.values())
        for e in self.ENGS:
            self.pending_barrier[e] = set(allidx) | set(self.pending_barrier.get(e, ()))

    def phase_scope(self):
        old = self.stack
        self.stack = contextlib.ExitStack()
        return old

    def end_scope(self, old):
        self.stack.close()
        self.stack = old

    def dma(self, eng, out_ap, in_ap, reads=(), writes=(), chan=None, **kw):
        o = self.op(eng, lambda e: e.dma_start(out=out_ap, in_=in_ap, **kw), reads, writes)
        o.is_dma = True
        o.chan = chan if chan is not None else ("auto", writes[0].name)
        return o

    def act(self, out, in_, func, R, W, **kw):
        return self.op("act", lambda e: e.activation(out, in_, func, **kw), R, W)

    def tt(self, eng, out, a, b, op, R, W):
        return self.op(eng, lambda e: e.tensor_tensor(out, a, b, op), R, W)

    def ts(self, eng, out, a, s1, s2, op0, op1, R, W):
        if s2 is None:
            return self.op(eng, lambda e: e.tensor_scalar(out, a, s1, None, op0), R, W)
        return self.op(eng, lambda e: e.tensor_scalar(out, a, s1, s2, op0, op1), R, W)

    def stt(self, eng, out, in0, scalar, in1, op0, op1, R, W):
        return self.op(eng, lambda e: e.scalar_tensor_tensor(out, in0, scalar, in1, op0, op1), R, W)

    def copy(self, eng, out, in_, R, W):
        if eng == "act":
            return self.op("act", lambda e: e.activation(out, in_, AF.Copy), R, W)
        return self.op(eng, lambda e: e.tensor_copy(out, in_), R, W)

    def red(self, out, in_, R, W):
        return self.op("dve", lambda e: e.tensor_reduce(out, in_, AX.X, ALU.add), R, W)

    def recip(self, out, in_, R, W):
        return self.op("dve", lambda e: e.reciprocal(out, in_), R, W)

    def mm(self, out, lhsT, rhs, start, stop, R, W):
        return self.op("pe", lambda e: e.matmul(out, lhsT, rhs, start=start, stop=stop), R, W)

    def tr(self, out, in_, ident, R, W):
        return self.op("pe", lambda e: e.transpose(out, in_, ident), R, W)

    def memset(self, eng, out, val, R, W):
        return self.op(eng, lambda e: e.memset(out, val), R, W)

    def _init_lower(self):
        nc = self.nc
        self.esem = {e: nc.alloc_semaphore(name="ms_" + e) for e in self.ENGS}
        self.csem = {}
        self.ecount = {e: 0 for e in self.ENGS}
        self.ccount = {}
        self.known = {e: {} for e in self.ENGS}
        self.last_w = {}
        self.readers = {}
        self.chan_last = {}
        self.flushed = 0
        self.n_instr = {e: 0 for e in self.ENGS}
        self._lower_ready = True

    def flush(self, final=False):
        if not getattr(self, "_lower_ready", False):
            self._init_lower()
        nc = self.nc
        ops = self.ops
        new = ops[self.flushed:]
        phase_start = self.flushed
        last_w, readers, chan_last = self.last_w, self.readers, self.chan_last
        for o in new:
            deps = set(o.deps)
            for r in o.reads:
                k = r.name
                if k in last_w:
                    deps.add(last_w[k])
            for w in o.writes:
                k = w.name
                if k in last_w:
                    deps.add(last_w[k])
                for rd in readers.get(k, ()):
                    deps.add(rd)
            if o.is_dma and o.chan in chan_last:
                deps.add(chan_last[o.chan])
            deps.discard(o.idx)
            o.deps = deps
            for w in o.writes:
                last_w[w.name] = o.idx
                readers[w.name] = []
            for r in o.reads:
                readers.setdefault(r.name, []).append(o.idx)
            if o.is_dma:
                chan_last[o.chan] = o.idx
        for o in new:
            o.needed = False
        for o in new:
            for d in o.deps:
                p = ops[d]
                if p.is_dma or p.idx < self.flushed:
                    continue
                if p.eng != o.eng or (o.eng in SES_ENGS):
                    p.needed = True
        lastc = {}
        for o in new:
            if not o.is_dma:
                lastc[o.eng] = o
        for o in lastc.values():
            o.needed = True
        for o in new:
            if o.is_dma and o.chan not in self.csem:
                self.csem[o.chan] = nc.alloc_semaphore(name="ch%d" % len(self.csem))
                self.ccount[o.chan] = 0
        esem, csem, ecount, ccount = self.esem, self.csem, self.ecount, self.ccount
        streams = {e: [] for e in self.ENGS}
        for o in new:
            need = {}
            for d in o.deps:
                p = ops[d]
                if p.is_dma:
                    key = ("c", p.chan)
                else:
                    if p.eng == o.eng and (o.eng not in SES_ENGS):
                        continue
                    key = ("e", p.eng)
                    if p.ms is None:
                        assert p.idx < phase_start, (p.idx, p.eng, o.idx, o.eng)
                        continue
                val = p.ms
                if need.get(key, 0) < val:
                    need[key] = val
            kn = self.known[o.eng]
            waits = []
            for key, val in need.items():
                if kn.get(key, 0) >= val:
                    continue
                kn[key] = val
                sem = csem[key[1]] if key[0] == "c" else esem[key[1]]
                waits.append((sem, val))
            if o.is_dma:
                ccount[o.chan] += 16
                o.ms = ccount[o.chan]
                inc = (csem[o.chan], 16)
            elif o.needed:
                ecount[o.eng] += 1
                o.ms = ecount[o.eng]
                inc = (esem[o.eng], 1)
            else:
                o.ms = None
                inc = None
            streams[o.eng].append((waits, o.fn, inc))
            o.fn = None
        self.flushed = len(ops)
        for e in self.ENGS:
            self.n_instr[e] += len(streams[e])
        fin = None
        if final:
            fin = [(csem[c], ccount[c]) for c in csem if ccount[c] > 0]
        else:
            self.barrier()

        def run(eng_handle, items, final=None):
            for waits, fn, inc in items:
                for sem, val in waits:
                    eng_handle.wait_ge(sem, val)
                ins = fn(eng_handle)
                if inc is not None:
                    ins.then_inc(inc[0], inc[1])
            if final:
                for sem, val in final:
                    eng_handle.wait_ge(sem, val)

        with nc.Block() as block:
            @block.tensor
            def _(e):
                run(e, streams["pe"])

            @block.scalar
            def _(e):
                run(e, streams["act"])

            @block.vector
            def _(e):
                run(e, streams["dve"])

            @block.gpsimd
            def _(e):
                run(e, streams["pool"])

            @block.sync
            def _(e):
                run(e, streams["sp"], fin)

    def emit(self):
        self.flush(final=True)
        self.stack.close()


D = 1024
NH, QK, NOPE, ROPE, VD = 8, 96, 64, 32, 64
QR, KVR = 256, 128
DH, DQK, DV = 4, 64, 128
DFF, NF = 2816, 22
PLE = 256
EPS = 1e-6
IN_COLS = 1952
OQ, OKV, OKR, ODQ, ODK, ODV = 0, 256, 384, 416, 928, 1440
SM_MLA = float(96 ** -0.5)
SM_DIFF = float(64 ** -0.5)
LAM_INIT = 0.8 - 0.6 * 1.0
NUM_BUCKETS = 32
RMAX, RMIN = 127, -1151
LV = RMAX - RMIN + 1
NEAR = 5


class Cfg:
    def __init__(self, S=8192, L=8):
        self.S, self.L = S, L
        self.NBLK = S // 512
        self.NT = S // 128
        self.own_pos = [p for p in range(self.NBLK) if (p // L) % 2 == 1]
        self.seg_starts = [p for p in self.own_pos if p % L == 0]
        self.NOWN = len(self.own_pos) * 512
        self.NQ = self.NOWN + 64 * len(self.seg_starts)

    def qcol(self, pos):
        return self.own_pos.index(pos) * 512

    def halocol(self, seg_pos):
        return self.NOWN + 64 * self.seg_starts.index(seg_pos)


def t5_bucket_np(rel):
    nb = NUM_BUCKETS // 2
    max_exact = nb // 2
    rel = np.asarray(rel, dtype=np.int64)
    sign_off = (rel > 0).astype(np.int64) * nb
    n = np.abs(rel)
    nf = np.maximum(n, 1).astype(np.float32)
    large = max_exact + (np.log(nf / np.float32(max_exact)) / np.float32(np.log(1024 / max_exact))
                         * np.float32(nb - max_exact)).astype(np.int32)
    large = np.minimum(large, nb - 1)
    return sign_off + np.where(n < max_exact, n, large)


def host_inputs(inputs, core, C):
    b, role = core // 2, core % 2
    S, L = C.S, C.L
    x = np.asarray(inputs["x"], dtype=np.float32)[b, :S]
    p = np.asarray(inputs["p"], dtype=np.float32)[0, b, :S]
    shift = 512 * L if role == 0 else 0
    real = np.arange(S) - shift
    ok = real >= 0
    xbuf = np.zeros((S, D), np.float32)
    xbuf[ok] = x[real[ok]]
    own_tok = np.concatenate([np.arange(pp * 512, (pp + 1) * 512) for pp in C.own_pos])
    p_own = np.ascontiguousarray(p[real[own_tok]])
    valid = ok.astype(np.float32).reshape(C.NT, 128).T.copy()
    pos = np.where(ok, real, 0).astype(np.float32)
    inv_freq = (10000.0 ** (-np.arange(16, dtype=np.float32) / np.float32(16))).astype(np.float32)
    ang = pos[:, None] * inv_freq[None, :]
    cos, sin = np.cos(ang).astype(np.float32), np.sin(ang).astype(np.float32)
    cc = np.concatenate([cos, cos], 1).reshape(C.NT, 128, 32).transpose(1, 0, 2).copy()
    sn = np.concatenate([-sin, sin], 1).reshape(C.NT, 128, 32).transpose(1, 0, 2).copy()
    rel = RMAX - np.arange(LV)
    oh = (t5_bucket_np(rel)[None, :] == np.arange(NUM_BUCKETS)[:, None]).astype(np.float32)
    m = {"xbuf": xbuf, "p_own": p_own, "valid": valid, "rope_cc": cc, "rope_sn": sn, "t5_onehot": oh}
    for k, v in inputs.items():
        if k in ("x", "p"):
            continue
        m[k] = np.ascontiguousarray(np.asarray(v, dtype=np.float32))
    return m, own_tok, real


class NS:
    pass


def bc_ap(t, n, off=0):
    return bass.AP(t.t, off, [[0, 128], [1, n]])


def declare_dram(P, C, debug=()):
    G = NS()
    S, NT, NQ = C.S, C.NT, C.NQ
    ei = lambda n, shp: P.dram(n, shp, F32, kind="ExternalInput")
    G.xbuf = ei("xbuf", [S, D])
    G.p_own = ei("p_own", [C.NOWN, PLE])
    G.valid = ei("valid", [128, NT])
    G.rope_cc = ei("rope_cc", [128, NT, 32])
    G.rope_sn = ei("rope_sn", [128, NT, 32])
    G.t5_onehot = ei("t5_onehot", [NUM_BUCKETS, LV])
    shapes = {
        "attn_norm_g": [1, D], "w_in": [1, D, IN_COLS], "q_lat_norm_g": [1, QR], "w_uq": [1, QR, NH * QK],
        "kv_lat_norm_g": [1, KVR], "w_ukv": [1, KVR, NH * 128], "mla_q_norm_g": [1, QK], "mla_k_norm_g": [1, QK],
        "diff_q_norm_g": [1, DQK], "diff_k_norm_g": [1, DQK], "lambda_q1": [1, DQK], "lambda_k1": [1, DQK],
        "lambda_q2": [1, DQK], "lambda_k2": [1, DQK], "diff_out_norm_g": [1, DV], "rel_bias": [NUM_BUCKETS, DH],
        "w_out": [1, D, D], "ffn_norm_g": [1, D], "w_gate": [1, D, DFF], "w_up": [1, D, DFF],
        "conv_w": [1, 3, DFF], "conv_b": [1, DFF], "w_down": [1, DFF, D], "ple_norm_g": [1, D],
        "w_ple_gate": [1, D, D], "w_ple_proj": [1, PLE, D],
    }
    for k, shp in shapes.items():
        setattr(G, k, ei(k, shp))
    G.out = P.dram("out", [C.NOWN, D], F32, kind="ExternalOutput")

    def scr(name, shp, dt=BF16):
        kind = "ExternalOutput" if name in debug else "Internal"
        t = P.dram(name, shp, dt, kind=kind)
        setattr(G, name, t)
        return t
    scr("KTm", [NH, QK, S])
    scr("Vm", [NH, 128, NT, 128])
    scr("QTm", [NH, QK, NQ])
    scr("KTd", [DH, 128, S])
    scr("Vd", [DH, 128, NT, 128])
    scr("QTd", [DH, 128, NQ])
    scr("yT", [128, 8, NQ])
    scr("wgu", [NF, 128, 2, 8, 128])
    scr("wdn", [128, NF, D])
    scr("Rs", [DH, 128, LV], F32)
    return G


def phase_A(P, C, G):
    old = P.phase_scope()
    sb, ps = P.sbuf, P.psum
    NT = C.NT
    win = sb("win", [128, 8, IN_COLS], BF16)
    wuq = sb("wuq", [128, 2, NH * QK], BF16)
    wukv = sb("wukv", [128, NH * 128], BF16)
    P.dma("pool", win[:], G.w_in[0].rearrange("(c p) n -> p c n", p=128), [G.w_in], [win])
    P.dma("pool", wuq[:], G.w_uq[0].rearrange("(c p) n -> p c n", p=128), [G.w_uq], [wuq])
    P.dma("pool", wukv[:], G.w_ukv[0], [G.w_ukv], [wukv])
    def bload(name, src, n):
        t = sb(name, [128, n], F32)
        P.dma("sp", t[:], bc_ap(src, n), [src], [t])
        return t
    gA = bload("gA", G.attn_norm_g, D)
    gql = bload("gql", G.q_lat_norm_g, QR)
    gkvl = bload("gkvl", G.kv_lat_norm_g, KVR)
    qg = bload("qg", G.mla_q_norm_g, QK)
    kg = bload("kg", G.mla_k_norm_g, QK)
    dqg = bload("dqg", G.diff_q_norm_g, DQK)
    dkg = bload("dkg", G.diff_k_norm_g, DQK)
    ccs = [sb("cc%d" % k, [128, 4, 32], F32) for k in range(2)]
    sns = [sb("sn%d" % k, [128, 4, 32], F32) for k in range(2)]
    valid = sb("validA", [128, NT], F32)
    P.dma("sp", valid[:], G.valid[:, :], [G.valid], [valid])
    ident = sb("identA", [128, 128], BF16)
    P.memset("pool", ident[:], 1.0, [], [ident])
    P.op("pool", lambda e: e.affine_select(ident[:], ident[:], [[-1, 128]], ALU.is_equal, 0.0,
                                           base=0, channel_multiplier=1), [ident], [ident])
    pT = ps("pT", [128, 8, 128], BF16)
    pTx = ps("pTx", [128, 8, 128], BF16)
    pZ0 = ps("pZ0", [128, 512], F32)
    pZ1 = ps("pZ1", [128, 512], F32)
    pZ2 = ps("pZ2", [128, 512], F32)
    pZ3 = ps("pZ3", [128, 512], F32)
    pK = ps("pK", [128, 1024], F32)
    NXS = 2
    xts = [sb("xt%d" % i, [128, D], F32) for i in range(NXS)]
    NSL = 2

    def slot_tiles(i):
        t = NS()
        t.junk = sb("junk%d" % i, [128, D], BF16)
        t.xg = sb("xg%d" % i, [128, D], BF16)
        t.xT = sb("xT%d" % i, [128, 8, 128], BF16)
        t.zsb = sb("zsb%d" % i, [128, IN_COLS], F32)
        t.st = sb("st%d" % i, [128, 16], F32)
        t.stk = sb("stk%d" % i, [128, 16], F32)
        t.stq = sb("stq%d" % i, [128, 16], F32)
        t.shk = sb("shk%d" % i, [128, 8, 8], F32)
        t.shq = sb("shq%d" % i, [128, 8, 8], F32)
        t.shd = sb("shd%d" % i, [128, 8, 2], F32)
        t.she = sb("she%d" % i, [128, 8, 2], F32)
        t.sqd = sb("sqd%d" % i, [128, 512], F32)
        t.tmpd = sb("tmpd%d" % i, [128, 512], F32)
        t.sqe = sb("sqe%d" % i, [128, 512], F32)
        t.tmpe = sb("tmpe%d" % i, [128, 512], F32)
        t.dfin2 = sb("dfin2%d" % i, [128, 512], BF16)
        t.jf = sb("jf%d" % i, [128, 256], F32)
        t.ukv = sb("ukv%d" % i, [128, 128], BF16)
        t.ukvT = sb("ukvT%d" % i, [128, 128], BF16)
        t.sq = sb("sq%d" % i, [128, 768], F32)
        t.tmp = sb("tmp%d" % i, [128, 768], F32)
        t.tmp2 = sb("tmp2%d" % i, [128, 768], F32)
        t.kfin = sb("kfin%d" % i, [128, 8, QK], BF16)
        t.krg = sb("krg%d" % i, [128, 32], F32)
        t.r1 = sb("r1%d" % i, [128, 8, 32], F32)
        t.r2 = sb("r2%d" % i, [128, 8, 32], F32)
        t.dfin = sb("dfin%d" % i, [128, 512], BF16)
        t.uq = sb("uq%d" % i, [128, 256], BF16)
        t.uqT = sb("uqT%d" % i, [128, 2, 128], BF16)
        return t
    slots = [slot_tiles(i) for i in range(NSL)]

    def blk_tiles(i):
        t = NS()
        t.KT = sb("KTblk%d" % i, [QK, NH, 512], BF16)
        t.V = sb("Vblk%d" % i, [128, NH, 4, 128], BF16)
        t.dKT = sb("dKTblk%d" % i, [128, DH, 512], BF16)
        t.dV = sb("dVblk%d" % i, [128, DH, 4, 128], BF16)
        t.QT = sb("QTblk%d" % i, [QK, NH, 512], BF16)
        t.dQT = sb("dQTblk%d" % i, [128, DH, 512], BF16)
        return t
    blks = [blk_tiles(i) for i in range(2)]

    def rsqrt_inplace(v, n, R):
        P.act(v, v, AF.Sqrt, R, R)
        P.recip(v, v, R, R)

    ntiles = C.NBLK * 4
    ENG_EW = "dve"

    def load_x(t):
        xt = xts[t % NXS]
        P.dma("sp", xt[:], G.xbuf[t * 128:(t + 1) * 128, :], [G.xbuf], [xt])
    load_x(0)
    if ntiles > 1:
        load_x(1)

    ENG_EW = "dve"

    def tile_info(t):
        pos, sub = t // 4, t % 4
        own = pos in C.own_pos
        halo_next = (pos + 1) in C.seg_starts
        need_q = own or (halo_next and sub == 3)
        return pos, sub, own, halo_next, need_q

    def front(t):
        xt = xts[t % NXS]
        T = slots[t % NSL]
        st = T.st
        P.act(T.junk[:], xt[:], AF.Square, [xt], [T.junk, st], accum_out=st[:, 0:1])
        P.ts("dve", st[:, 0:1], st[:, 0:1], 1.0 / D, EPS, ALU.mult, ALU.add, [st], [st])
        rsqrt_inplace(st[:, 0:1], 1, [st])
        P.tt("dve", st[:, 1:2], st[:, 0:1], st[:, 0:1], ALU.mult, [st], [st])
        P.tt("dve", T.xg[:], xt[:], gA[:], ALU.mult, [xt, gA], [T.xg])
        for c in range(8):
            P.tr(pTx[:, c, :], T.xg[:, c * 128:(c + 1) * 128], ident[:], [T.xg, ident], [pTx])
        P.copy("act", T.xT[:], pTx[:], [pTx], [T.xT])

    def zgroups(need_q):
        if need_q:
            g = [(pZ0, 0, 0, OKR + ROPE), (pZ1, 0, ODQ, ODK)]
        else:
            g = [(pZ0, OKV, OKV, OKR + ROPE)]
        return g + [(pZ2, 0, ODK, ODV), (pZ3, 0, ODV, IN_COLS)]

    def zmm(t):
        T = slots[t % NSL]
        need_q = tile_info(t)[4]
        for (pz, o0, c0, c1) in zgroups(need_q):
            for c in range(8):
                P.mm(pz[:, o0:o0 + (c1 - c0)], T.xT[:, c, :], win[:, c, c0:c1], c == 0, c == 7, [T.xT, win], [pz])

    def evac(t):
        T = slots[t % NSL]
        need_q = tile_info(t)[4]
        for gi, (pz, o0, c0, c1) in enumerate(zgroups(need_q)):
            P.copy("act" if gi % 2 == 0 else "dve", T.zsb[:, c0:c1], pz[:, o0:o0 + (c1 - c0)], [pz], [T.zsb])

    def kv_head(t):
        T = slots[t % NSL]
        st, stk, zsb = T.st, T.stk, T.zsb
        zkv = zsb[:, OKV:OKV + KVR]
        P.act(T.jf[:, 0:KVR], zkv, AF.Square, [zsb], [T.jf, stk], accum_out=stk[:, 2:3])
        P.tt("dve", T.ukv[:], zkv, gkvl[:], ALU.mult, [zsb, gkvl], [T.ukv])
        P.tr(pT[:, 0, :], T.ukv[:], ident[:], [T.ukv, ident], [pT])
        P.copy("act", T.ukvT[:], pT[:, 0, :], [pT], [T.ukvT])
        for hf in range(2):
            P.mm(pK[:, hf * 512:(hf + 1) * 512], T.ukvT[:], wukv[:, hf * 512:(hf + 1) * 512], True, True,
                 [T.ukvT, wukv], [pK])

    def k_pre(t):
        pos, sub, own, halo_next, need_q = tile_info(t)
        B = blks[pos % 2]
        T = slots[t % NSL]
        st, stk, sh, zsb = T.st, T.stk, T.shk, T.zsb
        P.tt("dve", stk[:, 3:4], stk[:, 2:3], st[:, 1:2], ALU.mult, [stk, st], [stk])
        P.ts("dve", stk[:, 3:4], stk[:, 3:4], 1.0 / KVR, EPS, ALU.mult, ALU.add, [stk], [stk])
        rsqrt_inplace(stk[:, 3:4], 1, [stk])
        P.tt("dve", stk[:, 3:4], stk[:, 3:4], st[:, 0:1], ALU.mult, [stk, st], [stk])
        P.tt("dve", stk[:, 4:5], stk[:, 3:4], stk[:, 3:4], ALU.mult, [stk], [stk])
        zkr = zsb[:, OKR:OKR + ROPE]
        P.act(T.jf[:, 128:160], zkr, AF.Square, [zsb], [T.jf, stk], accum_out=stk[:, 5:6])
        P.tt("dve", stk[:, 5:6], stk[:, 5:6], st[:, 1:2], ALU.mult, [stk, st], [stk])
        pK3 = pK[:].rearrange("p (h c) -> p h c", h=NH)
        sqk3 = T.sq[:, 0:512].rearrange("p (h d) -> p h d", h=NH)
        P.act(sqk3, pK3[:, :, 0:NOPE], AF.Square, [pK], [T.sq])
        P.red(sh[:, :, 0], sqk3, [T.sq], [sh])
        P.ts("dve", sh[:, :, 0], sh[:, :, 0], stk[:, 4:5], stk[:, 5:6], ALU.mult, ALU.add, [sh, stk], [sh])
        P.ts("dve", sh[:, :, 0], sh[:, :, 0], 1.0 / QK, EPS, ALU.mult, ALU.add, [sh], [sh])
        rsqrt_inplace(sh[:, :, 0], 8, [sh])
        P.ts("dve", sh[:, :, 1], sh[:, :, 0], stk[:, 3:4], None, ALU.mult, None, [sh, stk], [sh])
        P.ts("dve", sh[:, :, 2], sh[:, :, 0], st[:, 0:1], None, ALU.mult, None, [sh, st], [sh])
        tk3 = T.tmp[:, 0:512].rearrange("p (h d) -> p h d", h=NH)
        P.tt("dve", tk3, pK3[:, :, 0:NOPE], sh[:, :, 1:2].to_broadcast([128, NH, NOPE]), ALU.mult, [pK, sh], [T.tmp])
        P.act(B.V[:, :, sub, 0:VD], pK3[:, :, NOPE:128], AF.Copy, [pK, stk], [B.V], scale=stk[:, 3:4])
        P.copy("pool", B.V[:, :, sub, VD:128], valid[:, t:t + 1].unsqueeze(2).to_broadcast([128, NH, 64]),
               [valid], [B.V])
        P.tt(ENG_EW, T.kfin[:, :, 0:NOPE], tk3, kg[:, 0:NOPE].unsqueeze(1).to_broadcast([128, NH, NOPE]), ALU.mult,
             [T.tmp, kg], [T.kfin])
        P.tt("dve", T.krg[:], zkr, kg[:, NOPE:QK], ALU.mult, [zsb, kg], [T.krg])
        cc, sn = ccs[pos % 2], sns[pos % 2]
        P.tt("pool", T.r1[:, 0, :], T.krg[:], cc[:, sub, :], ALU.mult, [T.krg, cc], [T.r1])
        P.tt("pool", T.r2[:, 0, 0:16], T.krg[:, 16:32], sn[:, sub, 0:16], ALU.mult, [T.krg, sn], [T.r2])
        P.tt("pool", T.r2[:, 0, 16:32], T.krg[:, 0:16], sn[:, sub, 16:32], ALU.mult, [T.krg, sn], [T.r2])
        P.tt("pool", T.r1[:, 0, :], T.r1[:, 0, :], T.r2[:, 0, :], ALU.add, [T.r1, T.r2], [T.r1])
        P.tt("dve", T.kfin[:, :, NOPE:QK], T.r1[:, 0:1, :].to_broadcast([128, NH, ROPE]),
             sh[:, :, 2:3].to_broadcast([128, NH, ROPE]), ALU.mult, [T.r1, sh], [T.kfin])

    def k_tail(t):
        pos, sub = tile_info(t)[0:2]
        B = blks[pos % 2]
        T = slots[t % NSL]
        for h in range(NH):
            P.tr(pT[0:QK, h, :], T.kfin[:, h, :], ident[:], [T.kfin, ident], [pT])
        P.copy("act", B.KT[:, :, sub * 128:(sub + 1) * 128], pT[0:QK, :, :], [pT], [B.KT])

    def dqk_pre(t, which):
        T = slots[t % NSL]
        st, zsb = T.st, T.zsb
        sqx, tmpx, shx, dfx = (T.sqd, T.tmpd, T.shd, T.dfin) if which == 0 else (T.sqe, T.tmpe, T.she, T.dfin2)
        zap = zsb[:, ODK:ODV] if which == 0 else zsb[:, ODQ:ODK]
        gain = dkg if which == 0 else dqg
        sq3 = sqx[:].rearrange("p (m d) -> p m d", m=8)
        P.act(sqx[:], zap, AF.Square, [zsb], [sqx])
        P.red(shx[:, :, 0], sq3, [sqx], [shx])
        P.ts("dve", shx[:, :, 0], shx[:, :, 0], st[:, 1:2], None, ALU.mult, None, [shx, st], [shx])
        P.ts("dve", shx[:, :, 0], shx[:, :, 0], 1.0 / DQK, EPS, ALU.mult, ALU.add, [shx], [shx])
        rsqrt_inplace(shx[:, :, 0], 8, [shx])
        P.ts("dve", shx[:, :, 0], shx[:, :, 0], st[:, 0:1], None, ALU.mult, None, [shx, st], [shx])
        tmp3 = tmpx[:].rearrange("p (m d) -> p m d", m=8)
        P.tt("dve", tmp3, zap.rearrange("p (m d) -> p m d", m=8),
             shx[:, :, 0:1].to_broadcast([128, 8, DQK]), ALU.mult, [zsb, shx], [tmpx])
        P.tt(ENG_EW, dfx[:].rearrange("p (m d) -> p m d", m=8), tmp3,
             gain[:].unsqueeze(1).to_broadcast([128, 8, DQK]), ALU.mult, [tmpx, gain], [dfx])

    def dqk_tail(t, which):
        pos, sub, own = tile_info(t)[0:3]
        B = blks[pos % 2]
        T = slots[t % NSL]
        dfx = T.dfin if which == 0 else T.dfin2
        for h in range(DH):
            P.tr(pT[:, h, :], dfx[:, h * 128:(h + 1) * 128], ident[:], [dfx, ident], [pT])
        if which == 0:
            P.copy("act", B.dKT[:, :, sub * 128:(sub + 1) * 128], pT[:, 0:DH, :], [pT], [B.dKT])
        elif own:
            P.copy("act", B.dQT[:, :, sub * 128:(sub + 1) * 128], pT[:, 0:DH, :], [pT], [B.dQT])
        else:
            P.copy("act", B.dQT[:, :, 0:64], pT[:, 0:DH, 64:128], [pT], [B.dQT])

    def dv_copy(t):
        pos, sub = tile_info(t)[0:2]
        B = blks[pos % 2]
        T = slots[t % NSL]
        P.act(B.dV[:, :, sub, :], T.zsb[:, ODV:IN_COLS].rearrange("p (h c) -> p h c", h=DH), AF.Copy, [T.zsb, T.st], [B.dV],
              scale=T.st[:, 0:1])

    def q_head(t):
        T = slots[t % NSL]
        st, stq, zsb = T.st, T.stq, T.zsb
        zq = zsb[:, OQ:OQ + QR]
        P.act(T.jf[:, 0:QR], zq, AF.Square, [zsb], [T.jf, stq], accum_out=stq[:, 6:7])
        P.tt("dve", T.uq[:], zq, gql[:], ALU.mult, [zsb, gql], [T.uq])
        for c in range(2):
            P.tr(pT[:, 1 + c, :], T.uq[:, c * 128:(c + 1) * 128], ident[:], [T.uq, ident], [pT])
        P.copy("act", T.uqT[:], pT[:, 1:3, :], [pT], [T.uqT])
        for (c0, c1) in ((0, 512), (512, 768)):
            for c in range(2):
                P.mm(pK[:, c0:c1], T.uqT[:, c, :], wuq[:, c, c0:c1], c == 0, c == 1, [T.uqT, wuq], [pK])

    def q_pre(t):
        pos, sub = tile_info(t)[0:2]
        cc, sn = ccs[pos % 2], sns[pos % 2]
        T = slots[t % NSL]
        st, stq, sh = T.st, T.stq, T.shq
        P.tt("dve", stq[:, 7:8], stq[:, 6:7], st[:, 1:2], ALU.mult, [stq, st], [stq])
        P.ts("dve", stq[:, 7:8], stq[:, 7:8], 1.0 / QR, EPS, ALU.mult, ALU.add, [stq], [stq])
        rsqrt_inplace(stq[:, 7:8], 1, [stq])
        P.tt("dve", stq[:, 7:8], stq[:, 7:8], st[:, 0:1], ALU.mult, [stq, st], [stq])
        P.tt("dve", stq[:, 8:9], stq[:, 7:8], stq[:, 7:8], ALU.mult, [stq], [stq])
        pQ3 = pK[:, 0:NH * QK].rearrange("p (h d) -> p h d", h=NH)
        sqq3 = T.sq[:, 0:NH * QK].rearrange("p (h d) -> p h d", h=NH)
        P.act(T.sq[:, 0:NH * QK], pK[:, 0:NH * QK], AF.Square, [pK], [T.sq])
        P.red(sh[:, :, 4], sqq3, [T.sq], [sh])
        P.ts("dve", sh[:, :, 4], sh[:, :, 4], stq[:, 8:9], None, ALU.mult, None, [sh, stq], [sh])
        P.ts("dve", sh[:, :, 4], sh[:, :, 4], 1.0 / QK, EPS, ALU.mult, ALU.add, [sh], [sh])
        rsqrt_inplace(sh[:, :, 4], 8, [sh])
        P.ts("dve", sh[:, :, 4], sh[:, :, 4], stq[:, 7:8], None, ALU.mult, None, [sh, stq], [sh])
        tq3 = T.tmp[:, 0:NH * QK].rearrange("p (h d) -> p h d", h=NH)
        tg3 = T.tmp2[:, 0:NH * QK].rearrange("p (h d) -> p h d", h=NH)
        P.tt("dve", tq3, pQ3, sh[:, :, 4:5].to_broadcast([128, NH, QK]), ALU.mult, [pK, sh], [T.tmp])
        P.tt(ENG_EW, tg3, tq3, qg[:].unsqueeze(1).to_broadcast([128, NH, QK]), ALU.mult, [T.tmp, qg], [T.tmp2])
        qfin = T.kfin
        P.copy("pool", qfin[:, :, 0:NOPE], tg3[:, :, 0:NOPE], [T.tmp2], [qfin])
        P.tt(ENG_EW, T.r1[:], tg3[:, :, NOPE:QK], cc[:, sub:sub + 1, :].to_broadcast([128, NH, ROPE]), ALU.mult,
             [T.tmp2, cc], [T.r1])
        P.tt(ENG_EW, T.r2[:, :, 0:16], tg3[:, :, NOPE + 16:QK], sn[:, sub:sub + 1, 0:16].to_broadcast([128, NH, 16]),
             ALU.mult, [T.tmp2, sn], [T.r2])
        P.tt(ENG_EW, T.r2[:, :, 16:32], tg3[:, :, NOPE:NOPE + 16], sn[:, sub:sub + 1, 16:32].to_broadcast([128, NH, 16]),
             ALU.mult, [T.tmp2, sn], [T.r2])
        P.tt(ENG_EW, qfin[:, :, NOPE:QK], T.r1[:], T.r2[:], ALU.add, [T.r1, T.r2], [qfin])

    def q_tail(t):
        pos, sub, own = tile_info(t)[0:3]
        B = blks[pos % 2]
        T = slots[t % NSL]
        qfin = T.kfin
        for h in range(NH):
            P.tr(pT[0:QK, h, :], qfin[:, h, :], ident[:], [qfin, ident], [pT])
        if own:
            P.copy("act", B.QT[:, :, sub * 128:(sub + 1) * 128], pT[0:QK, :, :], [pT], [B.QT])
        else:
            P.copy("act", B.QT[:, :, 0:64], pT[0:QK, :, 64:128], [pT], [B.QT])

    def captured(fn, *a):
        P.capture()
        fn(*a)
        return P.end_capture()

    def back_kv(t):
        kv_head(t)
        P.emit_zip([captured(k_pre, t), captured(dqk_pre, t, 0)])
        k_tail(t)
        dqk_tail(t, 0)
        dv_copy(t)

    def back_rest(t):
        if not tile_info(t)[4]:
            return
        q_head(t)
        P.emit_zip([captured(q_pre, t), captured(dqk_pre, t, 1)])
        q_tail(t)
        dqk_tail(t, 1)

    front(0)
    zmm(0)
    evac(0)
    for pos in range(C.NBLK):
        own = pos in C.own_pos
        halo_next = (pos + 1) in C.seg_starts
        B = blks[pos % 2]
        P.dma("sp", ccs[pos % 2][:], G.rope_cc[:, pos * 4:pos * 4 + 4, :], [G.rope_cc], [ccs[pos % 2]])
        P.dma("sp", sns[pos % 2][:], G.rope_sn[:, pos * 4:pos * 4 + 4, :], [G.rope_sn], [sns[pos % 2]])
        for sub in range(4):
            t = pos * 4 + sub
            if t + 2 < ntiles:
                load_x(t + 2)
            if t + 1 < ntiles:
                front(t + 1)
            back_kv(t)
            if t + 1 < ntiles:
                zmm(t + 1)
            back_rest(t)
            if t + 1 < ntiles:
                evac(t + 1)
            if int(os.environ.get("K_SERIAL", "0")):
                P.flush()
        c0 = pos * 512
        P.dma("sp", G.KTm[:, :, c0:c0 + 512].rearrange("h p s -> p h s"), B.KT[:], [B.KT], [G.KTm], chan=("st", "KT", pos % 2))
        P.dma("sp", G.Vm[:, :, pos * 4:pos * 4 + 4, :].rearrange("h p t c -> p h t c"), B.V[:], [B.V], [G.Vm],
              chan=("st", "V", pos % 2))
        P.dma("sp", G.KTd[:, :, c0:c0 + 512].rearrange("h p s -> p h s"), B.dKT[:], [B.dKT], [G.KTd],
              chan=("st", "dKT", pos % 2))
        P.dma("sp", G.Vd[:, :, pos * 4:pos * 4 + 4, :].rearrange("h p t c -> p h t c"), B.dV[:], [B.dV], [G.Vd],
              chan=("st", "dV", pos % 2))
        if own:
            q0 = C.qcol(pos)
            P.dma("sp", G.QTm[:, :, q0:q0 + 512].rearrange("h p s -> p h s"), B.QT[:], [B.QT], [G.QTm],
                  chan=("st", "QT", pos % 2))
            P.dma("sp", G.QTd[:, :, q0:q0 + 512].rearrange("h p s -> p h s"), B.dQT[:], [B.dQT], [G.QTd],
                  chan=("st", "dQT", pos % 2))
        elif halo_next:
            q0 = C.halocol(pos + 1)
            P.dma("sp", G.QTm[:, :, q0:q0 + 64].rearrange("h p s -> p h s"), B.QT[:, :, 0:64], [B.QT], [G.QTm],
                  chan=("st", "QT", pos % 2))
            P.dma("sp", G.QTd[:, :, q0:q0 + 64].rearrange("h p s -> p h s"), B.dQT[:, :, 0:64], [B.dQT], [G.QTd],
                  chan=("st", "dQT", pos % 2))
    P.flush()
    P.end_scope(old)


def phase_B(P, C, G):
    old = P.phase_scope()
    sb, ps = P.sbuf, P.psum
    NT, NQ, S = C.NT, C.NQ, C.S
    Sm = [ps("Sm%d" % k, [128, 1024], F32) for k in range(2)]
    Ab = [ps("A%d" % k, [128, 512], F32) for k in range(4)]
    ones_bf = sb("ones_bf", [128, 128], BF16)
    P.memset("pool", ones_bf[:], 1.0, [], [ones_bf])
    epsb = sb("epsb", [128, 1], F32)
    P.memset("pool", epsb[:], EPS, [], [epsb])
    valid = sb("validB", [128, NT], F32)
    P.dma("sp", valid[:], G.valid[:, :], [G.valid], [valid])
    validT = sb("validT", [128, NT, 128], BF16)
    P.copy("pool", validT[:], valid[:].unsqueeze(2).to_broadcast([128, NT, 128]), [valid], [validT])
    lam4 = sb("lam4", [128, 4, DQK], F32)
    for k, nm in enumerate(("lambda_q1", "lambda_k1", "lambda_q2", "lambda_k2")):
        P.dma("sp", lam4[:, k, :], bc_ap(getattr(G, nm), DQK), [getattr(G, nm)], [lam4], chan=("lam", k))
    lamt = sb("lamt", [128, 8], F32)
    lamp = sb("lamp", [128, 2, DQK], F32)
    P.tt("dve", lamp[:, 0, :], lam4[:, 0, :], lam4[:, 1, :], ALU.mult, [lam4], [lamp])
    P.tt("dve", lamp[:, 1, :], lam4[:, 2, :], lam4[:, 3, :], ALU.mult, [lam4], [lamp])
    P.red(lamt[:, 0:2], lamp[:], [lamp], [lamt])
    P.act(lamt[:, 2:4], lamt[:, 0:2], AF.Exp, [lamt], [lamt])
    P.tt("dve", lamt[:, 4:5], lamt[:, 3:4], lamt[:, 2:3], ALU.subtract, [lamt], [lamt])
    P.ts("dve", lamt[:, 5:6], lamt[:, 4:5], -LAM_INIT, None, ALU.add, None, [lamt], [lamt])
    neglam = lamt[:, 5:6]
    gsc = sb("gsc", [128, 1], F32)
    P.dma("sp", gsc[:], bass.AP(G.diff_out_norm_g.t, 0, [[1, 128], [1, 1]]), [G.diff_out_norm_g], [gsc])
    P.ts("dve", gsc[:], gsc[:], 1.0 - LAM_INIT, None, ALU.mult, None, [gsc], [gsc])
    KTs = [sb("KTs%d" % k, [128, S], BF16) for k in range(2)]
    Vs = [sb("Vs%d" % k, [128, NT, 128], BF16) for k in range(2)]
    Qs = [sb("Qs%d" % k, [128, NQ], BF16) for k in range(2)]
    NPT = 3
    Pts = [sb("Pt%d" % k, [128, 1024], BF16) for k in range(NPT)]
    densb = [sb("densb%d" % k, [128, 512], F32) for k in range(2)]
    tnum = [sb("tnum%d" % k, [128, 512], F32) for k in range(2)]
    osb = sb("osb", [128, 512], F32)
    sqb = sb("sqb", [128, 512], BF16)
    rstd = sb("rstdB", [128, 512], F32)
    ystage = [sb("ystage%d" % k, [128, 512], BF16) for k in range(2)]

    heads = [("m", h) for h in range(NH)] + [("d", h) for h in range(DH)]

    def load_head(idx):
        kind, h = heads[idx]
        sl = idx % 2
        if kind == "m":
            P.dma("sp", KTs[sl][0:QK, :], G.KTm[h], [G.KTm], [KTs[sl]])
            P.dma("sp", Vs[sl][:], G.Vm[h], [G.Vm], [Vs[sl]])
            P.dma("sp", Qs[sl][0:QK, :], G.QTm[h], [G.QTm], [Qs[sl]])
        else:
            P.dma("sp", KTs[sl][:], G.KTd[h], [G.KTd], [KTs[sl]])
            P.dma("sp", Vs[sl][:], G.Vd[h], [G.Vd], [Vs[sl]])
            P.dma("sp", Qs[sl][:], G.QTd[h], [G.QTd], [Qs[sl]])

    qtiles = []
    for pos in C.own_pos:
        if pos in C.seg_starts:
            qtiles.append((C.halocol(pos), 64, 4 * pos, False))
        qtiles.append((C.qcol(pos), 512, 4 * pos, True))

    def batches_of(kind, w, nfull, diag):
        bl = []
        nmap = 1 if kind == "m" else 2
        if diag:
            for j in range(4):
                ww = 512 - 128 * j
                exps = [(0, ww)] if nmap == 1 else [(0, ww), (512, 512 + ww)]
                bl.append(([(nfull + j, 0, 128 * j, ww, True, 0)], exps))
        if w == 512:
            if nmap == 1:
                for kt in range(0, nfull, 2):
                    bl.append(([(kt, 0, 0, 512, False, None), (kt + 1, 512, 0, 512, False, None)], [(0, 1024)]))
            else:
                for kt in range(nfull):
                    d = nfull - kt
                    bl.append(([(kt, 0, 0, 512, False, d if d <= NEAR else None)], [(0, 1024)]))
        else:
            per = 16 if nmap == 1 else 8
            for k0 in range(0, nfull, per):
                n_ = min(per, nfull - k0)
                ents = []
                for i_ in range(n_):
                    d = nfull - (k0 + i_)
                    ents.append((k0 + i_, 64 * i_, 0, 64, False, (d - 1) if d <= 6 else None))
                exps = [(0, 64 * n_)] if nmap == 1 else ([(0, 1024)] if n_ == 8 else [(0, 64 * n_), (512, 512 + 64 * n_)])
                bl.append((ents, exps))
        return bl

    load_head(0)
    late = {}

    def late_setup():
        tab = sb("tab", [NUM_BUCKETS, DH], F32)
        P.dma("sp", tab[:], G.rel_bias[:, :], [G.rel_bias], [tab])
        tabh = sb("tabh", [NUM_BUCKETS, DH], BF16)
        tabl = sb("tabl", [NUM_BUCKETS, DH], BF16)
        tabr = sb("tabr", [NUM_BUCKETS, DH], F32)
        P.copy("dve", tabh[:], tab[:], [tab], [tabh])
        P.copy("dve", tabr[:], tabh[:], [tabh], [tabr])
        P.tt("dve", tabr[:], tab[:], tabr[:], ALU.subtract, [tab, tabr], [tabr])
        P.copy("dve", tabl[:], tabr[:], [tabr], [tabl])
        ohf = sb("ohf", [NUM_BUCKETS, LV], F32)
        P.dma("sp", ohf[:], G.t5_onehot[:, :], [G.t5_onehot], [ohf])
        ohb = sb("ohb", [NUM_BUCKETS, LV], BF16)
        P.copy("dve", ohb[:], ohf[:], [ohf], [ohb])
        tbh = sb("tbh", [NUM_BUCKETS, DH, 128], BF16)
        tbl = sb("tbl", [NUM_BUCKETS, DH, 128], BF16)
        P.copy("dve", tbh[:], tabh[:].unsqueeze(2).to_broadcast([NUM_BUCKETS, DH, 128]), [tabh], [tbh])
        P.copy("dve", tbl[:], tabl[:].unsqueeze(2).to_broadcast([NUM_BUCKETS, DH, 128]), [tabl], [tbl])
        Rsb = sb("Rsb", [128, LV], F32)
        cfar = sb("cfar", [128, DH], F32)
        ncfar = sb("ncfar", [128, DH], F32)
        Et = sb("Et", [128, DH * 6, 512], BF16)
        Etmp = [sb("Etmp%d" % k, [128, 512], F32) for k in range(2)]
        cgs = [(0, 512), (512, 1024), (1024, LV)]
        for h in range(DH):
            for k, (c0, c1) in enumerate(cgs):
                P.mm(Ab[k][:, 0:c1 - c0], tbh[:, h, :], ohb[:, c0:c1], True, False, [tbh, ohb], [Ab[k]])
                P.mm(Ab[k][:, 0:c1 - c0], tbl[:, h, :], ohb[:, c0:c1], False, True, [tbl, ohb], [Ab[k]])
                P.copy("act", Rsb[:, c0:c1], Ab[k][:, 0:c1 - c0], [Ab[k]], [Rsb])
            P.copy("dve", cfar[:, h:h + 1], Rsb[:, LV - 1:LV], [Rsb], [cfar])
            P.ts("dve", ncfar[:, h:h + 1], cfar[:, h:h + 1], -1.0, None, ALU.mult, None, [cfar], [ncfar])
            P.act(Rsb[:], Rsb[:], AF.Exp, [Rsb, ncfar], [Rsb], bias=ncfar[:, h:h + 1])
            P.dma("sp", G.Rs[h], Rsb[:], [Rsb], [G.Rs], chan="Rs")
            for d in range(6):
                et = Etmp[(h * 6 + d) % 2]
                src = bass.AP(G.Rs.t, h * 128 * LV + RMAX + 128 * d, [[LV - 1, 128], [1, 512]])
                P.dma("sp", et[:], src, [G.Rs], [et])
                P.copy("pool", Et[:, h * 6 + d, :], et[:], [et], [Et])
        for f in range(NF):
            P.dma("pool", G.wgu[f, :, 0, :, :], G.w_gate[0][:, f * 128:(f + 1) * 128].rearrange("(c p) j -> p c j", p=128),
                  [G.w_gate], [G.wgu], chan=("wprep", (3 * f) % 6))
            P.dma("pool", G.wgu[f, :, 1, :, :], G.w_up[0][:, f * 128:(f + 1) * 128].rearrange("(c p) j -> p c j", p=128),
                  [G.w_up], [G.wgu], chan=("wprep", (3 * f + 1) % 6))
            P.dma("pool", G.wdn[:, f, :], G.w_down[0][f * 128:(f + 1) * 128, :], [G.w_down], [G.wdn],
                  chan=("wprep", (3 * f + 2) % 6))


        late["Et"], late["cfar"] = Et, cfar

    ycount = 0
    bcount = 0
    for idx, (kind, h) in enumerate(heads):
        if idx + 1 < len(heads):
            load_head(idx + 1)
        if idx == 1:
            late_setup()
        Et, cfar = late.get("Et"), late.get("cfar")
        sl = idx % 2
        KT, V, Q = KTs[sl], Vs[sl], Qs[sl]
        nmap = 1 if kind == "m" else 2
        kdim = QK if kind == "m" else DQK
        sm_scale = SM_MLA if kind == "m" else SM_DIFF
        for ti, (qc, w, nfull, diag) in enumerate(qtiles):
            bl = batches_of(kind, w, nfull, diag)
            nb = len(bl)
            if kind == "m":
                accs = [(Ab[ti % 2], V)]
            else:
                accs = None
            total_pv = sum(len(b[0]) for b in bl)

            def s_mm(bi_):
                S_ = Sm[(bcount + bi_) % 2]
                for (kt, so, qo, ww, _, _) in bl[bi_][0]:
                    for m in range(nmap):
                        P.mm(S_[:, m * 512 + so:m * 512 + so + ww], KT[64 * m:64 * m + kdim, kt * 128:(kt + 1) * 128],
                             Q[64 * m:64 * m + kdim, qc + qo:qc + qo + ww], True, True, [KT, Q], [S_])
            s_mm(0)
            pv = 0
            for bi_ in range(nb):
                ents, exps = bl[bi_]
                if bi_ + 1 < nb:
                    s_mm(bi_ + 1)
                S_ = Sm[(bcount + bi_) % 2]
                pt = Pts[(bcount + bi_) % NPT]
                for (c0, c1) in exps:
                    if kind == "m":
                        P.act(pt[:, c0:c1], S_[:, c0:c1], AF.Exp, [S_], [pt], scale=sm_scale)
                    else:
                        P.act(pt[:, c0:c1], S_[:, c0:c1], AF.Exp, [S_, cfar], [pt], scale=sm_scale, bias=cfar[:, h:h + 1])
                for (kt, so, qo, ww, mask, near) in ents:
                    if kind == "d" and near is not None:
                        e0 = 64 if w == 64 else 0
                        pv3 = pt[:].rearrange("p (m c) -> p m c", m=2)[:, :, so:so + ww]
                        P.tt("dve", pv3, pv3, Et[:, h * 6 + near, e0:e0 + ww].unsqueeze(1).to_broadcast([128, 2, ww]),
                             ALU.mult, [pt, Et], [pt])
                    first, last = (pv == 0), (pv == total_pv - 1)
                    pv += 1
                    parts = [(0, 128, 0, ww)] if not mask else [(0, 128, 64, ww), (0, 64, 0, 64)]
                    for pi_, (k0, k1, c0, c1) in enumerate(parts):
                        if c1 <= c0:
                            continue
                        if pi_ > 0:
                            first = False
                        if kind == "m":
                            O = Ab[ti % 2]
                            P.mm(O[:, qo + c0:qo + c1], V[k0:k1, kt, :], pt[k0:k1, so + c0:so + c1], first, last, [V, pt], [O])
                        else:
                            for m in range(2):
                                P.mm(Ab[2 * m][:, qo + c0:qo + c1], V[k0:k1, kt, :],
                                     pt[k0:k1, m * 512 + so + c0:m * 512 + so + c1], first, last, [V, pt], [Ab[2 * m]])
                                P.mm(Ab[2 * m + 1][:, qo + c0:qo + c1], validT[k0:k1, kt, :],
                                     pt[k0:k1, m * 512 + so + c0:m * 512 + so + c1], first, last,
                                     [validT, pt], [Ab[2 * m + 1]])
            bcount += nb
            if kind == "m":
                O = Ab[ti % 2]
                ds = densb[ti % 2]
                P.copy("act", ds[0:64, 0:w], O[64:128, 0:w], [O], [ds])
                P.ts("dve", ds[0:64, 0:w], ds[0:64, 0:w], 1e-30, None, ALU.max, None, [ds], [ds])
                P.recip(ds[0:64, 0:w], ds[0:64, 0:w], [ds], [ds])
                ys = ystage[ycount % 2]
                ycount += 1
                P.tt("dve", ys[0:64, 0:w], O[0:64, 0:w], ds[0:64, 0:w], ALU.mult, [O, ds], [ys])
                r0 = (h % 2) * 64
                P.dma("sp", G.yT[r0:r0 + 64, h // 2, qc:qc + w], ys[0:64, 0:w], [ys], [G.yT],
                      chan=("yst", ycount % 2))
            else:
                for m in range(2):
                    P.copy("act", densb[m][:, 0:w], Ab[2 * m + 1][:, 0:w], [Ab[2 * m + 1]], [densb[m]])
                    P.copy("act", tnum[m][:, 0:w], Ab[2 * m][:, 0:w], [Ab[2 * m]], [tnum[m]])
                for m in range(2):
                    ds = densb[m]
                    P.ts("dve", ds[:, 0:w], ds[:, 0:w], 1e-30, None, ALU.max, None, [ds], [ds])
                    P.recip(ds[:, 0:w], ds[:, 0:w], [ds], [ds])
                    P.tt("dve", tnum[m][:, 0:w], tnum[m][:, 0:w], ds[:, 0:w], ALU.mult, [tnum[m], ds], [tnum[m]])
                ys = ystage[ycount % 2]
                ycount += 1
                P.stt("dve", ys[:, 0:w], tnum[1][:, 0:w], neglam, tnum[0][:, 0:w], ALU.mult, ALU.add,
                      [tnum[0], tnum[1], lamt], [ys])
                P.dma("sp", G.yT[:, 4 + h, qc:qc + w], ys[:, 0:w], [ys], [G.yT], chan=("yst", ycount % 2))
    P.flush()
    P.end_scope(old)


def phase_C(P, C, G):
    old = P.phase_scope()
    sb, ps = P.sbuf, P.psum
    pX = ps("pX", [128, D], F32)
    pTt = ps("pTt", [128, 8, 128], BF16)
    pG = [ps("pG%d" % k, [128, 512], F32) for k in range(2)]
    pU = [ps("pU%d" % k, [128, 512], F32) for k in range(2)]
    pH = ps("pH", [128, 512], F32)
    wout = sb("wout", [128, 8, D], BF16)
    wpg = sb("wpg", [128, 8, D], BF16)
    wpp = sb("wpp", [128, 2, D], BF16)
    wd = sb("wd", [128, NF, D], BF16)
    P.dma("pool", wout[:], G.w_out[0].rearrange("(c p) n -> p c n", p=128), [G.w_out], [wout])
    P.dma("pool", wpg[:], G.w_ple_gate[0].rearrange("(c p) n -> p c n", p=128), [G.w_ple_gate], [wpg])
    P.dma("pool", wpp[:], G.w_ple_proj[0].rearrange("(c p) n -> p c n", p=128), [G.w_ple_proj], [wpp])
    P.dma("sp", wd[:], G.wdn[:, :, :], [G.wdn], [wd])
    gF = sb("gF", [128, D], F32)
    gP = sb("gP", [128, D], F32)
    P.dma("sp", gF[:], bc_ap(G.ffn_norm_g, D), [G.ffn_norm_g], [gF])
    P.dma("sp", gP[:], bc_ap(G.ple_norm_g, D), [G.ple_norm_g], [gP])
    ident = sb("identC", [128, 128], BF16)
    P.memset("pool", ident[:], 1.0, [], [ident])
    P.op("pool", lambda e: e.affine_select(ident[:], ident[:], [[-1, 128]], ALU.is_equal, 0.0,
                                           base=0, channel_multiplier=1), [ident], [ident])
    identf = sb("identf", [128, 128], F32)
    P.copy("pool", identf[:], ident[:], [ident], [identf])
    cwr = sb("cwr", [NF, 4, 128], F32)
    for jj in range(3):
        P.dma("sp", cwr[:, jj, :], G.conv_w[0, jj].rearrange("(f p) -> f p", p=128), [G.conv_w], [cwr], chan=("cw", jj))
    P.dma("sp", cwr[:, 3, :], G.conv_b[0].rearrange("(f p) -> f p", p=128), [G.conv_b], [cwr], chan=("cw", 3))
    cw = sb("cw", [128, 4, NF], F32)
    pHc = pH[:, 0:4 * NF].rearrange("p (j f) -> p j f", j=4)
    for jj in range(4):
        P.op("pe", lambda e, jj=jj: e.transpose(pHc[:, jj, :], cwr[:, jj, :], identf[0:NF, 0:NF]), [cwr, identf], [pH])
    P.copy("act", cw[:], pHc, [pH], [cw])
    yTb = [sb("yTb%d" % k, [128, 8, 512], BF16) for k in range(1)]
    yTh = sb("yTh", [128, 8, 64], BF16)
    xin = [sb("xin%d" % k, [128, D], F32) for k in range(2)]
    x1b = sb("x1b", [128, 4, D], F32)
    h2 = [sb("h2_%d" % k, [128, D], BF16) for k in range(3)]
    junk = sb("junkC", [128, D], BF16)
    stc = [sb("stc%d" % k, [128, 4], F32) for k in range(3)]
    h2T = sb("h2T", [128, 8, 512], BF16)
    h2Th = sb("h2Th", [128, 8, 64], BF16)
    actT = sb("actT", [128, NF, 512], BF16)
    wgu = [sb("wgu%d" % k, [128, 2, 8, 128], BF16) for k in range(3)]
    gext = [sb("gext%d" % k, [128, 514], F32) for k in range(2)]
    carry = sb("carry", [128, NF, 2], F32)
    cacc = [sb("cacc%d" % k, [128, 512], F32) for k in range(2)]
    sl = [sb("sl%d" % k, [128, 512], F32) for k in range(2)]
    h3T = sb("h3T", [128, 8, 128], BF16)
    pin = [sb("pin%d" % k, [128, PLE], F32) for k in range(2)]
    pb = sb("pb", [128, PLE], BF16)
    pTT = sb("pTT", [128, 2, 128], BF16)
    sig = sb("sig", [128, D], F32)
    outt = [sb("outt%d" % k, [128, D], F32) for k in range(2)]
    P.memset("pool", carry[:], 0.0, [], [carry])
    ones_c = sb("ones_c", [128, 128], BF16)
    P.memset("pool", ones_c[:], 1.0, [], [ones_c])
    epsc = sb("epsc", [128, 1], F32)
    P.memset("pool", epsc[:], EPS, [], [epsc])
    gscc = sb("gscc", [128, 1], F32)
    P.dma("sp", gscc[:], bass.AP(G.diff_out_norm_g.t, 0, [[1, 128], [1, 1]]), [G.diff_out_norm_g], [gscc])
    P.ts("dve", gscc[:], gscc[:], 1.0 - LAM_INIT, None, ALU.mult, None, [gscc], [gscc])
    sqc = sb("sqc", [128, 512], BF16)
    rsc = sig

    def diff_out_norm(yt, w):
        for hh in range(DH):
            ych = yt[:, 4 + hh, 0:w]
            P.act(sqc[:, 0:w], ych, AF.Square, [yt], [sqc])
            P.mm(pH[:, 0:w], ones_c[:], sqc[:, 0:w], True, True, [ones_c, sqc], [pH])
            P.act(rsc[:, 0:w], pH[:, 0:w], AF.Sqrt, [pH, epsc], [rsc], scale=1.0 / DV, bias=epsc[:])
            P.recip(rsc[:, 0:w], rsc[:, 0:w], [rsc], [rsc])
            P.stt("dve", ych, ych, gscc[:], rsc[:, 0:w], ALU.mult, ALU.mult, [yt, gscc, rsc], [yt])

    def rsqrt_inplace(v, R):
        P.act(v, v, AF.Sqrt, R, R)
        P.recip(v, v, R, R)

    xcount = [0]

    def wout_mm(np_, ylhs, yref, xrows, x1dst, x1ref):
        xt = xin[xcount[0] % 2]
        xcount[0] += 1
        P.dma("sp", xt[0:np_, :], xrows, [G.xbuf], [xt])
        for hf in range(2):
            for c in range(8):
                P.mm(pX[0:np_, hf * 512:(hf + 1) * 512], ylhs(c), wout[:, c, hf * 512:(hf + 1) * 512], c == 0, c == 7,
                     [wout, yref], [pX])
        return xt

    def wout_add(xt, np_, ylhs, yref, xrows, x1dst, x1ref):
        P.tt("dve", x1dst, pX[0:np_, :], xt[0:np_, :], ALU.add, [pX, xt], [x1ref])

    def wout_p2(np_, x1dst, x1ref, stt_, h2t, h2T_dst, h2Tref):
        P.act(junk[0:np_, :], x1dst, AF.Square, [x1ref], [junk, stt_], accum_out=stt_[0:np_, 0:1])
        P.ts("dve", stt_[0:np_, 0:1], stt_[0:np_, 0:1], 1.0 / D, EPS, ALU.mult, ALU.add, [stt_], [stt_])
        rsqrt_inplace(stt_[0:np_, 0:1], [stt_])
        P.stt("dve", h2t[0:np_, :], x1dst, stt_[0:np_, 0:1], gF[0:np_, :], ALU.mult, ALU.mult, [x1ref, stt_, gF], [h2t])
        for c in range(8):
            P.tr(pTt[:, c, 0:np_], h2t[0:np_, c * 128:(c + 1) * 128], ident[0:np_, 0:np_], [h2t, ident], [pTt])
        P.copy("act", h2T_dst, pTt[:, :, 0:np_], [pTt], [h2Tref])

    def load_w(f):
        w = wgu[f % 3]
        P.dma("sp", w[:], G.wgu[f], [G.wgu], [w])

    for bi, pos in enumerate(C.own_pos):
        qc = C.qcol(pos)
        yb = yTb[0]
        P.dma("sp", yb[:], G.yT[:, :, qc:qc + 512], [G.yT], [yb])
        diff_out_norm(yb, 512)
        first = pos in C.seg_starts
        stages = []
        if first:
            hc = C.halocol(pos)
            P.dma("sp", yTh[:], G.yT[:, :, hc:hc + 64], [G.yT], [yTh])
            diff_out_norm(yTh, 64)

            def ylhs_h(c):
                return yTh[:, c, :]
            x1h = outt[0]
            stages.append(((64, ylhs_h, yTh, G.xbuf[pos * 512 - 64:pos * 512, :], x1h[0:64, :], x1h),
                           (64, x1h[0:64, :], x1h, stc[2], h2[2], h2Th[:], h2Th)))
        for sub in range(4):
            def ylhs(c, sub=sub):
                return yb[:, c, sub * 128:(sub + 1) * 128]
            r0 = pos * 512 + sub * 128
            stages.append(((128, ylhs, yb, G.xbuf[r0:r0 + 128, :], x1b[:, sub, :], x1b),
                           (128, x1b[:, sub, :], x1b, stc[sub % 2], h2[sub % 2],
                            h2T[:, :, sub * 128:(sub + 1) * 128], h2T)))
        xt0 = wout_mm(*stages[0][0])
        wout_add(xt0, *stages[0][0])
        for k in range(len(stages)):
            if k + 1 < len(stages):
                xt1 = wout_mm(*stages[k + 1][0])
            wout_p2(*stages[k][1])
            if k + 1 < len(stages):
                wout_add(xt1, *stages[k + 1][0])
        load_w(0)
        load_w(1)

        def ffn_front(f):
            if f + 2 < NF:
                load_w(f + 2)
            w = wgu[f % 3]
            if first:
                for c in range(8):
                    P.mm(pH[:, 0:64], w[:, 0, c, :], h2Th[:, c, :], c == 0, c == 7, [w, h2Th], [pH])
                P.copy("act", carry[:, f, :], pH[:, 62:64], [pH], [carry])
            g_ps, u_ps = pG[f % 2], pU[f % 2]
            for c in range(8):
                P.mm(g_ps[:], w[:, 0, c, :], h2T[:, c, :], c == 0, c == 7, [w, h2T], [g_ps])
            for c in range(8):
                P.mm(u_ps[:], w[:, 1, c, :], h2T[:, c, :], c == 0, c == 7, [w, h2T], [u_ps])
            ge = gext[f % 2]
            P.copy("act", ge[:, 2:514], g_ps[:], [g_ps], [ge])
            P.copy("pool", ge[:, 0:2], carry[:, f, :], [carry], [ge])
            P.copy("pool", carry[:, f, :], ge[:, 512:514], [ge], [carry])

        def ffn_back(f):
            u_ps = pU[f % 2]
            ge, ca, s_ = gext[f % 2], cacc[f % 2], sl[f % 2]
            P.ts("dve", ca[:], ge[:, 2:514], cw[:, 2, f:f + 1], cw[:, 3, f:f + 1], ALU.mult, ALU.add, [ge, cw], [ca])
            P.stt("dve", ca[:], ge[:, 1:513], cw[:, 1, f:f + 1], ca[:], ALU.mult, ALU.add, [ge, cw, ca], [ca])
            P.stt("dve", ca[:], ge[:, 0:512], cw[:, 0, f:f + 1], ca[:], ALU.mult, ALU.add, [ge, cw, ca], [ca])
            P.act(s_[:], ca[:], AF.Silu, [ca], [s_])
            P.tt("dve", actT[:, f, :], u_ps[:], s_[:], ALU.mult, [u_ps, s_], [actT])

        ffn_front(0)
        for f in range(NF):
            if f + 1 < NF:
                ffn_front(f + 1)
            ffn_back(f)
        def down_mm(sub):
            for hf in range(2):
                for f in range(NF):
                    P.mm(pX[:, hf * 512:(hf + 1) * 512], actT[:, f, sub * 128:(sub + 1) * 128],
                         wd[:, f, hf * 512:(hf + 1) * 512], f == 0, f == NF - 1, [actT, wd], [pX])

        def down_add(sub):
            x2_ = x1b[:, sub, :]
            P.tt("dve", x2_, pX[:], x2_, ALU.add, [pX, x1b], [x1b])

        def ple(sub):
            x2 = x1b[:, sub, :]
            stt_ = stc[sub % 2]
            P.act(junk[:], x2, AF.Square, [x1b], [junk, stt_], accum_out=stt_[:, 1:2])
            P.ts("dve", stt_[:, 1:2], stt_[:, 1:2], 1.0 / D, EPS, ALU.mult, ALU.add, [stt_], [stt_])
            rsqrt_inplace(stt_[:, 1:2], [stt_])
            h3 = h2[sub % 2]
            P.stt("dve", h3[:], x2, stt_[:, 1:2], gP[:], ALU.mult, ALU.mult, [x1b, stt_, gP], [h3])
            pi = pin[sub % 2]
            orow = qc + sub * 128
            P.dma("sp", pi[:], G.p_own[orow:orow + 128, :], [G.p_own], [pi])
            P.copy("pool", pb[:], pi[:], [pi], [pb])
            for c in range(2):
                P.tr(pH[:].bitcast(BF16)[:, c * 128:(c + 1) * 128], pb[:, c * 128:(c + 1) * 128], ident[:], [pb, ident], [pH])
            P.copy("act", pTT[:], pH[:].bitcast(BF16)[:, 0:256].rearrange("p (c t) -> p c t", c=2), [pH], [pTT])
            for c in range(8):
                P.tr(pTt[:, c, :], h3[:, c * 128:(c + 1) * 128], ident[:], [h3, ident], [pTt])
            P.copy("act", h3T[:], pTt[:], [pTt], [h3T])
            ot = outt[sub % 2]
            gl = (pG[0], pU[0])
            pj = (pG[1], pU[1])
            for hf in range(2):
                for c in range(2):
                    P.mm(pj[hf][:], pTT[:, c, :], wpp[:, c, hf * 512:(hf + 1) * 512], c == 0, c == 1, [pTT, wpp], [pj[hf]])
            for hf in range(2):
                for c in range(8):
                    P.mm(gl[hf][:], h3T[:, c, :], wpg[:, c, hf * 512:(hf + 1) * 512], c == 0, c == 7, [h3T, wpg], [gl[hf]])
                P.act(sig[:, hf * 512:(hf + 1) * 512], gl[hf][:], AF.Sigmoid, [gl[hf]], [sig])
                P.tt("dve", ot[:, hf * 512:(hf + 1) * 512], pj[hf][:], sig[:, hf * 512:(hf + 1) * 512], ALU.mult,
                     [pj[hf], sig], [ot])
            P.tt("pool", ot[:], ot[:], x2, ALU.add, [ot, x1b], [ot])
            P.dma("pool", G.out[orow:orow + 128, :], ot[:], [ot], [G.out], chan=("out", sub % 2))

        down_mm(0)
        down_add(0)
        for sub in range(4):
            if sub + 1 < 4:
                down_mm(sub + 1)
            ple(sub)
            if sub + 1 < 4:
                down_add(sub + 1)
    P.end_scope_deferred = old


def build_program(C, debug=(), phases="ABC"):
    nc = bass.Bass("TRN2", target_bir_lowering=False)
    P = Prog(nc)
    G = declare_dram(P, C, debug)
    if "A" in phases:
        phase_A(P, C, G)
    if "B" in phases:
        phase_B(P, C, G)
    if "C" in phases:
        phase_C(P, C, G)
    P.flush(final=True)
    if getattr(P, "end_scope_deferred", None) is not None:
        P.end_scope(P.end_scope_deferred)
    return nc, P


SEQ_FULL = 8192
SEG_L = 2
N_CORES = 8
_CACHE = {}


def kernel(**inputs):
    C = Cfg(SEQ_FULL, SEG_L)
    if "prog" not in _CACHE:
        _CACHE["prog"] = build_program(C)
    nc, _ = _CACHE["prog"]
    in_maps, metas = [], []
    for core in range(N_CORES):
        m, own_tok, real = host_inputs(inputs, core, C)
        in_maps.append(m)
        metas.append((own_tok, real))
    res = run_bass_kernel_spmd(nc, in_maps, core_ids=list(range(N_CORES)))
    out = np.zeros((4, SEQ_FULL, D), np.float32)
    for core in range(N_CORES):
        own_tok, real = metas[core]
        out[core // 2, real[own_tok]] = np.asarray(res.results[core]["out"], dtype=np.float32)
    return out
```
